# Optimizing a Trainium2 kernel written in Bass

```python
import math
import jax, jax.numpy as jnp
from jax import lax
import numpy as np

D_MODEL = 1024
BATCH = 16
SEQ = 4096
DEPTH = 4

N_MEM = 256
A_HEADS = 4
A_QK_DIM = 64
A_V_DIM = 2 * A_QK_DIM
A_QK_COLS = A_HEADS * 2 * A_QK_DIM
A_WIDTH = A_HEADS * A_V_DIM
POOL_WINDOWS = (2, 4, 8, 16)
POOL_GROUPS = len(POOL_WINDOWS)
POOL_GROUP_DIM = (D_MODEL // 2) // POOL_GROUPS
POOL_WIDTH = POOL_GROUPS * POOL_GROUP_DIM
EVEN_IN = 2 * A_QK_COLS + A_WIDTH + POOL_WIDTH
EVEN_MIX = A_WIDTH + POOL_WIDTH
CONV_WIDTH = 3
CONV_DIM = D_MODEL
X_HEADS = 4
X_HEAD_DIM = D_MODEL // X_HEADS
D_FF = 4 * D_MODEL
REL_BUCKETS = 32
REL_MAX_EXACT = REL_BUCKETS // 2
REL_MAX_DIST = 128
Q_BLOCK = 128
N_EVEN = (DEPTH + 1) // 2
N_ODD = DEPTH // 2
EPS = 1e-6

kernel_name = "hybrid_diffattn_pool_shortconv_trunk"


def rms_norm(x, g):
    xf = x.astype(jnp.float32)
    y = xf * lax.rsqrt(jnp.mean(xf * xf, axis=-1, keepdims=True) + EPS)
    return (y * g.astype(jnp.float32)).astype(x.dtype)


def t5_bucket(n):
    small = n < REL_MAX_EXACT
    nf = jnp.maximum(n, 1).astype(jnp.float32)
    large = REL_MAX_EXACT + (jnp.log(nf / REL_MAX_EXACT)
                             / math.log(REL_MAX_DIST / REL_MAX_EXACT)
                             * (REL_BUCKETS - REL_MAX_EXACT)).astype(jnp.int32)
    large = jnp.minimum(large, REL_BUCKETS - 1)
    return jnp.where(small, n, large)


def diff_attention(q1, q2, k1, k2, v, lam, bias_dist):
    B, S = q1.shape[0], q1.shape[1]
    nb = S // Q_BLOCK
    scale = A_QK_DIM ** -0.5
    qb1 = q1.reshape(B, nb, Q_BLOCK, A_HEADS, A_QK_DIM).transpose(1, 0, 2, 3, 4)
    qb2 = q2.reshape(B, nb, Q_BLOCK, A_HEADS, A_QK_DIM).transpose(1, 0, 2, 3, 4)
    starts = jnp.arange(nb, dtype=jnp.int32) * Q_BLOCK
    kpos = jnp.arange(S, dtype=jnp.int32)

    def block(args):
        qa, qc, start = args
        qpos = start + jnp.arange(Q_BLOCK, dtype=jnp.int32)
        dist = qpos[:, None] - kpos[None, :]
        causal = dist >= 0
        bias = bias_dist[:, jnp.clip(dist, 0, S - 1)]

        def probs(q, k):
            s = jnp.einsum('bqhd,bkhd->bhqk', q, k).astype(jnp.float32) * scale + bias
            s = jnp.where(causal, s, -jnp.inf)
            return jax.nn.softmax(s, axis=-1)

        a = probs(qa, k1) - lam * probs(qc, k2)
        return jnp.einsum('bhqk,bkhd->bqhd', a.astype(v.dtype), v)

    out = lax.map(block, (qb1, qb2, starts))
    return out.transpose(1, 0, 2, 3, 4).reshape(B, S, A_HEADS, A_V_DIM)


def multi_scale_pool(u, pool_w, pool_scale):
    B, S = u.shape[0], u.shape[1]
    ug = u.reshape(B, S, POOL_GROUPS, POOL_GROUP_DIM).astype(jnp.float32)
    c = jnp.concatenate([jnp.zeros((B, 1, POOL_GROUPS, POOL_GROUP_DIM), jnp.float32),
                         jnp.cumsum(ug, axis=1)], axis=1)
    t = jnp.arange(S, dtype=jnp.int32)
    pooled = []
    for gi, w in enumerate(POOL_WINDOWS):
        cg = c[:, :, gi]
        cp = jnp.concatenate([jnp.zeros((B, w - 1, POOL_GROUP_DIM), jnp.float32), cg], axis=1)
        win_sum = cg[:, 1:] - cp[:, :S]
        count = jnp.minimum(t + 1, w).astype(jnp.float32)[None, :, None]
        pooled.append(win_sum / count - ug[:, :, gi])
    p = jnp.stack(pooled, axis=2).astype(u.dtype)
    y = jnp.einsum('bsgc,gcd->bsgd', p, pool_w) * pool_scale.reshape(POOL_GROUPS, POOL_GROUP_DIM)
    return y.reshape(B, S, POOL_WIDTH)


def even_mixer(h, w_in, w_out, lq1, lk1, lq2, lk2, subln_g, pool_w, pool_scale,
               bias_dist, lambda_init):
    B, S = h.shape[0], h.shape[1]
    proj = h @ w_in
    q = proj[..., :A_QK_COLS].reshape(B, S, A_HEADS, 2, A_QK_DIM)
    k = proj[..., A_QK_COLS:2 * A_QK_COLS].reshape(B, S, A_HEADS, 2, A_QK_DIM)
    v = proj[..., 2 * A_QK_COLS:2 * A_QK_COLS + A_WIDTH].reshape(B, S, A_HEADS, A_V_DIM)
    u = proj[..., 2 * A_QK_COLS + A_WIDTH:]
    lam = (jnp.exp(jnp.sum(lq1.astype(jnp.float32) * lk1.astype(jnp.float32)))
           - jnp.exp(jnp.sum(lq2.astype(jnp.float32) * lk2.astype(jnp.float32)))
           + lambda_init)
    o = diff_attention(q[..., 0, :], q[..., 1, :], k[..., 0, :], k[..., 1, :], v, lam, bias_dist)
    o = (rms_norm(o, subln_g) * (1.0 - lambda_init)).reshape(B, S, A_WIDTH)
    y = multi_scale_pool(u, pool_w, pool_scale)
    return jnp.concatenate([o, y], axis=-1) @ w_out


def odd_mixer(h, w_in, conv_w, w_out):
    S = h.shape[1]
    proj = h @ w_in
    b_gate = proj[..., :CONV_DIM]
    c_gate = proj[..., CONV_DIM:2 * CONV_DIM]
    z = c_gate * proj[..., 2 * CONV_DIM:]
    zp = jnp.pad(z, ((0, 0), (CONV_WIDTH - 1, 0), (0, 0)))
    y = zp[:, 0:S] * conv_w[0]
    for tap in range(1, CONV_WIDTH):
        y = y + zp[:, tap:tap + S] * conv_w[tap]
    return (b_gate * y) @ w_out


def cross_attention(h, mem_n, wq, wkv, wo):
    B, S = h.shape[0], h.shape[1]
    M = mem_n.shape[1]
    q = (h @ wq).reshape(B, S, X_HEADS, X_HEAD_DIM)
    kv = (mem_n @ wkv).reshape(B, M, 2, X_HEADS, X_HEAD_DIM)
    s = jnp.einsum('bshd,bmhd->bhsm', q, kv[:, :, 0]).astype(jnp.float32) * X_HEAD_DIM ** -0.5
    p = jax.nn.softmax(s, axis=-1)
    o = jnp.einsum('bhsm,bmhd->bshd', p.astype(h.dtype), kv[:, :, 1]).reshape(B, S, D_MODEL)
    return o @ wo


def sq_relu_mlp(h, w1, w2):
    a = jax.nn.relu(h @ w1)
    return (a * a) @ w2


def setup_inputs(seed: int = 0) -> dict:
    key = jax.random.key(seed)
    ks = iter(jax.random.split(key, 32))
    f32 = jnp.float32

    def nrm(shape, scale):
        return jax.random.normal(next(ks), shape, f32) * scale

    def gain(shape):
        return 1.0 + nrm(shape, 0.02)

    return {
        "x": nrm((BATCH, SEQ, D_MODEL), 1.0),
        "mem": nrm((BATCH, N_MEM, D_MODEL), 1.0),
        "rel_bias": nrm((REL_BUCKETS, A_HEADS), 0.5),
        "mem_norm_g": gain((D_MODEL,)),
        "norm_mix_g": gain((DEPTH, D_MODEL)),
        "norm_xattn_g": gain((DEPTH, D_MODEL)),
        "norm_mlp_g": gain((DEPTH, D_MODEL)),
        "final_norm_g": gain((D_MODEL,)),
        "ab_w_in": nrm((N_EVEN, D_MODEL, EVEN_IN), D_MODEL ** -0.5),
        "ab_w_out": nrm((N_EVEN, EVEN_MIX, D_MODEL), EVEN_MIX ** -0.5),
        "lambda_q1": nrm((N_EVEN, A_QK_DIM), 0.1),
        "lambda_k1": nrm((N_EVEN, A_QK_DIM), 0.1),
        "lambda_q2": nrm((N_EVEN, A_QK_DIM), 0.1),
        "lambda_k2": nrm((N_EVEN, A_QK_DIM), 0.1),
        "subln_g": gain((N_EVEN, A_V_DIM)),
        "pool_w": nrm((N_EVEN, POOL_GROUPS, POOL_GROUP_DIM, POOL_GROUP_DIM), POOL_GROUP_DIM ** -0.5),
        "pool_scale": 1.0 + nrm((N_EVEN, POOL_WIDTH), 0.1),
        "conv_w_in": nrm((N_ODD, D_MODEL, 3 * CONV_DIM), D_MODEL ** -0.5),
        "conv_w": nrm((N_ODD, CONV_WIDTH, CONV_DIM), CONV_WIDTH ** -0.5),
        "conv_w_out": nrm((N_ODD, CONV_DIM, D_MODEL), CONV_DIM ** -0.5),
        "xattn_wq": nrm((DEPTH, D_MODEL, D_MODEL), D_MODEL ** -0.5),
        "xattn_wkv": nrm((DEPTH, D_MODEL, 2 * D_MODEL), D_MODEL ** -0.5),
        "xattn_wo": nrm((DEPTH, D_MODEL, D_MODEL), D_MODEL ** -0.5),
        "mlp_w1": nrm((DEPTH, D_MODEL, D_FF), D_MODEL ** -0.5),
        "mlp_w2": nrm((DEPTH, D_FF, D_MODEL), D_FF ** -0.5),
    }


def reference(x, mem, rel_bias, mem_norm_g, norm_mix_g, norm_xattn_g, norm_mlp_g,
              final_norm_g, ab_w_in, ab_w_out, lambda_q1, lambda_k1, lambda_q2,
              lambda_k2, subln_g, pool_w, pool_scale, conv_w_in, conv_w, conv_w_out,
              xattn_wq, xattn_wkv, xattn_wo, mlp_w1, mlp_w2):
    S = x.shape[1]
    buckets = t5_bucket(jnp.arange(S, dtype=jnp.int32))
    bias_dist = rel_bias.astype(jnp.float32)[buckets].T
    mem_n = rms_norm(mem, mem_norm_g)
    h = x
    for l in range(DEPTH):
        i = l // 2
        hn = rms_norm(h, norm_mix_g[l])
        if l % 2 == 0:
            lambda_init = 0.8 - 0.6 * math.exp(-0.3 * l)
            h = h + even_mixer(hn, ab_w_in[i], ab_w_out[i], lambda_q1[i], lambda_k1[i],
                               lambda_q2[i], lambda_k2[i], subln_g[i], pool_w[i],
                               pool_scale[i], bias_dist, lambda_init)
        else:
            h = h + odd_mixer(hn, conv_w_in[i], conv_w[i], conv_w_out[i])
        h = h + cross_attention(rms_norm(h, norm_xattn_g[l]), mem_n,
                                xattn_wq[l], xattn_wkv[l], xattn_wo[l])
        h = h + sq_relu_mlp(rms_norm(h, norm_mlp_g[l]), mlp_w1[l], mlp_w2[l])
    return rms_norm(h, final_norm_g)
```

```python
import math
import numpy as np
from contextlib import ExitStack
import concourse.bass as bass
import concourse.mybir as mybir
from concourse.bass_utils import run_bass_kernel_spmd

F32 = mybir.dt.float32
BF16 = mybir.dt.bfloat16
AF = mybir.ActivationFunctionType
ALU = mybir.AluOpType

D = 1024
NMEM = 256
EPS = 1e-6
G = 512
FW = 1152


class Buf:
    __slots__ = ("w", "r", "al", "stamp")

    def __init__(self):
        self.w = None
        self.r = {}
        self.al = ()
        self.stamp = 0


class Op:
    __slots__ = ("fn", "waits", "signal", "dma")

    def __init__(self, fn, waits):
        self.fn = fn
        self.waits = waits
        self.signal = False
        self.dma = None


class Prog:
    ENGS = ("pe", "act", "dve", "pool", "sp")

    def __init__(self, nc, es):
        self.nc = nc
        self.es = es
        self.ops = {e: [] for e in self.ENGS}
        self.ncomp = {e: 0 for e in self.ENGS}
        self.comp_idx = {e: [] for e in self.ENGS}
        self.seen = {e: {} for e in self.ENGS}
        self.sems = {e: es.enter_context(nc.semaphore("s_" + e)) for e in self.ENGS}
        self.dma_cnt = {}
        self.gctr = 0

    def dma_sem(self, name):
        s = self.es.enter_context(self.nc.semaphore(name))
        self.dma_cnt[id(s)] = [s, 0]
        return s

    def _need(self, eng, ev, waits, same_ok):
        if ev is None:
            return
        if ev[0] == 'c' and ev[1] == eng and not same_ok and eng == "pe":
            return
        key = (ev[0], ev[1])
        if self.seen[eng].get(key, -1) >= ev[2]:
            return
        self.seen[eng][key] = ev[2]
        waits.append(ev)

    def op(self, eng, fn, reads=(), writes=(), dma=None):
        waits = []
        for b in reads:
            self._need(eng, b.w, waits, True)
        for b in writes:
            self._need(eng, b.w, waits, False)
            for ev in b.r.values():
                self._need(eng, ev, waits, False)
            for a in b.al:
                self._need(eng, a.w, waits, True)
                for ev in a.r.values():
                    self._need(eng, ev, waits, True)
        o = Op(fn, waits)
        if dma is not None:
            ent = self.dma_cnt[id(dma)]
            ent[1] += 16
            o.dma = dma
            ev = ('d', id(dma), ent[1])
        else:
            seq = self.ncomp[eng]
            self.ncomp[eng] += 1
            self.comp_idx[eng].append(len(self.ops[eng]))
            ev = ('c', eng, seq)
        self.ops[eng].append(o)
        self.gctr += 1
        for b in reads:
            b.r[(ev[0], ev[1])] = ev
            b.stamp = self.gctr
        for b in writes:
            b.w = ev
            b.r = {}
            b.stamp = self.gctr
        return ev

    def wait_event(self, eng, ev):
        waits = []
        self._need(eng, ev, waits, True)
        if waits:
            self.ops[eng].append(Op(None, waits))

    def emit(self):
        nc = self.nc
        for e in self.ENGS:
            for o in self.ops[e]:
                for ev in o.waits:
                    if ev[0] == 'c':
                        self.ops[ev[1]][self.comp_idx[ev[1]][ev[2]]].signal = True
        cnt = {}
        for e in self.ENGS:
            c = 0
            arr = []
            for idx in self.comp_idx[e]:
                if self.ops[e][idx].signal:
                    c += 1
                arr.append(c)
            cnt[e] = arr
        sem_by_id = {k: v[0] for k, v in self.dma_cnt.items()}

        def run(e, h):
            for o in self.ops[e]:
                for ev in o.waits:
                    if ev[0] == 'c':
                        h.wait_ge(self.sems[ev[1]], cnt[ev[1]][ev[2]])
                    else:
                        h.wait_ge(sem_by_id[ev[1]], ev[2])
                if o.fn is None:
                    continue
                ins = o.fn(h)
                if o.dma is not None:
                    ins.then_inc(o.dma, 16)
                elif o.signal:
                    ins.then_inc(self.sems[e], 1)

        with nc.Block() as block:
            @block.tensor
            def _(h):
                run("pe", h)

            @block.scalar
            def _(h):
                run("act", h)

            @block.vector
            def _(h):
                run("dve", h)

            @block.gpsimd
            def _(h):
                run("pool", h)

            @block.sync
            def _(h):
                run("sp", h)


def t5_bucket_np(n):
    n = np.asarray(n, dtype=np.int64)
    nf = np.maximum(n, 1).astype(np.float32)
    large = 16 + (np.log(nf / np.float32(16)) / np.float32(math.log(128 / 16)) * np.float32(16)).astype(np.int32)
    large = np.minimum(large, 31)
    return np.where(n < 16, n, large)


def host_consts():
    ident = np.eye(128, dtype=np.float32)
    onehot = np.zeros((33, FW), np.float32)
    for i in range(FW):
        d = i - 511
        if d < 0:
            onehot[32, i] = -240000.0
        else:
            onehot[int(t5_bucket_np(d)), i] = 8.0
    invc = np.zeros((128, 4, 16), np.float32)
    for gi, w in enumerate((2, 4, 8, 16)):
        for t in range(16):
            invc[:, gi, t] = 1.0 / min(t + 1, w)
    return {"c_ident": ident, "c_onehot": onehot, "c_invc": invc}


WSPEC = [
    ("ab_w_in", "even", 1024, 2048), ("ab_w_out", "even", 1024, 1024),
    ("conv_w_in", "odd", 1024, 3072), ("conv_w_out", "odd", 1024, 1024),
    ("xattn_wkv", "all", 1024, 2048), ("xattn_wq", "all", 1024, 1024), ("xattn_wo", "all", 1024, 1024),
    ("mlp_w1", "all", 1024, 4096), ("mlp_w2", "all", 4096, 1024),
]


def _I(name, *a, **k):
    return lambda h: getattr(h, name)(*a, **k)


def build_program(S, NSEQ, DEPTH=4):
    NG = S // G
    NT = S // 128
    nc = bass.Bass("TRN2", target_bir_lowering=False)
    n_even = (DEPTH + 1) // 2
    n_odd = DEPTH // 2

    def din(name, shape):
        return nc.dram_tensor(name, list(shape), F32, kind="ExternalInput")

    x = din("x", [NSEQ, S, D]).ap()
    mem = din("mem", [NSEQ, NMEM, D]).ap()
    rel_bias = din("rel_bias", [32, 4]).ap()
    mem_norm_g = din("mem_norm_g", [1, D]).ap()
    norm_mix_g = din("norm_mix_g", [DEPTH, D]).ap()
    norm_xattn_g = din("norm_xattn_g", [DEPTH, D]).ap()
    norm_mlp_g = din("norm_mlp_g", [DEPTH, D]).ap()
    final_norm_g = din("final_norm_g", [1, D]).ap()
    W = {}
    for name, kind, K, N in WSPEC:
        n = {"even": n_even, "odd": n_odd, "all": DEPTH}[kind]
        W[name] = din(name, [max(n, 1), K, N]).ap()
    lam_in = [din(nm, [max(n_even, 1), 64]).ap() for nm in ("lambda_q1", "lambda_k1", "lambda_q2", "lambda_k2")]
    subln_t = din("subln_g", [max(n_even, 1), 128])
    pool_w = din("pool_w", [max(n_even, 1), 4, 128, 128]).ap()
    pscale_t = din("pool_scale", [max(n_even, 1), 512])
    convw_t = din("conv_w", [max(n_odd, 1), 3, D])
    c_ident = din("c_ident", [128, 128]).ap()
    c_onehot = din("c_onehot", [33, FW]).ap()
    c_invc = din("c_invc", [128, 4, 16]).ap()
    y = nc.dram_tensor("y", [NSEQ, S, D], F32, kind="ExternalOutput").ap()

    def col_ap(t, off):
        return bass.AP(t, off, [[1, 128], [1, 1]])

    blk_ids = {}
    nblk = 0
    for l in range(DEPTH):
        for name, kind, K, N in WSPEC:
            if (kind == "even" and l % 2 == 1) or (kind == "odd" and l % 2 == 0):
                continue
            for kb in range(K // 1024):
                for nb in range(N // 512):
                    blk_ids[(l, name, kb, nb)] = nblk
                    nblk += 1
    wblk = nc.dram_tensor("wblk", [nblk, 128, 4096], BF16).ap()
    poolw_bf = nc.dram_tensor("poolw_bf", [max(n_even, 1), 128, 512], BF16).ap()
    hscr = nc.dram_tensor("hscr", [NSEQ, S, D], F32).ap()
    Fd_t = nc.dram_tensor("Fd", [4, FW], BF16)
    Fd = Fd_t.ap()
    FR = 130
    Fd2_t = nc.dram_tensor("Fd2", [4, FR, FW], BF16)
    Fd2 = Fd2_t.ap()

    with ExitStack() as es:
        P = Prog(nc, es)

        def sb(name, shape, dt):
            return es.enter_context(nc.sbuf_tensor(name, list(shape), dt))

        hres2 = [sb(f"hres{i}", [128, 4, D], F32) for i in range(2)]
        hnT = sb("hnT", [128, 8, G], BF16)
        hn_tmps = [sb(f"hn_tmp{i}", [128, D], BF16) for i in range(2)]
        wring = [sb(f"wr{i}", [128, 8, 512], BF16) for i in range(4)]
        memT = sb("memT", [128, 8, NMEM], BF16)
        xKT = sb("xKT", [128, 8, NMEM], BF16)
        xV = sb("xV", [128, 2, D], BF16)
        gb = sb("gb", [128, D], F32)
        KT = sb("KT", [128, 4, S], BF16)
        Vc = sb("Vc", [128, NT, 512], BF16)
        featT = sb("featT", [128, 8, G], BF16)
        qT = sb("qT", [128, 8, G], BF16)
        aT = sb("aT", [128, 16, G], BF16)
        PT = [sb(f"PT{i}", [128, G], BF16) for i in range(4)]
        epall = sb("epall", [128, 4 * G], F32)
        ep = [epall[:, i * G:(i + 1) * G] for i in range(4)]
        aT32 = aT[:].rearrange("p c n -> p (c n)").bitcast(F32)
        uT = aT32[:, 0:2112].rearrange("p (g n) -> p g n", g=4)
        pw = [aT32[:, 2112 + i * 528:2112 + (i + 1) * 528] for i in range(2)]
        pTb = [aT32[:, 3168 + i * 256:3168 + (i + 1) * 256].bitcast(BF16) for i in range(2)]
        biasT = sb("biasT", [128, 5, G], BF16)
        zbuf = [aT32[:, 0:514]] * 2
        zh = sb("zh", [128, 8, 2], F32)
        uh = sb("uh", [128, 4, 16], F32)
        idf = pw[1][:, 0:128]
        idb = sb("idb", [128, 128], BF16)
        onesb = sb("onesb", [128, 128], BF16)
        stt = sb("stt", [128, 4, 4], F32)
        b31 = sb("b31", [128, 4], F32)
        lamt = pw[0][:, 0:512].rearrange("p (i k d) -> p i k d", i=2, k=4)
        lams = sb("lams", [128, 8], F32)
        neglam = sb("neglam", [128, 2], F32)
        gs = sb("gs", [128, 2], F32)
        pscale = sb("pscale", [128, 2, 4], F32)
        poolw_sb = sb("poolw_sb", [128, 4, 128], BF16)
        convw = sb("convw", [128, 2, 3, 8], F32)
        invc = sb("invc", [128, 4, 16], F32)
        rb33 = sb("rb33", [33, 4], F32)
        oh = epall[0:33, 0:FW]
        Fsb = featT[:].rearrange("p c n -> p (c n)")[0:4, 0:FW]

        banks = [es.enter_context(nc.psum_tensor(f"bk{i}", [128, 512], F32)) for i in range(8)]
        bankB = [Buf() for _ in range(8)]
        pinned = set()
        rot = [0]

        def alloc_bank():
            best = None
            for k in range(8):
                i = (rot[0] + k) % 8
                if i in pinned:
                    continue
                if best is None or bankB[i].stamp < bankB[best].stamp:
                    best = i
            rot[0] = best + 1
            bankB[best].stamp = P.gctr + 1
            return best

        Bd = {}

        def bf(name):
            if name not in Bd:
                Bd[name] = Buf()
            return Bd[name]

        _al = [bf("uT"), bf(("pw", 0)), bf(("pw", 1)), bf(("pTb", 0)), bf(("pTb", 1)), bf(("zb", 0))]
        aTB = [bf(("aT", k)) for k in range(16)]
        ftB = [bf(("featT", k)) for k in range(8)]
        qTB = [bf(("qT", k)) for k in range(8)]
        for _a in aTB:
            _a.al = tuple(_al)
        for _b in _al:
            _b.al = tuple(aTB)
        bf("uT").al = tuple(aTB) + (bf(("zb", 0)),)
        bf(("zb", 0)).al = tuple(aTB) + (bf("uT"),)
        gbf = biasT[:].rearrange("p r n -> p (r n)").bitcast(F32)[:, 0:D]
        hB2 = [[Buf() for _ in range(4)] for _ in range(2)]
        junk = qT[:].rearrange("p c n -> p (c n)")[:, 0:D]

        s_misc = P.dma_sem("d_misc")
        n_extra = min(4, NT // 8)
        s_wr = [P.dma_sem(f"d_wr{i}") for i in range(4 + n_extra)]
        s_h = [P.dma_sem("d_h0"), P.dma_sem("d_h1")]
        s_hst = [P.dma_sem("d_hst0"), P.dma_sem("d_hst1")]
        s_gb = P.dma_sem("d_gb")
        s_bias = P.dma_sem("d_bias")
        s_f = P.dma_sem("d_f")
        s_pw = P.dma_sem("d_pw")
        s_f2 = P.dma_sem("d_f2")

        evac_ctr = [0]
        ptc = [0]

        def evac(bi, src_ap, dst_ap, dst_bufs, eng=None):
            if eng is None:
                eng = "act" if evac_ctr[0] % 2 == 0 else "dve"
                evac_ctr[0] += 1
            if eng == "act":
                P.op("act", _I("copy", out=dst_ap, in_=src_ap), reads=[bankB[bi]], writes=dst_bufs)
            else:
                P.op("dve", _I("tensor_copy", out=dst_ap, in_=src_ap), reads=[bankB[bi]], writes=dst_bufs)

        setup_bufs = []

        def sdma(out_ap, in_ap, bname):
            P.op("sp", _I("dma_start", out=out_ap, in_=in_ap), writes=[bf(bname)], dma=s_misc)
            if bf(bname) not in setup_bufs:
                setup_bufs.append(bf(bname))

        sdma(idf, c_ident, ("pw", 1))
        sdma(invc[:], c_invc, "invc")
        sdma(b31[:], rel_bias[31:32, :].partition_broadcast(128), "b31")
        sdma(oh, c_onehot, "ep0"); sdma(oh, c_onehot, "ep1") if False else None; setup_bufs.extend([bf("ep1"), bf("ep2")])
        sdma(rb33[0:32, :], rel_bias, "rb33")
        for i in range(n_even):
            for k in range(4):
                sdma(lamt[:, i, k, :], lam_in[k][i:i + 1, :].partition_broadcast(128), ("pw", 0))
            sdma(gs[:, i:i + 1], col_ap(subln_t, i * 128), "gs")
            for gi in range(4):
                sdma(pscale[:, i, gi:gi + 1], col_ap(pscale_t, i * 512 + gi * 128), "pscale")
        for i in range(n_odd):
            for k in range(3):
                for c in range(8):
                    sdma(convw[:, i, k, c:c + 1], col_ap(convw_t, (i * 3 + k) * D + c * 128), "convw")
        fence = ('d', id(s_misc), P.dma_cnt[id(s_misc)][1])
        for b in setup_bufs:
            b.w = fence

        P.op("dve", _I("tensor_copy", out=idb[:], in_=idf), reads=[bf(("pw", 1))], writes=[bf("idb")])
        P.op("dve", _I("memset", onesb[:], 1.0), writes=[bf("onesb")])
        P.op("dve", _I("memset", rb33[32:33, :], 1.0), reads=[bf("rb33")], writes=[bf("rb33x")])
        for j0 in range(0, FW, 384):
            bi = alloc_bank()
            P.op("pe", _I("matmul", banks[bi][0:4, 0:384], lhsT=rb33[:, :], rhs=oh[:, j0:j0 + 384], start=True, stop=True),
                 reads=[bf("rb33"), bf("rb33x"), bf("ep0"), bf("ep1"), bf("ep2")], writes=[bankB[bi]])
            evac(bi, banks[bi][0:4, 0:384], Fsb[:, j0:j0 + 384], ftB, eng="dve")
        P.op("sp", _I("dma_start", out=Fd, in_=Fsb), reads=ftB, writes=[bf("Fd0")], dma=s_f)
        for hh in range(4):
            P.op("sp", _I("dma_start", out=Fd2[hh], in_=Fd[hh:hh + 1, :].partition_broadcast(FR)), reads=[bf("Fd0")], writes=[bf("Fd")],
                 dma=s_f2)
        bf("Fd").w = ('d', id(s_f2), P.dma_cnt[id(s_f2)][1])
        for i in range(n_even):
            lam_init = 0.8 - 0.6 * math.exp(-0.3 * (2 * i))
            for m in range(2):
                P.op("dve", _I("tensor_tensor", out=lamt[:, i, 2 * m, :], in0=lamt[:, i, 2 * m, :], in1=lamt[:, i, 2 * m + 1, :], op=ALU.mult),
                     reads=[bf(("pw", 0))], writes=[bf(("pw", 0))])
                P.op("act", _I("activation", out=lamt[:, i, 2 * m + 1, :], in_=lamt[:, i, 2 * m, :], func=AF.Identity,
                               accum_out=lams[:, m:m + 1]),
                     reads=[bf(("pw", 0))], writes=[bf(("pw", 0)), bf("lams")])
                P.op("act", _I("activation", out=lams[:, 2 + m:3 + m], in_=lams[:, m:m + 1], func=AF.Exp),
                     reads=[bf("lams")], writes=[bf("lams")])
            P.op("dve", _I("tensor_tensor", out=lams[:, 4:5], in0=lams[:, 3:4], in1=lams[:, 2:3], op=ALU.subtract),
                 reads=[bf("lams")], writes=[bf("lams")])
            P.op("dve", _I("tensor_scalar", out=neglam[:, i:i + 1], in0=lams[:, 4:5], scalar1=-lam_init, scalar2=None, op0=ALU.add),
                 reads=[bf("lams")], writes=[bf("neglam")])
            P.op("dve", _I("tensor_scalar", out=gs[:, i:i + 1], in0=gs[:, i:i + 1], scalar1=1.0 - lam_init, scalar2=None, op0=ALU.mult),
                 reads=[bf("gs")], writes=[bf("gs")])

        wconv_buf = {}
        NCV = 64
        s_cv = [P.dma_sem(f"d_cv{k}") for k in range(NCV)]
        cv_evs = []

        cv_pending = []

        def cv_dma(fn, b, lazy=True):
            if lazy:
                cv_pending.append((fn, b))
                return
            k = len(cv_evs)
            if k >= NCV:
                P.wait_event("pool", cv_evs[k - NCV])
            cv_evs.append(P.op("pool", fn, writes=[b], dma=s_cv[k % NCV]))

        CV_ORDER = ["xattn_wkv", "ab_w_in", "conv_w_in", "ab_w_out", "conv_w_out", "xattn_wq", "xattn_wo", "mlp_w1", "mlp_w2"]
        wspec_sorted = sorted(WSPEC, key=lambda w: CV_ORDER.index(w[0]))
        for l in range(DEPTH):
            if l % 2 == 0:
                wconv_buf[(l, "pool_w")] = Buf()
            for name, kind, K, N in WSPEC:
                if (kind == "even" and l % 2 == 1) or (kind == "odd" and l % 2 == 0):
                    continue
                for kb in range(K // 1024):
                    for nb in range(N // 512):
                        wconv_buf[blk_ids[(l, name, kb, nb)]] = Buf()

        def convert_layer(l):
            i = l // 2
            if l % 2 == 0:
                pb = wconv_buf[(l, "pool_w")]
                cv_dma(_I("dma_start", out=poolw_bf[i].rearrange("c (g d) -> c g d", g=4), in_=pool_w[i].rearrange("g c d -> c g d")), pb)
            for name, kind, K, N in wspec_sorted:
                if (kind == "even" and l % 2 == 1) or (kind == "odd" and l % 2 == 0):
                    continue
                li = l if kind == "all" else i
                for kb in range(K // 1024):
                    for nb in range(N // 512):
                        bid = blk_ids[(l, name, kb, nb)]
                        b = wconv_buf[bid]
                        src = W[name][li, kb * 1024:(kb + 1) * 1024, nb * 512:(nb + 1) * 512].rearrange("(c p) n -> p c n", p=128)
                        dst = wblk[bid].rearrange("p (c n) -> p c n", c=8)
                        cv_dma(_I("dma_start", out=dst, in_=src), b)

        def cv_flush(n):
            for _ in range(n):
                if cv_pending:
                    cv_dma(*cv_pending.pop(0), lazy=False)

        convert_layer(0)
        cv_flush(1000)

        wr_ctr = [0]
        wrB = [Buf() for _ in range(4 + n_extra)]
        for k in range(n_extra):
            wring.append(Vc[:, 8 * k:8 * k + 8, :])
            vb = tuple(bf(("V", gg)) for gg in (2 * k, 2 * k + 1) if gg < NG)
            wrB[4 + k].al = vb
            for b_ in vb:
                b_.al = (wrB[4 + k],)

        def load_block(l, name, kb, nb):
            bid = blk_ids[(l, name, kb, nb)]
            slot = wr_ctr[0] % (4 + n_extra if l % 2 == 1 else 4)
            wr_ctr[0] += 1
            P.op("sp", _I("dma_start", out=wring[slot][:], in_=wblk[bid].rearrange("p (c n) -> p c n", c=8)),
                 reads=[wconv_buf[bid]], writes=[wrB[slot]], dma=s_wr[slot])
            return slot

        def load_gain(row_ap, final=False):
            if final:
                P.op("pool", _I("dma_start", out=gbf, in_=row_ap.partition_broadcast(128)), writes=[bf("biasT")], dma=s_bias)
            else:
                P.op("pool", _I("dma_start", out=gb[:], in_=row_ap.partition_broadcast(128)), writes=[bf("gb")], dma=s_gb)

        st_ctr = [0]

        def rms_rstd(src_ap, src_bufs, n):
            k = st_ctr[0] % 4
            st_ctr[0] += 1
            sk = stt[:, k, :]
            sB = bf(("st", k))
            P.op("act", _I("activation", out=junk[:, 0:n], in_=src_ap, func=AF.Square, accum_out=sk[:, 0:1]),
                 reads=src_bufs, writes=[qTB[0], qTB[1], sB])
            P.op("act", _I("activation", out=sk[:, 1:2], in_=sk[:, 0:1], func=AF.Ln, scale=1.0 / n, bias=EPS),
                 reads=[sB], writes=[sB])
            P.op("act", _I("activation", out=sk[:, 2:3], in_=sk[:, 1:2], func=AF.Exp, scale=-0.5),
                 reads=[sB], writes=[sB])
            return sk[:, 2:3], sB

        nrm_ctr = [0]

        def norm_pre(src_ap, src_buf):
            slot = nrm_ctr[0] % 2
            nrm_ctr[0] += 1
            rs, sB = rms_rstd(src_ap, [src_buf], D)
            for hf in range(2):
                P.op("dve", _I("scalar_tensor_tensor", out=hn_tmps[slot][:, hf * 512:(hf + 1) * 512], in0=src_ap[:, hf * 512:(hf + 1) * 512],
                               scalar=rs, in1=gb[:, hf * 512:(hf + 1) * 512], op0=ALU.mult, op1=ALU.mult),
                     reads=[src_buf, sB, bf("gb")], writes=[bf(("hn_tmp", slot, hf))])
            return slot

        def norm_tr(slot, t, dstT, dst_buf):
            bi = alloc_bank()
            pv = banks[bi][:].bitcast(BF16).rearrange("p (c n) -> p c n", c=8)
            for c in range(8):
                P.op("pe", _I("transpose", out=pv[:, c, :], in_=hn_tmps[slot][:, c * 128:(c + 1) * 128], identity=idb[:]),
                     reads=[bf(("hn_tmp", slot, c // 4)), bf("idb")], writes=[bankB[bi]])
            evac(bi, pv, dstT[:, :, t * 128:(t + 1) * 128], [dst_buf])

        def norm_T(src_aps, src_bufs, dstT, dst_buf, ntile):
            slots = {}
            for t in range(ntile):
                slots[t] = norm_pre(src_aps[t], src_bufs[t])
                if t >= 1:
                    norm_tr(slots[t - 1], t - 1, dstT, dst_buf)
            norm_tr(slots[ntile - 1], ntile - 1, dstT, dst_buf)

        def proj_add(l, name, hcur, hB, nxt_gain=None, corder=(0, 1, 2, 3, 4, 5, 6, 7)):
            slots = [load_block(l, name, 0, nb) for nb in range(2)]
            if nxt_gain is not None:
                load_gain(nxt_gain)
            nslot = {}
            for t in range(4):
                for nb in range(2):
                    bi = alloc_bank()
                    for ci, c in enumerate(corder):
                        P.op("pe", _I("matmul", banks[bi][:], lhsT=featT[:, c, t * 128:(t + 1) * 128], rhs=wring[slots[nb]][:, c, :],
                                      start=(ci == 0), stop=(ci == 7)),
                             reads=[ftB[c], wrB[slots[nb]]], writes=[bankB[bi]])
                    P.op("dve", _I("tensor_tensor", out=hcur[:, t, nb * 512:(nb + 1) * 512], in0=banks[bi][:],
                                   in1=hcur[:, t, nb * 512:(nb + 1) * 512], op=ALU.add),
                         reads=[bankB[bi], hB[t]], writes=[hB[t]])
                if nxt_gain is not None:
                    nslot[t] = norm_pre(hcur[:, t, :], hB[t])
                    if t >= 1:
                        norm_tr(nslot[t - 1], t - 1, hnT, bf("hnT"))
            if nxt_gain is not None:
                norm_tr(nslot[3], 3, hnT, bf("hnT"))

        def featmajor_proj(slot, col0, dst_ap, dst_bufs, rhs_ap=None, rhs_buf=None, n=G, eng=None):
            bi = alloc_bank()
            rhs_ap = hnT if rhs_ap is None else rhs_ap
            rhs_buf = bf("hnT") if rhs_buf is None else rhs_buf
            for c in range(8):
                P.op("pe", _I("matmul", banks[bi][:, 0:n], lhsT=wring[slot][:, c, col0:col0 + 128], rhs=rhs_ap[:, c, 0:n],
                              start=(c == 0), stop=(c == 7)),
                     reads=[wrB[slot], rhs_buf], writes=[bankB[bi]])
            if dst_ap is not None:
                evac(bi, banks[bi][:, 0:n], dst_ap, dst_bufs, eng=eng)
            return bi

        qT4 = qT[:].rearrange("p (h m) n -> p h m n", m=2)
        def load_h(ub, s, l, g):
            srcT = x if l == 0 else hscr
            P.op("pool", _I("dma_start", out=hres2[ub][:], in_=srcT[s, g * G:(g + 1) * G, :].rearrange("(t p) d -> p t d", p=128)),
                 reads=[bf(("hscr", s, g))], writes=hB2[ub], dma=s_h[ub])

        final_ev = None
        unit = 0
        gain_ready = False
        for s in range(NSEQ):
            ub0 = unit % 2
            load_gain(mem_norm_g[0:1, :])
            P.op("pool", _I("dma_start", out=hres2[ub0][:, 0:2, :], in_=mem[s].rearrange("(t p) d -> p t d", p=128)),
                 writes=hB2[ub0][0:2], dma=s_h[ub0])
            norm_T([hres2[ub0][:, t, :] for t in range(2)], hB2[ub0][0:2], memT, bf("memT"), 2)
            load_h(ub0, s, 0, 0)
            gain_ready = False
            norm_done = False

            for l in range(DEPTH):
                i = l // 2
                even = (l % 2 == 0)
                last = (l == DEPTH - 1)
                slots = [load_block(l, "xattn_wkv", 0, nb) for nb in range(4)]
                for kc in range(8):
                    featmajor_proj(slots[kc // 4], (kc % 4) * 128, xKT[:, kc, :], [bf("xKT")], rhs_ap=memT, rhs_buf=bf("memT"), n=NMEM)
                for mt in range(2):
                    for half in range(2):
                        bi = alloc_bank()
                        for c in range(8):
                            P.op("pe", _I("matmul", banks[bi][:], lhsT=memT[:, c, mt * 128:(mt + 1) * 128], rhs=wring[slots[2 + half]][:, c, :],
                                          start=(c == 0), stop=(c == 7)),
                                 reads=[bf("memT"), wrB[slots[2 + half]]], writes=[bankB[bi]])
                        evac(bi, banks[bi][:], xV[:, mt, half * 512:(half + 1) * 512], [bf("xV")])
                if even:
                    P.op("sp", _I("dma_start", out=poolw_sb[:], in_=poolw_bf[i].rearrange("c (g d) -> c g d", g=4)),
                         reads=[wconv_buf[(l, "pool_w")]], writes=[bf("poolw_sb")], dma=s_pw)
                    P.op("pool", _I("memset", uh[:], 0.0), writes=[bf("uh")])
                else:
                    P.op("pool", _I("memset", zh[:], 0.0), writes=[bf("zh")])
                if s == 0 and l + 1 < DEPTH:
                    convert_layer(l + 1)
                    cv_per_unit = -(-len(cv_pending) // NG)

                for g in range(NG):
                    tok0 = g * G
                    ub = unit % 2
                    hcur = hres2[ub]
                    hB = hB2[ub]
                    hap = [hcur[:, t, :] for t in range(4)]
                    if g + 1 < NG:
                        nxt = (l, g + 1)
                    elif l + 1 < DEPTH:
                        nxt = (l + 1, 0)
                    else:
                        nxt = None

                    if not norm_done:
                        if not gain_ready:
                            load_gain(norm_mix_g[l:l + 1, :])
                        norm_T(hap, hB, hnT, bf("hnT"), 4)
                    norm_done = False
                    if even:
                        sq = load_block(l, "ab_w_in", 0, 0)
                        sk = load_block(l, "ab_w_in", 0, 1)
                        sv = load_block(l, "ab_w_in", 0, 2)
                        su = load_block(l, "ab_w_in", 0, 3)
                        P.op("pool", _I("memset", qT4[64:128, :, 0, :], 0.0), writes=qTB)
                        P.op("pool", _I("memset", qT4[0:64, :, 1, :], 0.0), writes=qTB)
                        for hh in range(4):
                            bi = featmajor_proj(sq, hh * 128, None, None)
                            evac(bi, banks[bi][0:64, :], qT4[0:64, hh, 0, :], [qTB[2 * hh]])
                            evac(bi, banks[bi][64:128, :], qT4[64:128, hh, 1, :], [qTB[2 * hh + 1]])
                        for hh in range(4):
                            featmajor_proj(sk, hh * 128, KT[:, hh, tok0:tok0 + G], [bf(("KT", g))])
                        for t in range(4):
                            bi = alloc_bank()
                            for c in range(8):
                                P.op("pe", _I("matmul", banks[bi][:], lhsT=hnT[:, c, t * 128:(t + 1) * 128], rhs=wring[sv][:, c, :],
                                              start=(c == 0), stop=(c == 7)),
                                     reads=[bf("hnT"), wrB[sv]], writes=[bankB[bi]])
                            evac(bi, banks[bi][:], Vc[:, g * 4 + t, :], [bf(("V", g))])
                        for gi in range(4):
                            featmajor_proj(su, gi * 128, uT[:, gi, 16:528], [bf("uT")])
                        P.op("pool", _I("tensor_copy", out=uT[:, :, 0:16], in_=uh[:]), reads=[bf("uh")], writes=[bf("uT")])
                        pts4 = [(pTb[0], bf(("pTb", 0))), (pTb[1], bf(("pTb", 1))),
                                (hn_tmps[0][:, 0:G], bf(("hn_tmp", 0, 0))), (hn_tmps[0][:, G:2 * G], bf(("hn_tmp", 0, 1)))]
                        for gi, wdw in enumerate((2, 4, 8, 16)):
                            nst = int(math.log2(wdw))
                            cur = uT[:, gi, :]
                            curB = bf("uT")
                            for k in range(nst):
                                sh = 1 << k
                                lo = 2 * sh - 1
                                dstb = pw[k % 2]
                                P.op("pool", _I("tensor_tensor", out=dstb[:, lo:528], in0=cur[:, lo:528], in1=cur[:, lo - sh:528 - sh], op=ALU.add),
                                     reads=[curB], writes=[bf(("pw", k % 2))])
                                cur = dstb
                                curB = bf(("pw", k % 2))
                            pt, ptB = pts4[gi]
                            P.op("dve", _I("scalar_tensor_tensor", out=pt, in0=cur[:, 16:528], scalar=1.0 / wdw, in1=uT[:, gi, 16:528],
                                           op0=ALU.mult, op1=ALU.subtract),
                                 reads=[curB, bf("uT")], writes=[ptB])
                            if g == 0:
                                P.op("dve", _I("tensor_tensor", out=ep[0][:, 0:16], in0=cur[:, 16:32], in1=invc[:, gi, :], op=ALU.mult),
                                     reads=[curB, bf("invc")], writes=[bf("ep0")])
                                P.op("dve", _I("tensor_tensor", out=pt[:, 0:16], in0=ep[0][:, 0:16], in1=uT[:, gi, 16:32], op=ALU.subtract),
                                     reads=[bf("ep0"), bf("uT")], writes=[ptB])
                        P.op("pool", _I("tensor_copy", out=uh[:], in_=uT[:, :, 512:528]), reads=[bf("uT")], writes=[bf("uh")])

                        def pool_part_b(i=i, pts4=pts4):
                            for gi in range(4):
                                pt, ptB = pts4[gi]
                                bi = alloc_bank()
                                P.op("pe", _I("matmul", banks[bi][:], lhsT=poolw_sb[:, gi, :], rhs=pt, start=True, stop=True),
                                     reads=[bf("poolw_sb"), ptB], writes=[bankB[bi]])
                                P.op("act", _I("activation", out=featT[:, 4 + gi, :], in_=banks[bi][:], func=AF.Copy, scale=pscale[:, i, gi:gi + 1]),
                                     reads=[bankB[bi], bf("pscale")], writes=[ftB[4 + gi]])

                        nk = 4 * g + 4
                        sqb = hn_tmps[1][:, 0:G]
                        sqB = bf(("hn_tmp", 1, 0))
                        pend_ep1 = []
                        pend_p2 = []
                        for hh in range(4):
                            src_ap = bass.AP(Fd2_t, hh * FR * FW + 127, [[FW - 1, 128], [128, 5], [1, G]])
                            P.op("pool", _I("dma_start", out=biasT[:], in_=src_ap), reads=[bf("Fd")], writes=[bf("biasT")], dma=s_bias)
                            for m in range(2):
                                acc = []

                                def do_pv(j, ptile, ptB_, c0, hh=hh, acc=acc, nk=nk):
                                    if not acc:
                                        for _ in range(2):
                                            bi_ = alloc_bank()
                                            pinned.add(bi_)
                                            acc.append(bi_)
                                    P.op("pe", _I("matmul", banks[acc[0]][:, c0:G], lhsT=Vc[:, j, hh * 128:(hh + 1) * 128], rhs=ptile[:, c0:G],
                                                  start=(j == 0), stop=(j == nk - 1)),
                                         reads=[bf(("V", j // 4)), ptB_], writes=[bankB[acc[0]]])
                                    P.op("pe", _I("matmul", banks[acc[1]][:, c0:G], lhsT=onesb[:], rhs=ptile[:, c0:G],
                                                  start=(j == 0), stop=(j == nk - 1)),
                                         reads=[bf("onesb"), ptB_], writes=[bankB[acc[1]]])

                                pend = []
                                for j in range(nk):
                                    near = j >= 4 * g - 1
                                    ri = j - 4 * g + 1
                                    c0 = 128 * (j - 4 * g) if j > 4 * g else 0
                                    bi = alloc_bank()
                                    if near:
                                        P.op("pe", _I("matmul", banks[bi][:, c0:G], lhsT=idb[:], rhs=biasT[:, 4 - ri, c0:G], start=True, stop=False),
                                             reads=[bf("idb"), bf("biasT")], writes=[bankB[bi]])
                                    P.op("pe", _I("matmul", banks[bi][:, c0:G], lhsT=KT[:, hh, j * 128:(j + 1) * 128], rhs=qT4[:, hh, m, c0:G],
                                                  start=(not near), stop=True),
                                         reads=[bf(("KT", j // 4)), qTB[2 * hh + m]], writes=[bankB[bi]])
                                    pidx = ptc[0] % 4
                                    ptc[0] += 1
                                    ptile = PT[pidx]
                                    ptB_ = bf(("PT", pidx))
                                    if near:
                                        P.op("act", _I("activation", out=ptile[:, c0:G], in_=banks[bi][:, c0:G], func=AF.Exp, scale=0.125),
                                             reads=[bankB[bi]], writes=[ptB_])
                                    else:
                                        P.op("act", _I("activation", out=ptile[:, c0:G], in_=banks[bi][:, c0:G], func=AF.Exp, scale=0.125,
                                                       bias=b31[:, hh:hh + 1]),
                                             reads=[bankB[bi], bf("b31")], writes=[ptB_])
                                    pend.append((j, ptile, ptB_, c0))
                                    if j == 0 and pend_ep1:
                                        pend_ep1.pop(0)()
                                    if j == 1 and pend_p2:
                                        pend_p2.pop(0)()
                                    if len(pend) > 2:
                                        do_pv(*pend.pop(0))
                                while pend:
                                    do_pv(*pend.pop(0))

                                def ep1(m=m, acc=acc):
                                    P.op("act", _I("activation", out=ep[m], in_=banks[acc[1]][:], func=AF.Ln),
                                         reads=[bankB[acc[1]]], writes=[bf(f"ep{m}")])
                                    P.op("act", _I("activation", out=ep[m], in_=ep[m], func=AF.Exp, scale=-1.0),
                                         reads=[bf(f"ep{m}")], writes=[bf(f"ep{m}")])
                                    P.op("dve", _I("tensor_tensor", out=ep[2 + m], in0=banks[acc[0]][:], in1=ep[m], op=ALU.mult),
                                         reads=[bankB[acc[0]], bf(f"ep{m}")], writes=[bf(f"ep{2 + m}")])
                                    for bi_ in acc:
                                        pinned.discard(bi_)

                                pend_ep1.append(ep1)

                            def part2(hh=hh, i=i):
                                P.op("dve", _I("scalar_tensor_tensor", out=ep[2], in0=ep[3], scalar=neglam[:, i:i + 1], in1=ep[2],
                                               op0=ALU.mult, op1=ALU.add),
                                     reads=[bf("ep3"), bf("ep2"), bf("neglam")], writes=[bf("ep2")])
                                P.op("act", _I("activation", out=sqb, in_=ep[2], func=AF.Square), reads=[bf("ep2")], writes=[sqB])
                                bi = alloc_bank()
                                P.op("pe", _I("matmul", banks[bi][:], lhsT=onesb[:], rhs=sqb, start=True, stop=True),
                                     reads=[bf("onesb"), sqB], writes=[bankB[bi]])
                                P.op("act", _I("activation", out=ep[0], in_=banks[bi][:], func=AF.Ln, scale=1.0 / 128, bias=EPS),
                                     reads=[bankB[bi]], writes=[bf("ep0")])
                                P.op("act", _I("activation", out=ep[0], in_=ep[0], func=AF.Exp, scale=-0.5),
                                     reads=[bf("ep0")], writes=[bf("ep0")])
                                P.op("dve", _I("scalar_tensor_tensor", out=featT[:, hh, :], in0=ep[2], scalar=gs[:, i:i + 1], in1=ep[0],
                                               op0=ALU.mult, op1=ALU.mult),
                                     reads=[bf("ep2"), bf("ep0"), bf("gs")], writes=[ftB[hh]])

                            pend_p2.append(part2)
                        while pend_ep1:
                            pend_ep1.pop(0)()
                        deferred = pend_p2.pop(0)
                        assert not pend_p2
                        pool_part_b()
                        deferred()
                        proj_add(l, "ab_w_out", hcur, hB, nxt_gain=norm_xattn_g[l:l + 1, :], corder=(4, 5, 6, 7, 0, 1, 2, 3))
                    else:
                        order = [0, 2, 4, 1, 3, 5]
                        slot_of = {}
                        for cc in range(8):
                            if cc % 4 == 0:
                                for nb in order[(cc // 4) * 3:(cc // 4) * 3 + 3]:
                                    slot_of[nb] = load_block(l, "conv_w_in", 0, nb)
                            col = (cc % 4) * 128
                            bb = featmajor_proj(slot_of[cc // 4], col, None, None)
                            featmajor_proj(slot_of[2 + cc // 4], col, ep[0][:], [bf("ep0")], eng="act")
                            bx = featmajor_proj(slot_of[4 + cc // 4], col, None, None)
                            zb = zbuf[cc % 2]
                            zB = bf(("zb", 0))
                            P.op("pool", _I("tensor_copy", out=zb[:, 0:2], in_=zh[:, cc, :]), reads=[bf("zh")], writes=[zB])
                            P.op("dve", _I("tensor_tensor", out=zb[:, 2:514], in0=banks[bx][:], in1=ep[0][:], op=ALU.mult),
                                 reads=[bankB[bx], bf("ep0")], writes=[zB])
                            P.op("pool", _I("tensor_copy", out=zh[:, cc, :], in_=zb[:, 512:514]), reads=[zB], writes=[bf("zh")])
                            P.op("pool", _I("tensor_scalar", out=ep[1][:], in0=zb[:, 0:512], scalar1=convw[:, i, 0, cc:cc + 1], scalar2=0.0,
                                            op0=ALU.mult, op1=ALU.add),
                                 reads=[zB, bf("convw")], writes=[bf("ep1")])
                            for k in (1, 2):
                                P.op("dve", _I("scalar_tensor_tensor", out=ep[1][:], in0=zb[:, k:k + 512], scalar=convw[:, i, k, cc:cc + 1],
                                               in1=ep[1][:], op0=ALU.mult, op1=ALU.add),
                                     reads=[zB, bf("convw"), bf("ep1")], writes=[bf("ep1")])
                            P.op("dve", _I("tensor_tensor", out=featT[:, cc, :], in0=banks[bb][:], in1=ep[1][:], op=ALU.mult),
                                 reads=[bankB[bb], bf("ep1")], writes=[ftB[cc]])
                        proj_add(l, "conv_w_out", hcur, hB, nxt_gain=norm_xattn_g[l:l + 1, :])

                    if nxt is not None:
                        load_h((unit + 1) % 2, s, nxt[0], nxt[1])
                    if s == 0:
                        cv_flush(cv_per_unit if g + 1 < NG else 1000)

                    sqs = [load_block(l, "xattn_wq", 0, nb) for nb in range(2)]
                    for oc in range(8):
                        featmajor_proj(sqs[oc // 4], (oc % 4) * 128, qT[:, oc, :], [qTB[oc]])
                    def x_scores(hh):
                        pts = []
                        for mt in range(2):
                            bi = alloc_bank()
                            for dc in range(2):
                                P.op("pe", _I("matmul", banks[bi][:], lhsT=xKT[:, 2 * hh + dc, mt * 128:(mt + 1) * 128], rhs=qT[:, 2 * hh + dc, :],
                                              start=(dc == 0), stop=(dc == 1)),
                                     reads=[bf("xKT"), qTB[2 * hh + dc]], writes=[bankB[bi]])
                            pidx = (2 * hh + mt) % 4
                            P.op("act", _I("activation", out=PT[pidx][:], in_=banks[bi][:], func=AF.Exp, scale=1.0 / 16),
                                 reads=[bankB[bi]], writes=[bf(("PT", pidx))])
                            pts.append((PT[pidx], bf(("PT", pidx))))
                        return pts

                    def x_pv(hh, pts):
                        bo = []
                        for dc in range(2):
                            bi = alloc_bank()
                            bo.append(bi)
                            for mt in range(2):
                                P.op("pe", _I("matmul", banks[bi][:], lhsT=xV[:, mt, hh * 256 + dc * 128: hh * 256 + (dc + 1) * 128],
                                              rhs=pts[mt][0][:], start=(mt == 0), stop=(mt == 1)),
                                     reads=[bf("xV"), pts[mt][1]], writes=[bankB[bi]])
                        bl = alloc_bank()
                        for mt in range(2):
                            P.op("pe", _I("matmul", banks[bl][:], lhsT=onesb[:], rhs=pts[mt][0][:], start=(mt == 0), stop=(mt == 1)),
                                 reads=[bf("onesb"), pts[mt][1]], writes=[bankB[bl]])
                        e = ep[hh % 2]
                        eB = bf(f"ep{hh % 2}")
                        P.op("act", _I("activation", out=e, in_=banks[bl][:], func=AF.Ln), reads=[bankB[bl]], writes=[eB])
                        P.op("act", _I("activation", out=e, in_=e, func=AF.Exp, scale=-1.0), reads=[eB], writes=[eB])
                        for dc in range(2):
                            P.op("dve", _I("tensor_tensor", out=featT[:, 2 * hh + dc, :], in0=banks[bo[dc]][:], in1=e, op=ALU.mult),
                                 reads=[bankB[bo[dc]], eB], writes=[ftB[2 * hh + dc]])

                    xp = x_scores(0)
                    for hh in range(4):
                        xn = x_scores(hh + 1) if hh + 1 < 4 else None
                        x_pv(hh, xp)
                        xp = xn
                    proj_add(l, "xattn_wo", hcur, hB, nxt_gain=norm_mlp_g[l:l + 1, :])

                    if last:
                        load_gain(final_norm_g[0:1, :], final=True)
                    if nxt is not None:
                        load_gain(norm_mix_g[nxt[0]:nxt[0] + 1, :])
                        gain_ready = True
                    else:
                        gain_ready = False
                    epc = 0
                    nub = (unit + 1) % 2
                    nh_ap = [hres2[nub][:, t, :] for t in range(4)]
                    nsl = {}
                    for fh in range(2):
                        for blk in range(4):
                            s1 = load_block(l, "mlp_w1", 0, fh * 4 + blk)
                            for fc in range(4):
                                bi = featmajor_proj(s1, fc * 128, None, None)
                                e = epc % 4
                                epc += 1
                                P.op("act", _I("activation", out=ep[e], in_=banks[bi][:], func=AF.Relu),
                                     reads=[bankB[bi]], writes=[bf(f"ep{e}")])
                                P.op("dve", _I("tensor_tensor", out=aT[:, blk * 4 + fc, :], in0=banks[bi][:], in1=ep[e], op=ALU.mult),
                                     reads=[bankB[bi], bf(f"ep{e}")], writes=[aTB[blk * 4 + fc]])
                        hide = (fh == 1 and nxt is not None)
                        if hide:
                            nsl[0] = norm_pre(nh_ap[0], hB2[nub][0])
                            nsl[1] = norm_pre(nh_ap[1], hB2[nub][1])
                        for dh in range(2):
                            s2 = [load_block(l, "mlp_w2", fh * 2 + kb, dh) for kb in range(2)]
                            for t in range(4):
                                bi = alloc_bank()
                                for kb in range(2):
                                    for c in range(8):
                                        P.op("pe", _I("matmul", banks[bi][:], lhsT=aT[:, kb * 8 + c, t * 128:(t + 1) * 128], rhs=wring[s2[kb]][:, c, :],
                                                      start=(kb == 0 and c == 0), stop=(kb == 1 and c == 7)),
                                             reads=[aTB[kb * 8 + c], wrB[s2[kb]]], writes=[bankB[bi]])
                                P.op("dve", _I("tensor_tensor", out=hcur[:, t, dh * 512:(dh + 1) * 512], in0=banks[bi][:],
                                               in1=hcur[:, t, dh * 512:(dh + 1) * 512], op=ALU.add),
                                     reads=[bankB[bi], hB[t]], writes=[hB[t]])
                                if hide and dh == 1 and t == 1:
                                    norm_tr(nsl[2], 2, hnT, bf("hnT"))
                                    norm_tr(nsl[3], 3, hnT, bf("hnT"))
                            if hide and dh == 0:
                                norm_tr(nsl[0], 0, hnT, bf("hnT"))
                                nsl[2] = norm_pre(nh_ap[2], hB2[nub][2])
                                norm_tr(nsl[1], 1, hnT, bf("hnT"))
                                nsl[3] = norm_pre(nh_ap[3], hB2[nub][3])
                        if hide:
                            norm_done = True

                    if last:
                        for t in range(4):
                            rs, sB = rms_rstd(hap[t], [hB[t]], D)
                            P.op("dve", _I("scalar_tensor_tensor", out=hcur[:, t, :], in0=hcur[:, t, :], scalar=rs, in1=gbf,
                                           op0=ALU.mult, op1=ALU.mult),
                                 reads=[hB[t], sB, bf("biasT")], writes=[hB[t]])
                    dst = y if last else hscr
                    final_ev = P.op("pool", _I("dma_start", out=dst[s, tok0:tok0 + G, :].rearrange("(t p) d -> p t d", p=128), in_=hcur[:]),
                                    reads=hB, writes=[bf(("hscr", s, g))], dma=s_hst[ub])
                    unit += 1
        P.wait_event("pool", final_ev)
        P.emit()
    return nc


_CACHE = {}


def _get_prog(S, NSEQ):
    key = (S, NSEQ)
    if key not in _CACHE:
        _CACHE[key] = build_program(S, NSEQ)
    return _CACHE[key]


def make_in_maps(inputs, ncores, nseq):
    consts = host_consts()
    maps = []
    for c in range(ncores):
        m = {}
        for k, v in inputs.items():
            v = np.asarray(v, dtype=np.float32)
            if k in ("x", "mem"):
                m[k] = np.ascontiguousarray(v[c * nseq:(c + 1) * nseq])
            elif k in ("mem_norm_g", "final_norm_g"):
                m[k] = np.ascontiguousarray(v.reshape(1, -1))
            else:
                m[k] = np.ascontiguousarray(v)
        m.update(consts)
        maps.append(m)
    return maps


def kernel(**inputs):
    x = np.asarray(inputs["x"])
    Bsz, S, _ = x.shape
    ncores = 8
    nseq = Bsz // ncores
    nc = _get_prog(S, nseq)
    maps = make_in_maps(inputs, ncores, nseq)
    res = run_bass_kernel_spmd(nc, maps, core_ids=list(range(ncores)))
    out = np.concatenate([np.asarray(r["y"]) for r in res.results], axis=0)
    return out.astype(np.float32)
```

```python
import math
import numpy as np
from contextlib import ExitStack
import concourse.bass as bass
import concourse.mybir as mybir
from concourse.bass_utils import run_bass_kernel_spmd

F32 = mybir.dt.float32
BF16 = mybir.dt.bfloat16
AF = mybir.ActivationFunctionType
ALU = mybir.AluOpType

D = 1024
NMEM = 256
EPS = 1e-6
G = 512
FW = 1152


class Buf:
    __slots__ = ("w", "r", "al", "stamp")

    def __init__(self):
        self.w = None
        self.r = {}
        self.al = ()
        self.stamp = 0


class Op:
    __slots__ = ("fn", "waits", "signal", "dma")

    def __init__(self, fn, waits):
        self.fn = fn
        self.waits = waits
        self.signal = False
        self.dma = None


class Prog:
    ENGS = ("pe", "act", "dve", "pool", "sp")

    def __init__(self, nc, es):
        self.nc = nc
        self.es = es
        self.ops = {e: [] for e in self.ENGS}
        self.ncomp = {e: 0 for e in self.ENGS}
        self.comp_idx = {e: [] for e in self.ENGS}
        self.seen = {e: {} for e in self.ENGS}
        self.sems = {e: es.enter_context(nc.semaphore("s_" + e)) for e in self.ENGS}
        self.dma_cnt = {}
        self.gctr = 0

    def dma_sem(self, name):
        s = self.es.enter_context(self.nc.semaphore(name))
        self.dma_cnt[id(s)] = [s, 0]
        return s

    def _need(self, eng, ev, waits, same_ok):
        if ev is None:
            return
        if ev[0] == 'c' and ev[1] == eng and not same_ok and eng == "pe":
            return
        key = (ev[0], ev[1])
        if self.seen[eng].get(key, -1) >= ev[2]:
            return
        self.seen[eng][key] = ev[2]
        waits.append(ev)

    def op(self, eng, fn, reads=(), writes=(), dma=None):
        waits = []
        for b in reads:
            self._need(eng, b.w, waits, True)
        for b in writes:
            self._need(eng, b.w, waits, False)
            for ev in b.r.values():
                self._need(eng, ev, waits, False)
            for a in b.al:
                self._need(eng, a.w, waits, True)
                for ev in a.r.values():
                    self._need(eng, ev, waits, True)
        o = Op(fn, waits)
        if dma is not None:
            ent = self.dma_cnt[id(dma)]
            ent[1] += 16
            o.dma = dma
            ev = ('d', id(dma), ent[1])
        else:
            seq = self.ncomp[eng]
            self.ncomp[eng] += 1
            self.comp_idx[eng].append(len(self.ops[eng]))
            ev = ('c', eng, seq)
        self.ops[eng].append(o)
        self.gctr += 1
        for b in reads:
            b.r[(ev[0], ev[1])] = ev
            b.stamp = self.gctr
        for b in writes:
            b.w = ev
            b.r = {}
            b.stamp = self.gctr
        return ev

    def wait_event(self, eng, ev):
        waits = []
        self._need(eng, ev, waits, True)
        if waits:
            self.ops[eng].append(Op(None, waits))

    def emit(self):
        nc = self.nc
        for e in self.ENGS:
            for o in self.ops[e]:
                for ev in o.waits:
                    if ev[0] == 'c':
                        self.ops[ev[1]][self.comp_idx[ev[1]][ev[2]]].signal = True
        cnt = {}
        for e in self.ENGS:
            c = 0
            arr = []
            for idx in self.comp_idx[e]:
                if self.ops[e][idx].signal:
                    c += 1
                arr.append(c)
            cnt[e] = arr
        sem_by_id = {k: v[0] for k, v in self.dma_cnt.items()}

        def run(e, h):
            for o in self.ops[e]:
                for ev in o.waits:
                    if ev[0] == 'c':
                        h.wait_ge(self.sems[ev[1]], cnt[ev[1]][ev[2]])
                    else:
                        h.wait_ge(sem_by_id[ev[1]], ev[2])
                if o.fn is None:
                    continue
                ins = o.fn(h)
                if o.dma is not None:
                    ins.then_inc(o.dma, 16)
                elif o.signal:
                    ins.then_inc(self.sems[e], 1)

        with nc.Block() as block:
            @block.tensor
            def _(h):
                run("pe", h)

            @block.scalar
            def _(h):
                run("act", h)

            @block.vector
            def _(h):
                run("dve", h)

            @block.gpsimd
            def _(h):
                run("pool", h)

            @block.sync
            def _(h):
                run("sp", h)


def t5_bucket_np(n):
    n = np.asarray(n, dtype=np.int64)
    nf = np.maximum(n, 1).astype(np.float32)
    large = 16 + (np.log(nf / np.float32(16)) / np.float32(math.log(128 / 16)) * np.float32(16)).astype(np.int32)
    large = np.minimum(large, 31)
    return np.where(n < 16, n, large)


def host_consts():
    ident = np.eye(128, dtype=np.float32)
    onehot = np.zeros((33, FW), np.float32)
    for i in range(FW):
        d = i - 511
        if d < 0:
            onehot[32, i] = -240000.0
        else:
            onehot[int(t5_bucket_np(d)), i] = 8.0
    invc = np.zeros((128, 4, 16), np.float32)
    for gi, w in enumerate((2, 4, 8, 16)):
        for t in range(16):
            invc[:, gi, t] = 1.0 / min(t + 1, w)
    return {"c_ident": ident, "c_onehot": onehot, "c_invc": invc}


WSPEC = [
    ("ab_w_in", "even", 1024, 2048), ("ab_w_out", "even", 1024, 1024),
    ("conv_w_in", "odd", 1024, 3072), ("conv_w_out", "odd", 1024, 1024),
    ("xattn_wkv", "all", 1024, 2048), ("xattn_wq", "all", 1024, 1024), ("xattn_wo", "all", 1024, 1024),
    ("mlp_w1", "all", 1024, 4096), ("mlp_w2", "all", 4096, 1024),
]


def _I(name, *a, **k):
    return lambda h: getattr(h, name)(*a, **k)


def build_program(S, NSEQ, DEPTH=4):
    NG = S // G
    NT = S // 128
    nc = bass.Bass("TRN2", target_bir_lowering=False)
    n_even = (DEPTH + 1) // 2
    n_odd = DEPTH // 2

    def din(name, shape):
        return nc.dram_tensor(name, list(shape), F32, kind="ExternalInput")

    x = din("x", [NSEQ, S, D]).ap()
    mem = din("mem", [NSEQ, NMEM, D]).ap()
    rel_bias = din("rel_bias", [32, 4]).ap()
    mem_norm_g = din("mem_norm_g", [1, D]).ap()
    norm_mix_g = din("norm_mix_g", [DEPTH, D]).ap()
    norm_xattn_g = din("norm_xattn_g", [DEPTH, D]).ap()
    norm_mlp_g = din("norm_mlp_g", [DEPTH, D]).ap()
    final_norm_g = din("final_norm_g", [1, D]).ap()
    W = {}
    for name, kind, K, N in WSPEC:
        n = {"even": n_even, "odd": n_odd, "all": DEPTH}[kind]
        W[name] = din(name, [max(n, 1), K, N]).ap()
    lam_in = [din(nm, [max(n_even, 1), 64]).ap() for nm in ("lambda_q1", "lambda_k1", "lambda_q2", "lambda_k2")]
    subln_t = din("subln_g", [max(n_even, 1), 128])
    pool_w = din("pool_w", [max(n_even, 1), 4, 128, 128]).ap()
    pscale_t = din("pool_scale", [max(n_even, 1), 512])
    convw_t = din("conv_w", [max(n_odd, 1), 3, D])
    c_ident = din("c_ident", [128, 128]).ap()
    c_onehot = din("c_onehot", [33, FW]).ap()
    c_invc = din("c_invc", [128, 4, 16]).ap()
    y = nc.dram_tensor("y", [NSEQ, S, D], F32, kind="ExternalOutput").ap()

    def col_ap(t, off):
        return bass.AP(t, off, [[1, 128], [1, 1]])

    blk_ids = {}
    nblk = 0
    for l in range(DEPTH):
        for name, kind, K, N in WSPEC:
            if (kind == "even" and l % 2 == 1) or (kind == "odd" and l % 2 == 0):
                continue
            for kb in range(K // 1024):
                for nb in range(N // 512):
                    blk_ids[(l, name, kb, nb)] = nblk
                    nblk += 1
    wblk = nc.dram_tensor("wblk", [nblk, 128, 4096], BF16).ap()
    poolw_bf = nc.dram_tensor("poolw_bf", [max(n_even, 1), 128, 512], BF16).ap()
    hscr = nc.dram_tensor("hscr", [NSEQ, S, D], F32).ap()
    Fd_t = nc.dram_tensor("Fd", [4, FW], BF16)
    Fd = Fd_t.ap()
    FR = 130
    Fd2_t = nc.dram_tensor("Fd2", [4, FR, FW], BF16)
    Fd2 = Fd2_t.ap()

    with ExitStack() as es:
        P = Prog(nc, es)

        def sb(name, shape, dt):
            return es.enter_context(nc.sbuf_tensor(name, list(shape), dt))

        hres2 = [sb(f"hres{i}", [128, 4, D], F32) for i in range(2)]
        hnT = sb("hnT", [128, 8, G], BF16)
        hn_tmps = [sb(f"hn_tmp{i}", [128, D], BF16) for i in range(2)]
        wring = [sb(f"wr{i}", [128, 8, 512], BF16) for i in range(4)]
        memT = sb("memT", [128, 8, NMEM], BF16)
        xKT = sb("xKT", [128, 8, NMEM], BF16)
        xV = sb("xV", [128, 2, D], BF16)
        gb = sb("gb", [128, D], F32)
        KT = sb("KT", [128, 4, S], BF16)
        Vc = sb("Vc", [128, NT, 512], BF16)
        featT = sb("featT", [128, 8, G], BF16)
        qT = sb("qT", [128, 8, G], BF16)
        aT = sb("aT", [128, 16, G], BF16)
        PT = [sb(f"PT{i}", [128, G], BF16) for i in range(4)]
        epall = sb("epall", [128, 4 * G], F32)
        ep = [epall[:, i * G:(i + 1) * G] for i in range(4)]
        aT32 = aT[:].rearrange("p c n -> p (c n)").bitcast(F32)
        uT = aT32[:, 0:2112].rearrange("p (g n) -> p g n", g=4)
        pw = [aT32[:, 2112 + i * 528:2112 + (i + 1) * 528] for i in range(2)]
        pTb = [aT32[:, 3168 + i * 256:3168 + (i + 1) * 256].bitcast(BF16) for i in range(2)]
        biasT = sb("biasT", [128, 5, G], BF16)
        zbuf = [aT32[:, 0:514]] * 2
        zh = sb("zh", [128, 8, 2], F32)
        uh = sb("uh", [128, 4, 16], F32)
        idf = pw[1][:, 0:128]
        idb = sb("idb", [128, 128], BF16)
        onesb = sb("onesb", [128, 128], BF16)
        stt = sb("stt", [128, 4, 4], F32)
        b31 = sb("b31", [128, 4], F32)
        lamt = pw[0][:, 0:512].rearrange("p (i k d) -> p i k d", i=2, k=4)
        lams = sb("lams", [128, 8], F32)
        neglam = sb("neglam", [128, 2], F32)
        gs = sb("gs", [128, 2], F32)
        pscale = sb("pscale", [128, 2, 4], F32)
        poolw_sb = sb("poolw_sb", [128, 4, 128], BF16)
        convw = sb("convw", [128, 2, 3, 8], F32)
        invc = sb("invc", [128, 4, 16], F32)
        rb33 = sb("rb33", [33, 4], F32)
        oh = epall[0:33, 0:FW]
        Fsb = featT[:].rearrange("p c n -> p (c n)")[0:4, 0:FW]

        banks = [es.enter_context(nc.psum_tensor(f"bk{i}", [128, 512], F32)) for i in range(8)]
        bankB = [Buf() for _ in range(8)]
        pinned = set()
        rot = [0]

        def alloc_bank():
            best = None
            for k in range(8):
                i = (rot[0] + k) % 8
                if i in pinned:
                    continue
                if best is None or bankB[i].stamp < bankB[best].stamp:
                    best = i
            rot[0] = best + 1
            bankB[best].stamp = P.gctr + 1
            return best

        Bd = {}

        def bf(name):
            if name not in Bd:
                Bd[name] = Buf()
            return Bd[name]

        _al = [bf("uT"), bf(("pw", 0)), bf(("pw", 1)), bf(("pTb", 0)), bf(("pTb", 1)), bf(("zb", 0))]
        aTB = [bf(("aT", k)) for k in range(16)]
        ftB = [bf(("featT", k)) for k in range(8)]
        qTB = [bf(("qT", k)) for k in range(8)]
        for _a in aTB:
            _a.al = tuple(_al)
        for _b in _al:
            _b.al = tuple(aTB)
        bf("uT").al = tuple(aTB) + (bf(("zb", 0)),)
        bf(("zb", 0)).al = tuple(aTB) + (bf("uT"),)
        gbf = biasT[:].rearrange("p r n -> p (r n)").bitcast(F32)[:, 0:D]
        hB2 = [[Buf() for _ in range(4)] for _ in range(2)]
        junk = qT[:].rearrange("p c n -> p (c n)")[:, 0:D]

        s_misc = P.dma_sem("d_misc")
        n_extra = min(4, NT // 8)
        s_wr = [P.dma_sem(f"d_wr{i}") for i in range(4 + n_extra)]
        s_h = [P.dma_sem("d_h0"), P.dma_sem("d_h1")]
        s_hst = [P.dma_sem("d_hst0"), P.dma_sem("d_hst1")]
        s_gb = P.dma_sem("d_gb")
        s_bias = P.dma_sem("d_bias")
        s_f = P.dma_sem("d_f")
        s_pw = P.dma_sem("d_pw")
        s_f2 = P.dma_sem("d_f2")

        evac_ctr = [0]
        ptc = [0]

        def evac(bi, src_ap, dst_ap, dst_bufs, eng=None):
            if eng is None:
                eng = "act" if evac_ctr[0] % 2 == 0 else "dve"
                evac_ctr[0] += 1
            if eng == "act":
                P.op("act", _I("copy", out=dst_ap, in_=src_ap), reads=[bankB[bi]], writes=dst_bufs)
            else:
                P.op("dve", _I("tensor_copy", out=dst_ap, in_=src_ap), reads=[bankB[bi]], writes=dst_bufs)

        setup_bufs = []

        def sdma(out_ap, in_ap, bname):
            P.op("sp", _I("dma_start", out=out_ap, in_=in_ap), writes=[bf(bname)], dma=s_misc)
            if bf(bname) not in setup_bufs:
                setup_bufs.append(bf(bname))

        sdma(idf, c_ident, ("pw", 1))
        sdma(invc[:], c_invc, "invc")
        sdma(b31[:], rel_bias[31:32, :].partition_broadcast(128), "b31")
        sdma(oh, c_onehot, "ep0"); sdma(oh, c_onehot, "ep1") if False else None; setup_bufs.extend([bf("ep1"), bf("ep2")])
        sdma(rb33[0:32, :], rel_bias, "rb33")
        for i in range(n_even):
            for k in range(4):
                sdma(lamt[:, i, k, :], lam_in[k][i:i + 1, :].partition_broadcast(128), ("pw", 0))
            sdma(gs[:, i:i + 1], col_ap(subln_t, i * 128), "gs")
            for gi in range(4):
                sdma(pscale[:, i, gi:gi + 1], col_ap(pscale_t, i * 512 + gi * 128), "pscale")
        for i in range(n_odd):
            for k in range(3):
                for c in range(8):
                    sdma(convw[:, i, k, c:c + 1], col_ap(convw_t, (i * 3 + k) * D + c * 128), "convw")
        fence = ('d', id(s_misc), P.dma_cnt[id(s_misc)][1])
        for b in setup_bufs:
            b.w = fence

        P.op("dve", _I("tensor_copy", out=idb[:], in_=idf), reads=[bf(("pw", 1))], writes=[bf("idb")])
        P.op("dve", _I("memset", onesb[:], 1.0), writes=[bf("onesb")])
        P.op("dve", _I("memset", rb33[32:33, :], 1.0), reads=[bf("rb33")], writes=[bf("rb33x")])
        for j0 in range(0, FW, 384):
            bi = alloc_bank()
            P.op("pe", _I("matmul", banks[bi][0:4, 0:384], lhsT=rb33[:, :], rhs=oh[:, j0:j0 + 384], start=True, stop=True),
                 reads=[bf("rb33"), bf("rb33x"), bf("ep0"), bf("ep1"), bf("ep2")], writes=[bankB[bi]])
            evac(bi, banks[bi][0:4, 0:384], Fsb[:, j0:j0 + 384], ftB, eng="dve")
        P.op("sp", _I("dma_start", out=Fd, in_=Fsb), reads=ftB, writes=[bf("Fd0")], dma=s_f)
        for hh in range(4):
            P.op("sp", _I("dma_start", out=Fd2[hh], in_=Fd[hh:hh + 1, :].partition_broadcast(FR)), reads=[bf("Fd0")], writes=[bf("Fd")],
                 dma=s_f2)
        bf("Fd").w = ('d', id(s_f2), P.dma_cnt[id(s_f2)][1])
        for i in range(n_even):
            lam_init = 0.8 - 0.6 * math.exp(-0.3 * (2 * i))
            for m in range(2):
                P.op("dve", _I("tensor_tensor", out=lamt[:, i, 2 * m, :], in0=lamt[:, i, 2 * m, :], in1=lamt[:, i, 2 * m + 1, :], op=ALU.mult),
                     reads=[bf(("pw", 0))], writes=[bf(("pw", 0))])
                P.op("act", _I("activation", out=lamt[:, i, 2 * m + 1, :], in_=lamt[:, i, 2 * m, :], func=AF.Identity,
                               accum_out=lams[:, m:m + 1]),
                     reads=[bf(("pw", 0))], writes=[bf(("pw", 0)), bf("lams")])
                P.op("act", _I("activation", out=lams[:, 2 + m:3 + m], in_=lams[:, m:m + 1], func=AF.Exp),
                     reads=[bf("lams")], writes=[bf("lams")])
            P.op("dve", _I("tensor_tensor", out=lams[:, 4:5], in0=lams[:, 3:4], in1=lams[:, 2:3], op=ALU.subtract),
                 reads=[bf("lams")], writes=[bf("lams")])
            P.op("dve", _I("tensor_scalar", out=neglam[:, i:i + 1], in0=lams[:, 4:5], scalar1=-lam_init, scalar2=None, op0=ALU.add),
                 reads=[bf("lams")], writes=[bf("neglam")])
            P.op("dve", _I("tensor_scalar", out=gs[:, i:i + 1], in0=gs[:, i:i + 1], scalar1=1.0 - lam_init, scalar2=None, op0=ALU.mult),
                 reads=[bf("gs")], writes=[bf("gs")])

        wconv_buf = {}
        NCV = 64
        s_cv = [P.dma_sem(f"d_cv{k}") for k in range(NCV)]
        cv_evs = []

        cv_pending = []

        def cv_dma(fn, b, lazy=True):
            if lazy:
                cv_pending.append((fn, b))
                return
            k = len(cv_evs)
            if k >= NCV:
                P.wait_event("pool", cv_evs[k - NCV])
            cv_evs.append(P.op("pool", fn, writes=[b], dma=s_cv[k % NCV]))

        CV_ORDER = ["xattn_wkv", "ab_w_in", "conv_w_in", "ab_w_out", "conv_w_out", "xattn_wq", "xattn_wo", "mlp_w1", "mlp_w2"]
        wspec_sorted = sorted(WSPEC, key=lambda w: CV_ORDER.index(w[0]))
        for l in range(DEPTH):
            if l % 2 == 0:
                wconv_buf[(l, "pool_w")] = Buf()
            for name, kind, K, N in WSPEC:
                if (kind == "even" and l % 2 == 1) or (kind == "odd" and l % 2 == 0):
                    continue
                for kb in range(K // 1024):
                    for nb in range(N // 512):
                        wconv_buf[blk_ids[(l, name, kb, nb)]] = Buf()

        def convert_layer(l):
            i = l // 2
            if l % 2 == 0:
                pb = wconv_buf[(l, "pool_w")]
                cv_dma(_I("dma_start", out=poolw_bf[i].rearrange("c (g d) -> c g d", g=4), in_=pool_w[i].rearrange("g c d -> c g d")), pb)
            for name, kind, K, N in wspec_sorted:
                if (kind == "even" and l % 2 == 1) or (kind == "odd" and l % 2 == 0):
                    continue
                li = l if kind == "all" else i
                for kb in range(K // 1024):
                    for nb in range(N // 512):
                        bid = blk_ids[(l, name, kb, nb)]
                        b = wconv_buf[bid]
                        src = W[name][li, kb * 1024:(kb + 1) * 1024, nb * 512:(nb + 1) * 512].rearrange("(c p) n -> p c n", p=128)
                        dst = wblk[bid].rearrange("p (c n) -> p c n", c=8)
                        cv_dma(_I("dma_start", out=dst, in_=src), b)

        def cv_flush(n):
            for _ in range(n):
                if cv_pending:
                    cv_dma(*cv_pending.pop(0), lazy=False)

        convert_layer(0)
        cv_flush(1000)

        wr_ctr = [0]
        wrB = [Buf() for _ in range(4 + n_extra)]
        for k in range(n_extra):
            wring.append(Vc[:, 8 * k:8 * k + 8, :])
            vb = tuple(bf(("V", gg)) for gg in (2 * k, 2 * k + 1) if gg < NG)
            wrB[4 + k].al = vb
            for b_ in vb:
                b_.al = (wrB[4 + k],)

        def load_block(l, name, kb, nb):
            bid = blk_ids[(l, name, kb, nb)]
            slot = wr_ctr[0] % (4 + n_extra if l % 2 == 1 else 4)
            wr_ctr[0] += 1
            P.op("sp", _I("dma_start", out=wring[slot][:], in_=wblk[bid].rearrange("p (c n) -> p c n", c=8)),
                 reads=[wconv_buf[bid]], writes=[wrB[slot]], dma=s_wr[slot])
            return slot

        def load_gain(row_ap, final=False):
            if final:
                P.op("pool", _I("dma_start", out=gbf, in_=row_ap.partition_broadcast(128)), writes=[bf("biasT")], dma=s_bias)
            else:
                P.op("pool", _I("dma_start", out=gb[:], in_=row_ap.partition_broadcast(128)), writes=[bf("gb")], dma=s_gb)

        st_ctr = [0]

        def rms_rstd(src_ap, src_bufs, n):
            k = st_ctr[0] % 4
            st_ctr[0] += 1
            sk = stt[:, k, :]
            sB = bf(("st", k))
            P.op("act", _I("activation", out=junk[:, 0:n], in_=src_ap, func=AF.Square, accum_out=sk[:, 0:1]),
                 reads=src_bufs, writes=[qTB[0], qTB[1], sB])
            P.op("act", _I("activation", out=sk[:, 1:2], in_=sk[:, 0:1], func=AF.Ln, scale=1.0 / n, bias=EPS),
                 reads=[sB], writes=[sB])
            P.op("act", _I("activation", out=sk[:, 2:3], in_=sk[:, 1:2], func=AF.Exp, scale=-0.5),
                 reads=[sB], writes=[sB])
            return sk[:, 2:3], sB

        nrm_ctr = [0]

        def norm_pre(src_ap, src_buf):
            slot = nrm_ctr[0] % 2
            nrm_ctr[0] += 1
            rs, sB = rms_rstd(src_ap, [src_buf], D)
            for hf in range(2):
                P.op("dve", _I("scalar_tensor_tensor", out=hn_tmps[slot][:, hf * 512:(hf + 1) * 512], in0=src_ap[:, hf * 512:(hf + 1) * 512],
                               scalar=rs, in1=gb[:, hf * 512:(hf + 1) * 512], op0=ALU.mult, op1=ALU.mult),
                     reads=[src_buf, sB, bf("gb")], writes=[bf(("hn_tmp", slot, hf))])
            return slot

        def norm_tr(slot, t, dstT, dst_buf):
            bi = alloc_bank()
            pv = banks[bi][:].bitcast(BF16).rearrange("p (c n) -> p c n", c=8)
            for c in range(8):
                P.op("pe", _I("transpose", out=pv[:, c, :], in_=hn_tmps[slot][:, c * 128:(c + 1) * 128], identity=idb[:]),
                     reads=[bf(("hn_tmp", slot, c // 4)), bf("idb")], writes=[bankB[bi]])
            evac(bi, pv, dstT[:, :, t * 128:(t + 1) * 128], [dst_buf])

        def norm_T(src_aps, src_bufs, dstT, dst_buf, ntile):
            slots = {}
            for t in range(ntile):
                slots[t] = norm_pre(src_aps[t], src_bufs[t])
                if t >= 1:
                    norm_tr(slots[t - 1], t - 1, dstT, dst_buf)
            norm_tr(slots[ntile - 1], ntile - 1, dstT, dst_buf)

        def proj_add(l, name, hcur, hB, nxt_gain=None, corder=(0, 1, 2, 3, 4, 5, 6, 7)):
            slots = [load_block(l, name, 0, nb) for nb in range(2)]
            if nxt_gain is not None:
                load_gain(nxt_gain)
            nslot = {}
            for t in range(4):
                for nb in range(2):
                    bi = alloc_bank()
                    for ci, c in enumerate(corder):
                        P.op("pe", _I("matmul", banks[bi][:], lhsT=featT[:, c, t * 128:(t + 1) * 128], rhs=wring[slots[nb]][:, c, :],
                                      start=(ci == 0), stop=(ci == 7)),
                             reads=[ftB[c], wrB[slots[nb]]], writes=[bankB[bi]])
                    P.op("dve", _I("tensor_tensor", out=hcur[:, t, nb * 512:(nb + 1) * 512], in0=banks[bi][:],
                                   in1=hcur[:, t, nb * 512:(nb + 1) * 512], op=ALU.add),
                         reads=[bankB[bi], hB[t]], writes=[hB[t]])
                if nxt_gain is not None:
                    nslot[t] = norm_pre(hcur[:, t, :], hB[t])
                    if t >= 1:
                        norm_tr(nslot[t - 1], t - 1, hnT, bf("hnT"))
            if nxt_gain is not None:
                norm_tr(nslot[3], 3, hnT, bf("hnT"))

        def featmajor_proj(slot, col0, dst_ap, dst_bufs, rhs_ap=None, rhs_buf=None, n=G, eng=None):
            bi = alloc_bank()
            rhs_ap = hnT if rhs_ap is None else rhs_ap
            rhs_buf = bf("hnT") if rhs_buf is None else rhs_buf
            for c in range(8):
                P.op("pe", _I("matmul", banks[bi][:, 0:n], lhsT=wring[slot][:, c, col0:col0 + 128], rhs=rhs_ap[:, c, 0:n],
                              start=(c == 0), stop=(c == 7)),
                     reads=[wrB[slot], rhs_buf], writes=[bankB[bi]])
            if dst_ap is not None:
                evac(bi, banks[bi][:, 0:n], dst_ap, dst_bufs, eng=eng)
            return bi

        qT4 = qT[:].rearrange("p (h m) n -> p h m n", m=2)
        def load_h(ub, s, l, g):
            srcT = x if l == 0 else hscr
            P.op("pool", _I("dma_start", out=hres2[ub][:], in_=srcT[s, g * G:(g + 1) * G, :].rearrange("(t p) d -> p t d", p=128)),
                 reads=[bf(("hscr", s, g))], writes=hB2[ub], dma=s_h[ub])

        final_ev = None
        unit = 0
        gain_ready = False
        for s in range(NSEQ):
            ub0 = unit % 2
            load_gain(mem_norm_g[0:1, :])
            P.op("pool", _I("dma_start", out=hres2[ub0][:, 0:2, :], in_=mem[s].rearrange("(t p) d -> p t d", p=128)),
                 writes=hB2[ub0][0:2], dma=s_h[ub0])
            norm_T([hres2[ub0][:, t, :] for t in range(2)], hB2[ub0][0:2], memT, bf("memT"), 2)
            load_h(ub0, s, 0, 0)
            gain_ready = False
            norm_done = False

            for l in range(DEPTH):
                i = l // 2
                even = (l % 2 == 0)
                last = (l == DEPTH - 1)
                slots = [load_block(l, "xattn_wkv", 0, nb) for nb in range(4)]
                for kc in range(8):
                    featmajor_proj(slots[kc // 4], (kc % 4) * 128, xKT[:, kc, :], [bf("xKT")], rhs_ap=memT, rhs_buf=bf("memT"), n=NMEM)
                for mt in range(2):
                    for half in range(2):
                        bi = alloc_bank()
                        for c in range(8):
                            P.op("pe", _I("matmul", banks[bi][:], lhsT=memT[:, c, mt * 128:(mt + 1) * 128], rhs=wring[slots[2 + half]][:, c, :],
                                          start=(c == 0), stop=(c == 7)),
                                 reads=[bf("memT"), wrB[slots[2 + half]]], writes=[bankB[bi]])
                        evac(bi, banks[bi][:], xV[:, mt, half * 512:(half + 1) * 512], [bf("xV")])
                if even:
                    P.op("sp", _I("dma_start", out=poolw_sb[:], in_=poolw_bf[i].rearrange("c (g d) -> c g d", g=4)),
                         reads=[wconv_buf[(l, "pool_w")]], writes=[bf("poolw_sb")], dma=s_pw)
                    P.op("pool", _I("memset", uh[:], 0.0), writes=[bf("uh")])
                else:
                    P.op("pool", _I("memset", zh[:], 0.0), writes=[bf("zh")])
                if s == 0 and l + 1 < DEPTH:
                    convert_layer(l + 1)
                    cv_per_unit = -(-len(cv_pending) // NG)

                for g in range(NG):
                    tok0 = g * G
                    ub = unit % 2
                    hcur = hres2[ub]
                    hB = hB2[ub]
                    hap = [hcur[:, t, :] for t in range(4)]
                    if g + 1 < NG:
                        nxt = (l, g + 1)
                    elif l + 1 < DEPTH:
                        nxt = (l + 1, 0)
                    else:
                        nxt = None

                    if not norm_done:
                        if not gain_ready:
                            load_gain(norm_mix_g[l:l + 1, :])
                        norm_T(hap, hB, hnT, bf("hnT"), 4)
                    norm_done = False
                    if even:
                        sq = load_block(l, "ab_w_in", 0, 0)
                        sk = load_block(l, "ab_w_in", 0, 1)
                        sv = load_block(l, "ab_w_in", 0, 2)
                        su = load_block(l, "ab_w_in", 0, 3)
                        P.op("pool", _I("memset", qT4[64:128, :, 0, :], 0.0), writes=qTB)
                        P.op("pool", _I("memset", qT4[0:64, :, 1, :], 0.0), writes=qTB)
                        for hh in range(4):
                            bi = featmajor_proj(sq, hh * 128, None, None)
                            evac(bi, banks[bi][0:64, :], qT4[0:64, hh, 0, :], [qTB[2 * hh]])
                            evac(bi, banks[bi][64:128, :], qT4[64:128, hh, 1, :], [qTB[2 * hh + 1]])
                        for hh in range(4):
                            featmajor_proj(sk, hh * 128, KT[:, hh, tok0:tok0 + G], [bf(("KT", g))])
                        for t in range(4):
                            bi = alloc_bank()
                            for c in range(8):
                                P.op("pe", _I("matmul", banks[bi][:], lhsT=hnT[:, c, t * 128:(t + 1) * 128], rhs=wring[sv][:, c, :],
                                              start=(c == 0), stop=(c == 7)),
                                     reads=[bf("hnT"), wrB[sv]], writes=[bankB[bi]])
                            evac(bi, banks[bi][:], Vc[:, g * 4 + t, :], [bf(("V", g))])
                        for gi in range(4):
                            featmajor_proj(su, gi * 128, uT[:, gi, 16:528], [bf("uT")])
                        P.op("pool", _I("tensor_copy", out=uT[:, :, 0:16], in_=uh[:]), reads=[bf("uh")], writes=[bf("uT")])
                        pts4 = [(pTb[0], bf(("pTb", 0))), (pTb[1], bf(("pTb", 1))),
                                (hn_tmps[0][:, 0:G], bf(("hn_tmp", 0, 0))), (hn_tmps[0][:, G:2 * G], bf(("hn_tmp", 0, 1)))]
                        for gi, wdw in enumerate((2, 4, 8, 16)):
                            nst = int(math.log2(wdw))
                            cur = uT[:, gi, :]
                            curB = bf("uT")
                            for k in range(nst):
                                sh = 1 << k
                                lo = 2 * sh - 1
                                dstb = pw[k % 2]
                                P.op("pool", _I("tensor_tensor", out=dstb[:, lo:528], in0=cur[:, lo:528], in1=cur[:, lo - sh:528 - sh], op=ALU.add),
                                     reads=[curB], writes=[bf(("pw", k % 2))])
                                cur = dstb
                                curB = bf(("pw", k % 2))
                            pt, ptB = pts4[gi]
                            P.op("dve", _I("scalar_tensor_tensor", out=pt, in0=cur[:, 16:528], scalar=1.0 / wdw, in1=uT[:, gi, 16:528],
                                           op0=ALU.mult, op1=ALU.subtract),
                                 reads=[curB, bf("uT")], writes=[ptB])
                            if g == 0:
                                P.op("dve", _I("tensor_tensor", out=ep[0][:, 0:16], in0=cur[:, 16:32], in1=invc[:, gi, :], op=ALU.mult),
                                     reads=[curB, bf("invc")], writes=[bf("ep0")])
                                P.op("dve", _I("tensor_tensor", out=pt[:, 0:16], in0=ep[0][:, 0:16], in1=uT[:, gi, 16:32], op=ALU.subtract),
                                     reads=[bf("ep0"), bf("uT")], writes=[ptB])
                        P.op("pool", _I("tensor_copy", out=uh[:], in_=uT[:, :, 512:528]), reads=[bf("uT")], writes=[bf("uh")])

                        def pool_part_b(i=i, pts4=pts4):
                            for gi in range(4):
                                pt, ptB = pts4[gi]
                                bi = alloc_bank()
                                P.op("pe", _I("matmul", banks[bi][:], lhsT=poolw_sb[:, gi, :], rhs=pt, start=True, stop=True),
                                     reads=[bf("poolw_sb"), ptB], writes=[bankB[bi]])
                                P.op("act", _I("activation", out=featT[:, 4 + gi, :], in_=banks[bi][:], func=AF.Copy, scale=pscale[:, i, gi:gi + 1]),
                                     reads=[bankB[bi], bf("pscale")], writes=[ftB[4 + gi]])

                        nk = 4 * g + 4
                        sqb = hn_tmps[1][:, 0:G]
                        sqB = bf(("hn_tmp", 1, 0))
                        pend_ep1 = []
                        pend_p2 = []
                        for hh in range(4):
                            src_ap = bass.AP(Fd2_t, hh * FR * FW + 127, [[FW - 1, 128], [128, 5], [1, G]])
                            P.op("pool", _I("dma_start", out=biasT[:], in_=src_ap), reads=[bf("Fd")], writes=[bf("biasT")], dma=s_bias)
                            for m in range(2):
                                acc = []

                                def do_pv(j, ptile, ptB_, c0, hh=hh, acc=acc, nk=nk):
                                    if not acc:
                                        for _ in range(2):
                                            bi_ = alloc_bank()
                                            pinned.add(bi_)
                                            acc.append(bi_)
                                    P.op("pe", _I("matmul", banks[acc[0]][:, c0:G], lhsT=Vc[:, j, hh * 128:(hh + 1) * 128], rhs=ptile[:, c0:G],
                                                  start=(j == 0), stop=(j == nk - 1)),
                                         reads=[bf(("V", j // 4)), ptB_], writes=[bankB[acc[0]]])
                                    P.op("pe", _I("matmul", banks[acc[1]][:, c0:G], lhsT=onesb[:], rhs=ptile[:, c0:G],
                                                  start=(j == 0), stop=(j == nk - 1)),
                                         reads=[bf("onesb"), ptB_], writes=[bankB[acc[1]]])

                                pend = []
                                for j in range(nk):
                                    near = j >= 4 * g - 1
                                    ri = j - 4 * g + 1
                                    c0 = 128 * (j - 4 * g) if j > 4 * g else 0
                                    bi = alloc_bank()
                                    if near:
                                        P.op("pe", _I("matmul", banks[bi][:, c0:G], lhsT=idb[:], rhs=biasT[:, 4 - ri, c0:G], start=True, stop=False),
                                             reads=[bf("idb"), bf("biasT")], writes=[bankB[bi]])
                                    P.op("pe", _I("matmul", banks[bi][:, c0:G], lhsT=KT[:, hh, j * 128:(j + 1) * 128], rhs=qT4[:, hh, m, c0:G],
                                                  start=(not near), stop=True),
                                         reads=[bf(("KT", j // 4)), qTB[2 * hh + m]], writes=[bankB[bi]])
                                    pidx = ptc[0] % 4
                                    ptc[0] += 1
                                    ptile = PT[pidx]
                                    ptB_ = bf(("PT", pidx))
                                    if near:
                                        P.op("act", _I("activation", out=ptile[:, c0:G], in_=banks[bi][:, c0:G], func=AF.Exp, scale=0.125),
                                             reads=[bankB[bi]], writes=[ptB_])
                                    else:
                                        P.op("act", _I("activation", out=ptile[:, c0:G], in_=banks[bi][:, c0:G], func=AF.Exp, scale=0.125,
                                                       bias=b31[:, hh:hh + 1]),
                                             reads=[bankB[bi], bf("b31")], writes=[ptB_])
                                    pend.append((j, ptile, ptB_, c0))
                                    if j == 0 and pend_ep1:
                                        pend_ep1.pop(0)()
                                    for jj, fn in list(pend_p2):
                                        if j >= min(jj, nk - 1):
                                            pend_p2.remove((jj, fn))
                                            fn()
                                    if len(pend) > 2:
                                        do_pv(*pend.pop(0))
                                while pend:
                                    do_pv(*pend.pop(0))

                                def ep1(m=m, acc=acc):
                                    P.op("act", _I("activation", out=ep[m], in_=banks[acc[1]][:], func=AF.Ln),
                                         reads=[bankB[acc[1]]], writes=[bf(f"ep{m}")])
                                    P.op("act", _I("activation", out=ep[m], in_=ep[m], func=AF.Exp, scale=-1.0),
                                         reads=[bf(f"ep{m}")], writes=[bf(f"ep{m}")])
                                    P.op("dve", _I("tensor_tensor", out=ep[2 + m], in0=banks[acc[0]][:], in1=ep[m], op=ALU.mult),
                                         reads=[bankB[acc[0]], bf(f"ep{m}")], writes=[bf(f"ep{2 + m}")])
                                    for bi_ in acc:
                                        pinned.discard(bi_)

                                pend_ep1.append(ep1)

                            def part2a(hh=hh, i=i):
                                P.op("dve", _I("scalar_tensor_tensor", out=ep[2], in0=ep[3], scalar=neglam[:, i:i + 1], in1=ep[2],
                                               op0=ALU.mult, op1=ALU.add),
                                     reads=[bf("ep3"), bf("ep2"), bf("neglam")], writes=[bf("ep2")])

                            def part2b(hh=hh, i=i):
                                P.op("act", _I("activation", out=sqb, in_=ep[2], func=AF.Square), reads=[bf("ep2")], writes=[sqB])

                            def part2(hh=hh, i=i):
                                bi = alloc_bank()
                                P.op("pe", _I("matmul", banks[bi][:], lhsT=onesb[:], rhs=sqb, start=True, stop=True),
                                     reads=[bf("onesb"), sqB], writes=[bankB[bi]])
                                P.op("act", _I("activation", out=ep[0], in_=banks[bi][:], func=AF.Ln, scale=1.0 / 128, bias=EPS),
                                     reads=[bankB[bi]], writes=[bf("ep0")])
                                P.op("act", _I("activation", out=ep[0], in_=ep[0], func=AF.Exp, scale=-0.5),
                                     reads=[bf("ep0")], writes=[bf("ep0")])
                                P.op("dve", _I("scalar_tensor_tensor", out=featT[:, hh, :], in0=ep[2], scalar=gs[:, i:i + 1], in1=ep[0],
                                               op0=ALU.mult, op1=ALU.mult),
                                     reads=[bf("ep2"), bf("ep0"), bf("gs")], writes=[ftB[hh]])

                            pend_p2.extend([(1, part2a), (3, part2b), (7, part2)])
                        while pend_ep1:
                            pend_ep1.pop(0)()
                        tail_p2 = [fn for _, fn in pend_p2]
                        assert len(tail_p2) == 3

                        def deferred(tail_p2=tail_p2):
                            for fn in tail_p2:
                                fn()
                        pool_part_b()
                        deferred()
                        proj_add(l, "ab_w_out", hcur, hB, nxt_gain=norm_xattn_g[l:l + 1, :], corder=(4, 5, 6, 7, 0, 1, 2, 3))
                    else:
                        order = [0, 2, 4, 1, 3, 5]
                        slot_of = {}
                        for cc in range(8):
                            if cc % 4 == 0:
                                for nb in order[(cc // 4) * 3:(cc // 4) * 3 + 3]:
                                    slot_of[nb] = load_block(l, "conv_w_in", 0, nb)
                            col = (cc % 4) * 128
                            bb = featmajor_proj(slot_of[cc // 4], col, None, None)
                            featmajor_proj(slot_of[2 + cc // 4], col, ep[0][:], [bf("ep0")], eng="act")
                            bx = featmajor_proj(slot_of[4 + cc // 4], col, None, None)
                            zb = zbuf[cc % 2]
                            zB = bf(("zb", 0))
                            P.op("pool", _I("tensor_copy", out=zb[:, 0:2], in_=zh[:, cc, :]), reads=[bf("zh")], writes=[zB])
                            P.op("dve", _I("tensor_tensor", out=zb[:, 2:514], in0=banks[bx][:], in1=ep[0][:], op=ALU.mult),
                                 reads=[bankB[bx], bf("ep0")], writes=[zB])
                            P.op("pool", _I("tensor_copy", out=zh[:, cc, :], in_=zb[:, 512:514]), reads=[zB], writes=[bf("zh")])
                            P.op("pool", _I("tensor_scalar", out=ep[1][:], in0=zb[:, 0:512], scalar1=convw[:, i, 0, cc:cc + 1], scalar2=0.0,
                                            op0=ALU.mult, op1=ALU.add),
                                 reads=[zB, bf("convw")], writes=[bf("ep1")])
                            for k in (1, 2):
                                P.op("dve", _I("scalar_tensor_tensor", out=ep[1][:], in0=zb[:, k:k + 512], scalar=convw[:, i, k, cc:cc + 1],
                                               in1=ep[1][:], op0=ALU.mult, op1=ALU.add),
                                     reads=[zB, bf("convw"), bf("ep1")], writes=[bf("ep1")])
                            P.op("dve", _I("tensor_tensor", out=featT[:, cc, :], in0=banks[bb][:], in1=ep[1][:], op=ALU.mult),
                                 reads=[bankB[bb], bf("ep1")], writes=[ftB[cc]])
                        proj_add(l, "conv_w_out", hcur, hB, nxt_gain=norm_xattn_g[l:l + 1, :])

                    if nxt is not None:
                        load_h((unit + 1) % 2, s, nxt[0], nxt[1])
                    if s == 0:
                        cv_flush(cv_per_unit if g + 1 < NG else 1000)

                    sqs = [load_block(l, "xattn_wq", 0, nb) for nb in range(2)]
                    for oc in range(8):
                        featmajor_proj(sqs[oc // 4], (oc % 4) * 128, qT[:, oc, :], [qTB[oc]])
                    def x_scores(hh):
                        pts = []
                        for mt in range(2):
                            bi = alloc_bank()
                            for dc in range(2):
                                P.op("pe", _I("matmul", banks[bi][:], lhsT=xKT[:, 2 * hh + dc, mt * 128:(mt + 1) * 128], rhs=qT[:, 2 * hh + dc, :],
                                              start=(dc == 0), stop=(dc == 1)),
                                     reads=[bf("xKT"), qTB[2 * hh + dc]], writes=[bankB[bi]])
                            pidx = (2 * hh + mt) % 4
                            P.op("act", _I("activation", out=PT[pidx][:], in_=banks[bi][:], func=AF.Exp, scale=1.0 / 16),
                                 reads=[bankB[bi]], writes=[bf(("PT", pidx))])
                            pts.append((PT[pidx], bf(("PT", pidx))))
                        return pts

                    def x_pv(hh, pts):
                        bo = []
                        for dc in range(2):
                            bi = alloc_bank()
                            bo.append(bi)
                            for mt in range(2):
                                P.op("pe", _I("matmul", banks[bi][:], lhsT=xV[:, mt, hh * 256 + dc * 128: hh * 256 + (dc + 1) * 128],
                                              rhs=pts[mt][0][:], start=(mt == 0), stop=(mt == 1)),
                                     reads=[bf("xV"), pts[mt][1]], writes=[bankB[bi]])
                        bl = alloc_bank()
                        for mt in range(2):
                            P.op("pe", _I("matmul", banks[bl][:], lhsT=onesb[:], rhs=pts[mt][0][:], start=(mt == 0), stop=(mt == 1)),
                                 reads=[bf("onesb"), pts[mt][1]], writes=[bankB[bl]])
                        e = ep[hh % 2]
                        eB = bf(f"ep{hh % 2}")
                        P.op("act", _I("activation", out=e, in_=banks[bl][:], func=AF.Ln), reads=[bankB[bl]], writes=[eB])
                        P.op("act", _I("activation", out=e, in_=e, func=AF.Exp, scale=-1.0), reads=[eB], writes=[eB])
                        for dc in range(2):
                            P.op("dve", _I("tensor_tensor", out=featT[:, 2 * hh + dc, :], in0=banks[bo[dc]][:], in1=e, op=ALU.mult),
                                 reads=[bankB[bo[dc]], eB], writes=[ftB[2 * hh + dc]])

                    xp = x_scores(0)
                    for hh in range(4):
                        xn = x_scores(hh + 1) if hh + 1 < 4 else None
                        x_pv(hh, xp)
                        xp = xn
                    proj_add(l, "xattn_wo", hcur, hB, nxt_gain=norm_mlp_g[l:l + 1, :])

                    if last:
                        load_gain(final_norm_g[0:1, :], final=True)
                    if nxt is not None:
                        load_gain(norm_mix_g[nxt[0]:nxt[0] + 1, :])
                        gain_ready = True
                    else:
                        gain_ready = False
                    epc = 0
                    nub = (unit + 1) % 2
                    nh_ap = [hres2[nub][:, t, :] for t in range(4)]
                    nsl = {}
                    for fh in range(2):
                        for blk in range(4):
                            s1 = load_block(l, "mlp_w1", 0, fh * 4 + blk)
                            for fc in range(4):
                                bi = featmajor_proj(s1, fc * 128, None, None)
                                e = epc % 4
                                epc += 1
                                P.op("act", _I("activation", out=ep[e], in_=banks[bi][:], func=AF.Relu),
                                     reads=[bankB[bi]], writes=[bf(f"ep{e}")])
                                P.op("dve", _I("tensor_tensor", out=aT[:, blk * 4 + fc, :], in0=banks[bi][:], in1=ep[e], op=ALU.mult),
                                     reads=[bankB[bi], bf(f"ep{e}")], writes=[aTB[blk * 4 + fc]])
                        hide = (fh == 1 and nxt is not None)
                        if hide:
                            nsl[0] = norm_pre(nh_ap[0], hB2[nub][0])
                            nsl[1] = norm_pre(nh_ap[1], hB2[nub][1])
                        for dh in range(2):
                            s2 = [load_block(l, "mlp_w2", fh * 2 + kb, dh) for kb in range(2)]
                            for t in range(4):
                                bi = alloc_bank()
                                for kb in range(2):
                                    for c in range(8):
                                        P.op("pe", _I("matmul", banks[bi][:], lhsT=aT[:, kb * 8 + c, t * 128:(t + 1) * 128], rhs=wring[s2[kb]][:, c, :],
                                                      start=(kb == 0 and c == 0), stop=(kb == 1 and c == 7)),
                                             reads=[aTB[kb * 8 + c], wrB[s2[kb]]], writes=[bankB[bi]])
                                P.op("dve", _I("tensor_tensor", out=hcur[:, t, dh * 512:(dh + 1) * 512], in0=banks[bi][:],
                                               in1=hcur[:, t, dh * 512:(dh + 1) * 512], op=ALU.add),
                                     reads=[bankB[bi], hB[t]], writes=[hB[t]])
                                if hide and dh == 1 and t == 1:
                                    norm_tr(nsl[2], 2, hnT, bf("hnT"))
                                    norm_tr(nsl[3], 3, hnT, bf("hnT"))
                            if hide and dh == 0:
                                norm_tr(nsl[0], 0, hnT, bf("hnT"))
                                nsl[2] = norm_pre(nh_ap[2], hB2[nub][2])
                                norm_tr(nsl[1], 1, hnT, bf("hnT"))
                                nsl[3] = norm_pre(nh_ap[3], hB2[nub][3])
                        if hide:
                            norm_done = True

                    if last:
                        for t in range(4):
                            rs, sB = rms_rstd(hap[t], [hB[t]], D)
                            P.op("dve", _I("scalar_tensor_tensor", out=hcur[:, t, :], in0=hcur[:, t, :], scalar=rs, in1=gbf,
                                           op0=ALU.mult, op1=ALU.mult),
                                 reads=[hB[t], sB, bf("biasT")], writes=[hB[t]])
                    dst = y if last else hscr
                    final_ev = P.op("pool", _I("dma_start", out=dst[s, tok0:tok0 + G, :].rearrange("(t p) d -> p t d", p=128), in_=hcur[:]),
                                    reads=hB, writes=[bf(("hscr", s, g))], dma=s_hst[ub])
                    unit += 1
        P.wait_event("pool", final_ev)
        P.emit()
    return nc


_CACHE = {}


def _get_prog(S, NSEQ):
    key = (S, NSEQ)
    if key not in _CACHE:
        _CACHE[key] = build_program(S, NSEQ)
    return _CACHE[key]


def make_in_maps(inputs, ncores, nseq):
    consts = host_consts()
    maps = []
    for c in range(ncores):
        m = {}
        for k, v in inputs.items():
            v = np.asarray(v, dtype=np.float32)
            if k in ("x", "mem"):
                m[k] = np.ascontiguousarray(v[c * nseq:(c + 1) * nseq])
            elif k in ("mem_norm_g", "final_norm_g"):
                m[k] = np.ascontiguousarray(v.reshape(1, -1))
            else:
                m[k] = np.ascontiguousarray(v)
        m.update(consts)
        maps.append(m)
    return maps


def kernel(**inputs):
    x = np.asarray(inputs["x"])
    Bsz, S, _ = x.shape
    ncores = 8
    nseq = Bsz // ncores
    nc = _get_prog(S, nseq)
    maps = make_in_maps(inputs, ncores, nseq)
    res = run_bass_kernel_spmd(nc, maps, core_ids=list(range(ncores)))
    out = np.concatenate([np.asarray(r["y"]) for r in res.results], axis=0)
    return out.astype(np.float32)
```

```python
import math
import numpy as np
from contextlib import ExitStack
import concourse.bass as bass
import concourse.mybir as mybir
from concourse.bass_utils import run_bass_kernel_spmd

F32 = mybir.dt.float32
BF16 = mybir.dt.bfloat16
AF = mybir.ActivationFunctionType
ALU = mybir.AluOpType

D = 1024
NMEM = 256
EPS = 1e-6
G = 512
FW = 1152


class Buf:
    __slots__ = ("w", "r", "al", "stamp")

    def __init__(self):
        self.w = None
        self.r = {}
        self.al = ()
        self.stamp = 0


class Op:
    __slots__ = ("fn", "waits", "signal", "dma")

    def __init__(self, fn, waits):
        self.fn = fn
        self.waits = waits
        self.signal = False
        self.dma = None


class Prog:
    ENGS = ("pe", "act", "dve", "pool", "sp")

    def __init__(self, nc, es):
        self.nc = nc
        self.es = es
        self.ops = {e: [] for e in self.ENGS}
        self.ncomp = {e: 0 for e in self.ENGS}
        self.comp_idx = {e: [] for e in self.ENGS}
        self.seen = {e: {} for e in self.ENGS}
        self.sems = {e: es.enter_context(nc.semaphore("s_" + e)) for e in self.ENGS}
        self.dma_cnt = {}
        self.gctr = 0

    def dma_sem(self, name):
        s = self.es.enter_context(self.nc.semaphore(name))
        self.dma_cnt[id(s)] = [s, 0]
        return s

    def _need(self, eng, ev, waits, same_ok):
        if ev is None:
            return
        if ev[0] == 'c' and ev[1] == eng and not same_ok and eng == "pe":
            return
        key = (ev[0], ev[1])
        if self.seen[eng].get(key, -1) >= ev[2]:
            return
        self.seen[eng][key] = ev[2]
        waits.append(ev)

    def op(self, eng, fn, reads=(), writes=(), dma=None):
        waits = []
        for b in reads:
            self._need(eng, b.w, waits, True)
        for b in writes:
            self._need(eng, b.w, waits, False)
            for ev in b.r.values():
                self._need(eng, ev, waits, False)
            for a in b.al:
                self._need(eng, a.w, waits, True)
                for ev in a.r.values():
                    self._need(eng, ev, waits, True)
        o = Op(fn, waits)
        if dma is not None:
            ent = self.dma_cnt[id(dma)]
            ent[1] += 16
            o.dma = dma
            ev = ('d', id(dma), ent[1])
        else:
            seq = self.ncomp[eng]
            self.ncomp[eng] += 1
            self.comp_idx[eng].append(len(self.ops[eng]))
            ev = ('c', eng, seq)
        self.ops[eng].append(o)
        self.gctr += 1
        for b in reads:
            b.r[(ev[0], ev[1])] = ev
            b.stamp = self.gctr
        for b in writes:
            b.w = ev
            b.r = {}
            b.stamp = self.gctr
        return ev

    def wait_event(self, eng, ev):
        waits = []
        self._need(eng, ev, waits, True)
        if waits:
            self.ops[eng].append(Op(None, waits))

    def emit(self):
        nc = self.nc
        for e in self.ENGS:
            for o in self.ops[e]:
                for ev in o.waits:
                    if ev[0] == 'c':
                        self.ops[ev[1]][self.comp_idx[ev[1]][ev[2]]].signal = True
        cnt = {}
        for e in self.ENGS:
            c = 0
            arr = []
            for idx in self.comp_idx[e]:
                if self.ops[e][idx].signal:
                    c += 1
                arr.append(c)
            cnt[e] = arr
        sem_by_id = {k: v[0] for k, v in self.dma_cnt.items()}

        def run(e, h):
            for o in self.ops[e]:
                for ev in o.waits:
                    if ev[0] == 'c':
                        h.wait_ge(self.sems[ev[1]], cnt[ev[1]][ev[2]])
                    else:
                        h.wait_ge(sem_by_id[ev[1]], ev[2])
                if o.fn is None:
                    continue
                ins = o.fn(h)
                if o.dma is not None:
                    ins.then_inc(o.dma, 16)
                elif o.signal:
                    ins.then_inc(self.sems[e], 1)

        with nc.Block() as block:
            @block.tensor
            def _(h):
                run("pe", h)

            @block.scalar
            def _(h):
                run("act", h)

            @block.vector
            def _(h):
                run("dve", h)

            @block.gpsimd
            def _(h):
                run("pool", h)

            @block.sync
            def _(h):
                run("sp", h)


def t5_bucket_np(n):
    n = np.asarray(n, dtype=np.int64)
    nf = np.maximum(n, 1).astype(np.float32)
    large = 16 + (np.log(nf / np.float32(16)) / np.float32(math.log(128 / 16)) * np.float32(16)).astype(np.int32)
    large = np.minimum(large, 31)
    return np.where(n < 16, n, large)


def host_consts():
    ident = np.eye(128, dtype=np.float32)
    onehot = np.zeros((33, FW), np.float32)
    for i in range(FW):
        d = i - 511
        if d < 0:
            onehot[32, i] = -240000.0
        else:
            onehot[int(t5_bucket_np(d)), i] = 8.0
    invc = np.zeros((128, 4, 16), np.float32)
    for gi, w in enumerate((2, 4, 8, 16)):
        for t in range(16):
            invc[:, gi, t] = 1.0 / min(t + 1, w)
    return {"c_ident": ident, "c_onehot": onehot, "c_invc": invc}


WSPEC = [
    ("ab_w_in", "even", 1024, 2048), ("ab_w_out", "even", 1024, 1024),
    ("conv_w_in", "odd", 1024, 3072), ("conv_w_out", "odd", 1024, 1024),
    ("xattn_wkv", "all", 1024, 2048), ("xattn_wq", "all", 1024, 1024), ("xattn_wo", "all", 1024, 1024),
    ("mlp_w1", "all", 1024, 4096), ("mlp_w2", "all", 4096, 1024),
]


def _I(name, *a, **k):
    return lambda h: getattr(h, name)(*a, **k)


def build_program(S, NSEQ, DEPTH=4):
    NG = S // G
    NT = S // 128
    nc = bass.Bass("TRN2", target_bir_lowering=False)
    n_even = (DEPTH + 1) // 2
    n_odd = DEPTH // 2

    def din(name, shape):
        return nc.dram_tensor(name, list(shape), F32, kind="ExternalInput")

    x = din("x", [NSEQ, S, D]).ap()
    mem = din("mem", [NSEQ, NMEM, D]).ap()
    rel_bias = din("rel_bias", [32, 4]).ap()
    mem_norm_g = din("mem_norm_g", [1, D]).ap()
    norm_mix_g = din("norm_mix_g", [DEPTH, D]).ap()
    norm_xattn_g = din("norm_xattn_g", [DEPTH, D]).ap()
    norm_mlp_g = din("norm_mlp_g", [DEPTH, D]).ap()
    final_norm_g = din("final_norm_g", [1, D]).ap()
    W = {}
    for name, kind, K, N in WSPEC:
        n = {"even": n_even, "odd": n_odd, "all": DEPTH}[kind]
        W[name] = din(name, [max(n, 1), K, N]).ap()
    lam_in = [din(nm, [max(n_even, 1), 64]).ap() for nm in ("lambda_q1", "lambda_k1", "lambda_q2", "lambda_k2")]
    subln_t = din("subln_g", [max(n_even, 1), 128])
    pool_w = din("pool_w", [max(n_even, 1), 4, 128, 128]).ap()
    pscale_t = din("pool_scale", [max(n_even, 1), 512])
    convw_t = din("conv_w", [max(n_odd, 1), 3, D])
    c_ident = din("c_ident", [128, 128]).ap()
    c_onehot = din("c_onehot", [33, FW]).ap()
    c_invc = din("c_invc", [128, 4, 16]).ap()
    y = nc.dram_tensor("y", [NSEQ, S, D], F32, kind="ExternalOutput").ap()

    def col_ap(t, off):
        return bass.AP(t, off, [[1, 128], [1, 1]])

    blk_ids = {}
    nblk = 0
    for l in range(DEPTH):
        for name, kind, K, N in WSPEC:
            if (kind == "even" and l % 2 == 1) or (kind == "odd" and l % 2 == 0):
                continue
            for kb in range(K // 1024):
                for nb in range(N // 512):
                    blk_ids[(l, name, kb, nb)] = nblk
                    nblk += 1
    wblk = nc.dram_tensor("wblk", [nblk, 128, 4096], BF16).ap()
    poolw_bf = nc.dram_tensor("poolw_bf", [max(n_even, 1), 128, 512], BF16).ap()
    hscr = nc.dram_tensor("hscr", [NSEQ, S, D], F32).ap()
    Fd_t = nc.dram_tensor("Fd", [4, FW], BF16)
    Fd = Fd_t.ap()
    FR = 130
    Fd2_t = nc.dram_tensor("Fd2", [4, FR, FW], BF16)
    Fd2 = Fd2_t.ap()

    with ExitStack() as es:
        P = Prog(nc, es)

        def sb(name, shape, dt):
            return es.enter_context(nc.sbuf_tensor(name, list(shape), dt))

        hres2 = [sb(f"hres{i}", [128, 4, D], F32) for i in range(2)]
        hnT = sb("hnT", [128, 8, G], BF16)
        hn_tmps = [sb(f"hn_tmp{i}", [128, D], BF16) for i in range(2)]
        wring = [sb(f"wr{i}", [128, 8, 512], BF16) for i in range(4)]
        memT = sb("memT", [128, 8, NMEM], BF16)
        xKT = sb("xKT", [128, 8, NMEM], BF16)
        xV = sb("xV", [128, 2, D], BF16)
        gb = sb("gb", [128, D], F32)
        KT = sb("KT", [128, 4, S], BF16)
        Vc = sb("Vc", [128, NT, 512], BF16)
        featT = sb("featT", [128, 8, G], BF16)
        qT = sb("qT", [128, 8, G], BF16)
        aT = sb("aT", [128, 16, G], BF16)
        PT = [sb(f"PT{i}", [128, G], BF16) for i in range(4)]
        epall = sb("epall", [128, 4 * G], F32)
        ep = [epall[:, i * G:(i + 1) * G] for i in range(4)]
        aT32 = aT[:].rearrange("p c n -> p (c n)").bitcast(F32)
        uT = aT32[:, 0:2112].rearrange("p (g n) -> p g n", g=4)
        pw = [aT32[:, 2112 + i * 528:2112 + (i + 1) * 528] for i in range(2)]
        pTb = [aT32[:, 3168 + i * 256:3168 + (i + 1) * 256].bitcast(BF16) for i in range(2)]
        biasT = sb("biasT", [128, 5, G], BF16)
        zbuf = [aT32[:, 0:514]] * 2
        zh = sb("zh", [128, 8, 2], F32)
        uh = sb("uh", [128, 4, 16], F32)
        idf = pw[1][:, 0:128]
        idb = sb("idb", [128, 128], BF16)
        onesb = sb("onesb", [128, 128], BF16)
        stt = sb("stt", [128, 4, 4], F32)
        b31 = sb("b31", [128, 4], F32)
        lamt = pw[0][:, 0:512].rearrange("p (i k d) -> p i k d", i=2, k=4)
        lams = sb("lams", [128, 8], F32)
        neglam = sb("neglam", [128, 2], F32)
        gs = sb("gs", [128, 2], F32)
        pscale = sb("pscale", [128, 2, 4], F32)
        poolw_sb = sb("poolw_sb", [128, 4, 128], BF16)
        convw = sb("convw", [128, 2, 3, 8], F32)
        invc = sb("invc", [128, 4, 16], F32)
        rb33 = sb("rb33", [33, 4], F32)
        oh = epall[0:33, 0:FW]
        Fsb = featT[:].rearrange("p c n -> p (c n)")[0:4, 0:FW]

        banks = [es.enter_context(nc.psum_tensor(f"bk{i}", [128, 512], F32)) for i in range(8)]
        bankB = [Buf() for _ in range(8)]
        pinned = set()
        rot = [0]

        def alloc_bank():
            best = None
            for k in range(8):
                i = (rot[0] + k) % 8
                if i in pinned:
                    continue
                if best is None or bankB[i].stamp < bankB[best].stamp:
                    best = i
            rot[0] = best + 1
            bankB[best].stamp = P.gctr + 1
            return best

        Bd = {}

        def bf(name):
            if name not in Bd:
                Bd[name] = Buf()
            return Bd[name]

        _al = [bf("uT"), bf(("pw", 0)), bf(("pw", 1)), bf(("pTb", 0)), bf(("pTb", 1)), bf(("zb", 0))]
        aTB = [bf(("aT", k)) for k in range(16)]
        ftB = [bf(("featT", k)) for k in range(8)]
        qTB = [bf(("qT", k)) for k in range(8)]
        for _a in aTB:
            _a.al = tuple(_al)
        for _b in _al:
            _b.al = tuple(aTB)
        bf("uT").al = tuple(aTB) + (bf(("zb", 0)),)
        bf(("zb", 0)).al = tuple(aTB) + (bf("uT"),)
        gbf = biasT[:].rearrange("p r n -> p (r n)").bitcast(F32)[:, 0:D]
        hB2 = [[Buf() for _ in range(4)] for _ in range(2)]
        junk = qT[:].rearrange("p c n -> p (c n)")[:, 0:D]

        s_misc = P.dma_sem("d_misc")
        n_extra = min(4, NT // 8)
        s_wr = [P.dma_sem(f"d_wr{i}") for i in range(4 + n_extra)]
        s_h = [P.dma_sem("d_h0"), P.dma_sem("d_h1")]
        s_hst = [P.dma_sem("d_hst0"), P.dma_sem("d_hst1")]
        s_gb = P.dma_sem("d_gb")
        s_bias = P.dma_sem("d_bias")
        s_f = P.dma_sem("d_f")
        s_pw = P.dma_sem("d_pw")
        s_f2 = P.dma_sem("d_f2")

        evac_ctr = [0]
        ptc = [0]

        def evac(bi, src_ap, dst_ap, dst_bufs, eng=None):
            if eng is None:
                eng = "act" if evac_ctr[0] % 2 == 0 else "dve"
                evac_ctr[0] += 1
            if eng == "act":
                P.op("act", _I("copy", out=dst_ap, in_=src_ap), reads=[bankB[bi]], writes=dst_bufs)
            else:
                P.op("dve", _I("tensor_copy", out=dst_ap, in_=src_ap), reads=[bankB[bi]], writes=dst_bufs)

        setup_bufs = []

        def sdma(out_ap, in_ap, bname):
            P.op("sp", _I("dma_start", out=out_ap, in_=in_ap), writes=[bf(bname)], dma=s_misc)
            if bf(bname) not in setup_bufs:
                setup_bufs.append(bf(bname))

        sdma(idf, c_ident, ("pw", 1))
        sdma(invc[:], c_invc, "invc")
        sdma(b31[:], rel_bias[31:32, :].partition_broadcast(128), "b31")
        sdma(oh, c_onehot, "ep0"); sdma(oh, c_onehot, "ep1") if False else None; setup_bufs.extend([bf("ep1"), bf("ep2")])
        sdma(rb33[0:32, :], rel_bias, "rb33")
        for i in range(n_even):
            for k in range(4):
                sdma(lamt[:, i, k, :], lam_in[k][i:i + 1, :].partition_broadcast(128), ("pw", 0))
            sdma(gs[:, i:i + 1], col_ap(subln_t, i * 128), "gs")
            for gi in range(4):
                sdma(pscale[:, i, gi:gi + 1], col_ap(pscale_t, i * 512 + gi * 128), "pscale")
        for i in range(n_odd):
            for k in range(3):
                for c in range(8):
                    sdma(convw[:, i, k, c:c + 1], col_ap(convw_t, (i * 3 + k) * D + c * 128), "convw")
        fence = ('d', id(s_misc), P.dma_cnt[id(s_misc)][1])
        for b in setup_bufs:
            b.w = fence

        P.op("dve", _I("tensor_copy", out=idb[:], in_=idf), reads=[bf(("pw", 1))], writes=[bf("idb")])
        P.op("dve", _I("memset", onesb[:], 1.0), writes=[bf("onesb")])
        P.op("dve", _I("memset", rb33[32:33, :], 1.0), reads=[bf("rb33")], writes=[bf("rb33x")])
        for j0 in range(0, FW, 384):
            bi = alloc_bank()
            P.op("pe", _I("matmul", banks[bi][0:4, 0:384], lhsT=rb33[:, :], rhs=oh[:, j0:j0 + 384], start=True, stop=True),
                 reads=[bf("rb33"), bf("rb33x"), bf("ep0"), bf("ep1"), bf("ep2")], writes=[bankB[bi]])
            evac(bi, banks[bi][0:4, 0:384], Fsb[:, j0:j0 + 384], ftB, eng="dve")
        P.op("sp", _I("dma_start", out=Fd, in_=Fsb), reads=ftB, writes=[bf("Fd0")], dma=s_f)
        for hh in range(4):
            P.op("sp", _I("dma_start", out=Fd2[hh], in_=Fd[hh:hh + 1, :].partition_broadcast(FR)), reads=[bf("Fd0")], writes=[bf("Fd")],
                 dma=s_f2)
        bf("Fd").w = ('d', id(s_f2), P.dma_cnt[id(s_f2)][1])
        for i in range(n_even):
            lam_init = 0.8 - 0.6 * math.exp(-0.3 * (2 * i))
            for m in range(2):
                P.op("dve", _I("tensor_tensor", out=lamt[:, i, 2 * m, :], in0=lamt[:, i, 2 * m, :], in1=lamt[:, i, 2 * m + 1, :], op=ALU.mult),
                     reads=[bf(("pw", 0))], writes=[bf(("pw", 0))])
                P.op("act", _I("activation", out=lamt[:, i, 2 * m + 1, :], in_=lamt[:, i, 2 * m, :], func=AF.Identity,
                               accum_out=lams[:, m:m + 1]),
                     reads=[bf(("pw", 0))], writes=[bf(("pw", 0)), bf("lams")])
                P.op("act", _I("activation", out=lams[:, 2 + m:3 + m], in_=lams[:, m:m + 1], func=AF.Exp),
                     reads=[bf("lams")], writes=[bf("lams")])
            P.op("dve", _I("tensor_tensor", out=lams[:, 4:5], in0=lams[:, 3:4], in1=lams[:, 2:3], op=ALU.subtract),
                 reads=[bf("lams")], writes=[bf("lams")])
            P.op("dve", _I("tensor_scalar", out=neglam[:, i:i + 1], in0=lams[:, 4:5], scalar1=-lam_init, scalar2=None, op0=ALU.add),
                 reads=[bf("lams")], writes=[bf("neglam")])
            P.op("dve", _I("tensor_scalar", out=gs[:, i:i + 1], in0=gs[:, i:i + 1], scalar1=1.0 - lam_init, scalar2=None, op0=ALU.mult),
                 reads=[bf("gs")], writes=[bf("gs")])

        wconv_buf = {}
        NCV = 64
        s_cv = [P.dma_sem(f"d_cv{k}") for k in range(NCV)]
        cv_evs = []

        cv_pending = []

        def cv_dma(fn, b, lazy=True):
            if lazy:
                cv_pending.append((fn, b))
                return
            k = len(cv_evs)
            if k >= NCV:
                P.wait_event("pool", cv_evs[k - NCV])
            cv_evs.append(P.op("pool", fn, writes=[b], dma=s_cv[k % NCV]))

        CV_ORDER = ["xattn_wkv", "ab_w_in", "conv_w_in", "ab_w_out", "conv_w_out", "xattn_wq", "xattn_wo", "mlp_w1", "mlp_w2"]
        wspec_sorted = sorted(WSPEC, key=lambda w: CV_ORDER.index(w[0]))
        for l in range(DEPTH):
            if l % 2 == 0:
                wconv_buf[(l, "pool_w")] = Buf()
            for name, kind, K, N in WSPEC:
                if (kind == "even" and l % 2 == 1) or (kind == "odd" and l % 2 == 0):
                    continue
                for kb in range(K // 1024):
                    for nb in range(N // 512):
                        wconv_buf[blk_ids[(l, name, kb, nb)]] = Buf()

        def convert_layer(l):
            i = l // 2
            if l % 2 == 0:
                pb = wconv_buf[(l, "pool_w")]
                cv_dma(_I("dma_start", out=poolw_bf[i].rearrange("c (g d) -> c g d", g=4), in_=pool_w[i].rearrange("g c d -> c g d")), pb)
            for name, kind, K, N in wspec_sorted:
                if (kind == "even" and l % 2 == 1) or (kind == "odd" and l % 2 == 0):
                    continue
                li = l if kind == "all" else i
                for kb in range(K // 1024):
                    for nb in range(N // 512):
                        bid = blk_ids[(l, name, kb, nb)]
                        b = wconv_buf[bid]
                        src = W[name][li, kb * 1024:(kb + 1) * 1024, nb * 512:(nb + 1) * 512].rearrange("(c p) n -> p c n", p=128)
                        dst = wblk[bid].rearrange("p (c n) -> p c n", c=8)
                        cv_dma(_I("dma_start", out=dst, in_=src), b)

        def cv_flush(n):
            for _ in range(n):
                if cv_pending:
                    cv_dma(*cv_pending.pop(0), lazy=False)

        convert_layer(0)
        cv_flush(1000)

        wr_ctr = [0]
        wrB = [Buf() for _ in range(4 + n_extra)]
        for k in range(n_extra):
            wring.append(Vc[:, 8 * k:8 * k + 8, :])
            vb = tuple(bf(("V", gg)) for gg in (2 * k, 2 * k + 1) if gg < NG)
            wrB[4 + k].al = vb
            for b_ in vb:
                b_.al = (wrB[4 + k],)

        def load_block(l, name, kb, nb):
            bid = blk_ids[(l, name, kb, nb)]
            slot = wr_ctr[0] % (4 + n_extra if l % 2 == 1 else 4)
            wr_ctr[0] += 1
            P.op("sp", _I("dma_start", out=wring[slot][:], in_=wblk[bid].rearrange("p (c n) -> p c n", c=8)),
                 reads=[wconv_buf[bid]], writes=[wrB[slot]], dma=s_wr[slot])
            return slot

        def load_gain(row_ap, final=False):
            if final:
                P.op("pool", _I("dma_start", out=gbf, in_=row_ap.partition_broadcast(128)), writes=[bf("biasT")], dma=s_bias)
            else:
                P.op("pool", _I("dma_start", out=gb[:], in_=row_ap.partition_broadcast(128)), writes=[bf("gb")], dma=s_gb)

        st_ctr = [0]

        def rms_rstd(src_ap, src_bufs, n):
            k = st_ctr[0] % 4
            st_ctr[0] += 1
            sk = stt[:, k, :]
            sB = bf(("st", k))
            P.op("act", _I("activation", out=junk[:, 0:n], in_=src_ap, func=AF.Square, accum_out=sk[:, 0:1]),
                 reads=src_bufs, writes=[qTB[0], qTB[1], sB])
            P.op("act", _I("activation", out=sk[:, 1:2], in_=sk[:, 0:1], func=AF.Ln, scale=1.0 / n, bias=EPS),
                 reads=[sB], writes=[sB])
            P.op("act", _I("activation", out=sk[:, 2:3], in_=sk[:, 1:2], func=AF.Exp, scale=-0.5),
                 reads=[sB], writes=[sB])
            return sk[:, 2:3], sB

        nrm_ctr = [0]

        def norm_pre(src_ap, src_buf):
            slot = nrm_ctr[0] % 2
            nrm_ctr[0] += 1
            rs, sB = rms_rstd(src_ap, [src_buf], D)
            for hf in range(2):
                P.op("dve", _I("scalar_tensor_tensor", out=hn_tmps[slot][:, hf * 512:(hf + 1) * 512], in0=src_ap[:, hf * 512:(hf + 1) * 512],
                               scalar=rs, in1=gb[:, hf * 512:(hf + 1) * 512], op0=ALU.mult, op1=ALU.mult),
                     reads=[src_buf, sB, bf("gb")], writes=[bf(("hn_tmp", slot, hf))])
            return slot

        def norm_tr(slot, t, dstT, dst_buf):
            bi = alloc_bank()
            pv = banks[bi][:].bitcast(BF16).rearrange("p (c n) -> p c n", c=8)
            for c in range(8):
                P.op("pe", _I("transpose", out=pv[:, c, :], in_=hn_tmps[slot][:, c * 128:(c + 1) * 128], identity=idb[:]),
                     reads=[bf(("hn_tmp", slot, c // 4)), bf("idb")], writes=[bankB[bi]])
            evac(bi, pv, dstT[:, :, t * 128:(t + 1) * 128], [dst_buf])

        def norm_T(src_aps, src_bufs, dstT, dst_buf, ntile):
            slots = {}
            for t in range(ntile):
                slots[t] = norm_pre(src_aps[t], src_bufs[t])
                if t >= 1:
                    norm_tr(slots[t - 1], t - 1, dstT, dst_buf)
            norm_tr(slots[ntile - 1], ntile - 1, dstT, dst_buf)

        def proj_add(l, name, hcur, hB, nxt_gain=None, corder=(0, 1, 2, 3, 4, 5, 6, 7)):
            slots = [load_block(l, name, 0, nb) for nb in range(2)]
            if nxt_gain is not None:
                load_gain(nxt_gain)
            nslot = {}
            for t in range(4):
                for nb in range(2):
                    bi = alloc_bank()
                    for ci, c in enumerate(corder):
                        P.op("pe", _I("matmul", banks[bi][:], lhsT=featT[:, c, t * 128:(t + 1) * 128], rhs=wring[slots[nb]][:, c, :],
                                      start=(ci == 0), stop=(ci == 7)),
                             reads=[ftB[c], wrB[slots[nb]]], writes=[bankB[bi]])
                    P.op("dve", _I("tensor_tensor", out=hcur[:, t, nb * 512:(nb + 1) * 512], in0=banks[bi][:],
                                   in1=hcur[:, t, nb * 512:(nb + 1) * 512], op=ALU.add),
                         reads=[bankB[bi], hB[t]], writes=[hB[t]])
                if nxt_gain is not None:
                    nslot[t] = norm_pre(hcur[:, t, :], hB[t])
                    if t >= 1:
                        norm_tr(nslot[t - 1], t - 1, hnT, bf("hnT"))
            if nxt_gain is not None:
                norm_tr(nslot[3], 3, hnT, bf("hnT"))

        def featmajor_proj(slot, col0, dst_ap, dst_bufs, rhs_ap=None, rhs_buf=None, n=G, eng=None):
            bi = alloc_bank()
            rhs_ap = hnT if rhs_ap is None else rhs_ap
            rhs_buf = bf("hnT") if rhs_buf is None else rhs_buf
            for c in range(8):
                P.op("pe", _I("matmul", banks[bi][:, 0:n], lhsT=wring[slot][:, c, col0:col0 + 128], rhs=rhs_ap[:, c, 0:n],
                              start=(c == 0), stop=(c == 7)),
                     reads=[wrB[slot], rhs_buf], writes=[bankB[bi]])
            if dst_ap is not None:
                evac(bi, banks[bi][:, 0:n], dst_ap, dst_bufs, eng=eng)
            return bi

        qT4 = qT[:].rearrange("p (h m) n -> p h m n", m=2)
        def load_h(ub, s, l, g):
            srcT = x if l == 0 else hscr
            P.op("pool", _I("dma_start", out=hres2[ub][:], in_=srcT[s, g * G:(g + 1) * G, :].rearrange("(t p) d -> p t d", p=128)),
                 reads=[bf(("hscr", s, g))], writes=hB2[ub], dma=s_h[ub])

        final_ev = None
        unit = 0
        gain_ready = False
        for s in range(NSEQ):
            ub0 = unit % 2
            load_gain(mem_norm_g[0:1, :])
            P.op("pool", _I("dma_start", out=hres2[ub0][:, 0:2, :], in_=mem[s].rearrange("(t p) d -> p t d", p=128)),
                 writes=hB2[ub0][0:2], dma=s_h[ub0])
            norm_T([hres2[ub0][:, t, :] for t in range(2)], hB2[ub0][0:2], memT, bf("memT"), 2)
            load_h(ub0, s, 0, 0)
            gain_ready = False
            norm_done = False

            for l in range(DEPTH):
                i = l // 2
                even = (l % 2 == 0)
                last = (l == DEPTH - 1)
                slots = [load_block(l, "xattn_wkv", 0, nb) for nb in range(4)]
                for kc in range(8):
                    featmajor_proj(slots[kc // 4], (kc % 4) * 128, xKT[:, kc, :], [bf("xKT")], rhs_ap=memT, rhs_buf=bf("memT"), n=NMEM)
                for mt in range(2):
                    for half in range(2):
                        bi = alloc_bank()
                        for c in range(8):
                            P.op("pe", _I("matmul", banks[bi][:], lhsT=memT[:, c, mt * 128:(mt + 1) * 128], rhs=wring[slots[2 + half]][:, c, :],
                                          start=(c == 0), stop=(c == 7)),
                                 reads=[bf("memT"), wrB[slots[2 + half]]], writes=[bankB[bi]])
                        evac(bi, banks[bi][:], xV[:, mt, half * 512:(half + 1) * 512], [bf("xV")])
                if even:
                    P.op("sp", _I("dma_start", out=poolw_sb[:], in_=poolw_bf[i].rearrange("c (g d) -> c g d", g=4)),
                         reads=[wconv_buf[(l, "pool_w")]], writes=[bf("poolw_sb")], dma=s_pw)
                    P.op("pool", _I("memset", uh[:], 0.0), writes=[bf("uh")])
                else:
                    P.op("pool", _I("memset", zh[:], 0.0), writes=[bf("zh")])
                if s == 0 and l + 1 < DEPTH:
                    convert_layer(l + 1)
                    cv_per_unit = -(-len(cv_pending) // NG)

                for g in range(NG):
                    tok0 = g * G
                    ub = unit % 2
                    hcur = hres2[ub]
                    hB = hB2[ub]
                    hap = [hcur[:, t, :] for t in range(4)]
                    if g + 1 < NG:
                        nxt = (l, g + 1)
                    elif l + 1 < DEPTH:
                        nxt = (l + 1, 0)
                    else:
                        nxt = None

                    if not norm_done:
                        if not gain_ready:
                            load_gain(norm_mix_g[l:l + 1, :])
                        norm_T(hap, hB, hnT, bf("hnT"), 4)
                    norm_done = False
                    if even:
                        sq = load_block(l, "ab_w_in", 0, 0)
                        sk = load_block(l, "ab_w_in", 0, 1)
                        sv = load_block(l, "ab_w_in", 0, 2)
                        su = load_block(l, "ab_w_in", 0, 3)
                        P.op("pool", _I("memset", qT4[64:128, :, 0, :], 0.0), writes=qTB)
                        P.op("pool", _I("memset", qT4[0:64, :, 1, :], 0.0), writes=qTB)
                        for hh in range(4):
                            bi = featmajor_proj(sq, hh * 128, None, None)
                            evac(bi, banks[bi][0:64, :], qT4[0:64, hh, 0, :], [qTB[2 * hh]])
                            evac(bi, banks[bi][64:128, :], qT4[64:128, hh, 1, :], [qTB[2 * hh + 1]])
                        for hh in range(4):
                            featmajor_proj(sk, hh * 128, KT[:, hh, tok0:tok0 + G], [bf(("KT", g))])
                        for t in range(4):
                            bi = alloc_bank()
                            for c in range(8):
                                P.op("pe", _I("matmul", banks[bi][:], lhsT=hnT[:, c, t * 128:(t + 1) * 128], rhs=wring[sv][:, c, :],
                                              start=(c == 0), stop=(c == 7)),
                                     reads=[bf("hnT"), wrB[sv]], writes=[bankB[bi]])
                            evac(bi, banks[bi][:], Vc[:, g * 4 + t, :], [bf(("V", g))])
                        for gi in range(4):
                            featmajor_proj(su, gi * 128, uT[:, gi, 16:528], [bf("uT")])
                        def load_bias(hh):
                            src_ap = bass.AP(Fd2_t, hh * FR * FW + 127, [[FW - 1, 128], [128, 5], [1, G]])
                            P.op("pool", _I("dma_start", out=biasT[:], in_=src_ap), reads=[bf("Fd")], writes=[bf("biasT")], dma=s_bias)

                        load_bias(0)
                        P.op("pool", _I("tensor_copy", out=uT[:, :, 0:16], in_=uh[:]), reads=[bf("uh")], writes=[bf("uT")])
                        pts4 = [(pTb[0], bf(("pTb", 0))), (pTb[1], bf(("pTb", 1))),
                                (hn_tmps[0][:, 0:G], bf(("hn_tmp", 0, 0))), (hn_tmps[0][:, G:2 * G], bf(("hn_tmp", 0, 1)))]
                        for gi, wdw in enumerate((2, 4, 8, 16)):
                            nst = int(math.log2(wdw))
                            cur = uT[:, gi, :]
                            curB = bf("uT")
                            for k in range(nst):
                                sh = 1 << k
                                lo = 2 * sh - 1
                                dstb = pw[k % 2]
                                P.op("pool", _I("tensor_tensor", out=dstb[:, lo:528], in0=cur[:, lo:528], in1=cur[:, lo - sh:528 - sh], op=ALU.add),
                                     reads=[curB], writes=[bf(("pw", k % 2))])
                                cur = dstb
                                curB = bf(("pw", k % 2))
                            pt, ptB = pts4[gi]
                            P.op("dve", _I("scalar_tensor_tensor", out=pt, in0=cur[:, 16:528], scalar=1.0 / wdw, in1=uT[:, gi, 16:528],
                                           op0=ALU.mult, op1=ALU.subtract),
                                 reads=[curB, bf("uT")], writes=[ptB])
                            if g == 0:
                                P.op("dve", _I("tensor_tensor", out=ep[0][:, 0:16], in0=cur[:, 16:32], in1=invc[:, gi, :], op=ALU.mult),
                                     reads=[curB, bf("invc")], writes=[bf("ep0")])
                                P.op("dve", _I("tensor_tensor", out=pt[:, 0:16], in0=ep[0][:, 0:16], in1=uT[:, gi, 16:32], op=ALU.subtract),
                                     reads=[bf("ep0"), bf("uT")], writes=[ptB])
                        P.op("pool", _I("tensor_copy", out=uh[:], in_=uT[:, :, 512:528]), reads=[bf("uT")], writes=[bf("uh")])

                        def pool_part_b(i=i, pts4=pts4):
                            for gi in range(4):
                                pt, ptB = pts4[gi]
                                bi = alloc_bank()
                                P.op("pe", _I("matmul", banks[bi][:], lhsT=poolw_sb[:, gi, :], rhs=pt, start=True, stop=True),
                                     reads=[bf("poolw_sb"), ptB], writes=[bankB[bi]])
                                P.op("act", _I("activation", out=featT[:, 4 + gi, :], in_=banks[bi][:], func=AF.Copy, scale=pscale[:, i, gi:gi + 1]),
                                     reads=[bankB[bi], bf("pscale")], writes=[ftB[4 + gi]])

                        nk = 4 * g + 4
                        sqb = hn_tmps[1][:, 0:G]
                        sqB = bf(("hn_tmp", 1, 0))
                        pend_ep1 = []
                        pend_p2 = []
                        for hh in range(4):
                            if hh > 0:
                                load_bias(hh)
                            for m in range(2):
                                acc = []

                                def do_pv(j, ptile, ptB_, c0, hh=hh, acc=acc, nk=nk):
                                    if not acc:
                                        for _ in range(2):
                                            bi_ = alloc_bank()
                                            pinned.add(bi_)
                                            acc.append(bi_)
                                    P.op("pe", _I("matmul", banks[acc[0]][:, c0:G], lhsT=Vc[:, j, hh * 128:(hh + 1) * 128], rhs=ptile[:, c0:G],
                                                  start=(j == 0), stop=(j == nk - 1)),
                                         reads=[bf(("V", j // 4)), ptB_], writes=[bankB[acc[0]]])
                                    P.op("pe", _I("matmul", banks[acc[1]][:, c0:G], lhsT=onesb[:], rhs=ptile[:, c0:G],
                                                  start=(j == 0), stop=(j == nk - 1)),
                                         reads=[bf("onesb"), ptB_], writes=[bankB[acc[1]]])

                                pend = []
                                for j in range(nk):
                                    near = j >= 4 * g - 1
                                    ri = j - 4 * g + 1
                                    c0 = 128 * (j - 4 * g) if j > 4 * g else 0
                                    bi = alloc_bank()
                                    if near:
                                        P.op("pe", _I("matmul", banks[bi][:, c0:G], lhsT=idb[:], rhs=biasT[:, 4 - ri, c0:G], start=True, stop=False),
                                             reads=[bf("idb"), bf("biasT")], writes=[bankB[bi]])
                                    P.op("pe", _I("matmul", banks[bi][:, c0:G], lhsT=KT[:, hh, j * 128:(j + 1) * 128], rhs=qT4[:, hh, m, c0:G],
                                                  start=(not near), stop=True),
                                         reads=[bf(("KT", j // 4)), qTB[2 * hh + m]], writes=[bankB[bi]])
                                    pidx = ptc[0] % 4
                                    ptc[0] += 1
                                    ptile = PT[pidx]
                                    ptB_ = bf(("PT", pidx))
                                    if near:
                                        P.op("act", _I("activation", out=ptile[:, c0:G], in_=banks[bi][:, c0:G], func=AF.Exp, scale=0.125),
                                             reads=[bankB[bi]], writes=[ptB_])
                                    else:
                                        P.op("act", _I("activation", out=ptile[:, c0:G], in_=banks[bi][:, c0:G], func=AF.Exp, scale=0.125,
                                                       bias=b31[:, hh:hh + 1]),
                                             reads=[bankB[bi], bf("b31")], writes=[ptB_])
                                    pend.append((j, ptile, ptB_, c0))
                                    if j == 0 and pend_ep1:
                                        pend_ep1.pop(0)()
                                    for jj, fn in list(pend_p2):
                                        if j >= min(jj, nk - 1):
                                            pend_p2.remove((jj, fn))
                                            fn()
                                    if len(pend) > 2:
                                        do_pv(*pend.pop(0))
                                while pend:
                                    do_pv(*pend.pop(0))

                                def ep1(m=m, acc=acc):
                                    P.op("act", _I("activation", out=ep[m], in_=banks[acc[1]][:], func=AF.Ln),
                                         reads=[bankB[acc[1]]], writes=[bf(f"ep{m}")])
                                    P.op("act", _I("activation", out=ep[m], in_=ep[m], func=AF.Exp, scale=-1.0),
                                         reads=[bf(f"ep{m}")], writes=[bf(f"ep{m}")])
                                    P.op("dve", _I("tensor_tensor", out=ep[2 + m], in0=banks[acc[0]][:], in1=ep[m], op=ALU.mult),
                                         reads=[bankB[acc[0]], bf(f"ep{m}")], writes=[bf(f"ep{2 + m}")])
                                    for bi_ in acc:
                                        pinned.discard(bi_)

                                pend_ep1.append(ep1)

                            def part2a(hh=hh, i=i):
                                P.op("dve", _I("scalar_tensor_tensor", out=ep[2], in0=ep[3], scalar=neglam[:, i:i + 1], in1=ep[2],
                                               op0=ALU.mult, op1=ALU.add),
                                     reads=[bf("ep3"), bf("ep2"), bf("neglam")], writes=[bf("ep2")])

                            def part2b(hh=hh, i=i):
                                P.op("act", _I("activation", out=sqb, in_=ep[2], func=AF.Square), reads=[bf("ep2")], writes=[sqB])

                            def part2(hh=hh, i=i):
                                bi = alloc_bank()
                                P.op("pe", _I("matmul", banks[bi][:], lhsT=onesb[:], rhs=sqb, start=True, stop=True),
                                     reads=[bf("onesb"), sqB], writes=[bankB[bi]])
                                P.op("act", _I("activation", out=ep[0], in_=banks[bi][:], func=AF.Ln, scale=1.0 / 128, bias=EPS),
                                     reads=[bankB[bi]], writes=[bf("ep0")])
                                P.op("act", _I("activation", out=ep[0], in_=ep[0], func=AF.Exp, scale=-0.5),
                                     reads=[bf("ep0")], writes=[bf("ep0")])
                                P.op("dve", _I("scalar_tensor_tensor", out=featT[:, hh, :], in0=ep[2], scalar=gs[:, i:i + 1], in1=ep[0],
                                               op0=ALU.mult, op1=ALU.mult),
                                     reads=[bf("ep2"), bf("ep0"), bf("gs")], writes=[ftB[hh]])

                            pend_p2.extend([(1, part2a), (3, part2b), (7, part2)])
                        while pend_ep1:
                            pend_ep1.pop(0)()
                        tail_p2 = [fn for _, fn in pend_p2]
                        assert len(tail_p2) == 3

                        def deferred(tail_p2=tail_p2):
                            for fn in tail_p2:
                                fn()
                        pool_part_b()
                        deferred()
                        proj_add(l, "ab_w_out", hcur, hB, nxt_gain=norm_xattn_g[l:l + 1, :], corder=(4, 5, 6, 7, 0, 1, 2, 3))
                    else:
                        order = [0, 2, 4, 1, 3, 5]
                        slot_of = {}
                        for cc in range(8):
                            if cc % 4 == 0:
                                for nb in order[(cc // 4) * 3:(cc // 4) * 3 + 3]:
                                    slot_of[nb] = load_block(l, "conv_w_in", 0, nb)
                            col = (cc % 4) * 128
                            bb = featmajor_proj(slot_of[cc // 4], col, None, None)
                            featmajor_proj(slot_of[2 + cc // 4], col, ep[0][:], [bf("ep0")], eng="act")
                            bx = featmajor_proj(slot_of[4 + cc // 4], col, None, None)
                            zb = zbuf[cc % 2]
                            zB = bf(("zb", 0))
                            P.op("pool", _I("tensor_copy", out=zb[:, 0:2], in_=zh[:, cc, :]), reads=[bf("zh")], writes=[zB])
                            P.op("dve", _I("tensor_tensor", out=zb[:, 2:514], in0=banks[bx][:], in1=ep[0][:], op=ALU.mult),
                                 reads=[bankB[bx], bf("ep0")], writes=[zB])
                            P.op("pool", _I("tensor_copy", out=zh[:, cc, :], in_=zb[:, 512:514]), reads=[zB], writes=[bf("zh")])
                            P.op("pool", _I("tensor_scalar", out=ep[1][:], in0=zb[:, 0:512], scalar1=convw[:, i, 0, cc:cc + 1], scalar2=0.0,
                                            op0=ALU.mult, op1=ALU.add),
                                 reads=[zB, bf("convw")], writes=[bf("ep1")])
                            for k in (1, 2):
                                P.op("dve", _I("scalar_tensor_tensor", out=ep[1][:], in0=zb[:, k:k + 512], scalar=convw[:, i, k, cc:cc + 1],
                                               in1=ep[1][:], op0=ALU.mult, op1=ALU.add),
                                     reads=[zB, bf("convw"), bf("ep1")], writes=[bf("ep1")])
                            P.op("dve", _I("tensor_tensor", out=featT[:, cc, :], in0=banks[bb][:], in1=ep[1][:], op=ALU.mult),
                                 reads=[bankB[bb], bf("ep1")], writes=[ftB[cc]])
                        proj_add(l, "conv_w_out", hcur, hB, nxt_gain=norm_xattn_g[l:l + 1, :])

                    if nxt is not None:
                        load_h((unit + 1) % 2, s, nxt[0], nxt[1])
                    if s == 0:
                        cv_flush(cv_per_unit if g + 1 < NG else 1000)

                    sqs = [load_block(l, "xattn_wq", 0, nb) for nb in range(2)]
                    for oc in range(8):
                        featmajor_proj(sqs[oc // 4], (oc % 4) * 128, qT[:, oc, :], [qTB[oc]])
                    def x_scores(hh):
                        pts = []
                        for mt in range(2):
                            bi = alloc_bank()
                            for dc in range(2):
                                P.op("pe", _I("matmul", banks[bi][:], lhsT=xKT[:, 2 * hh + dc, mt * 128:(mt + 1) * 128], rhs=qT[:, 2 * hh + dc, :],
                                              start=(dc == 0), stop=(dc == 1)),
                                     reads=[bf("xKT"), qTB[2 * hh + dc]], writes=[bankB[bi]])
                            pidx = (2 * hh + mt) % 4
                            P.op("act", _I("activation", out=PT[pidx][:], in_=banks[bi][:], func=AF.Exp, scale=1.0 / 16),
                                 reads=[bankB[bi]], writes=[bf(("PT", pidx))])
                            pts.append((PT[pidx], bf(("PT", pidx))))
                        return pts

                    def x_pv(hh, pts):
                        bo = []
                        for dc in range(2):
                            bi = alloc_bank()
                            bo.append(bi)
                            for mt in range(2):
                                P.op("pe", _I("matmul", banks[bi][:], lhsT=xV[:, mt, hh * 256 + dc * 128: hh * 256 + (dc + 1) * 128],
                                              rhs=pts[mt][0][:], start=(mt == 0), stop=(mt == 1)),
                                     reads=[bf("xV"), pts[mt][1]], writes=[bankB[bi]])
                        bl = alloc_bank()
                        for mt in range(2):
                            P.op("pe", _I("matmul", banks[bl][:], lhsT=onesb[:], rhs=pts[mt][0][:], start=(mt == 0), stop=(mt == 1)),
                                 reads=[bf("onesb"), pts[mt][1]], writes=[bankB[bl]])
                        e = ep[hh % 2]
                        eB = bf(f"ep{hh % 2}")
                        P.op("act", _I("activation", out=e, in_=banks[bl][:], func=AF.Ln), reads=[bankB[bl]], writes=[eB])
                        P.op("act", _I("activation", out=e, in_=e, func=AF.Exp, scale=-1.0), reads=[eB], writes=[eB])
                        for dc in range(2):
                            P.op("dve", _I("tensor_tensor", out=featT[:, 2 * hh + dc, :], in0=banks[bo[dc]][:], in1=e, op=ALU.mult),
                                 reads=[bankB[bo[dc]], eB], writes=[ftB[2 * hh + dc]])

                    xp = x_scores(0)
                    for hh in range(4):
                        xn = x_scores(hh + 1) if hh + 1 < 4 else None
                        x_pv(hh, xp)
                        xp = xn
                    proj_add(l, "xattn_wo", hcur, hB, nxt_gain=norm_mlp_g[l:l + 1, :])

                    if last:
                        load_gain(final_norm_g[0:1, :], final=True)
                    if nxt is not None:
                        load_gain(norm_mix_g[nxt[0]:nxt[0] + 1, :])
                        gain_ready = True
                    else:
                        gain_ready = False
                    epc = 0
                    nub = (unit + 1) % 2
                    nh_ap = [hres2[nub][:, t, :] for t in range(4)]
                    nsl = {}
                    for fh in range(2):
                        for blk in range(4):
                            s1 = load_block(l, "mlp_w1", 0, fh * 4 + blk)
                            for fc in range(4):
                                bi = featmajor_proj(s1, fc * 128, None, None)
                                e = epc % 4
                                epc += 1
                                P.op("act", _I("activation", out=ep[e], in_=banks[bi][:], func=AF.Relu),
                                     reads=[bankB[bi]], writes=[bf(f"ep{e}")])
                                P.op("dve", _I("tensor_tensor", out=aT[:, blk * 4 + fc, :], in0=banks[bi][:], in1=ep[e], op=ALU.mult),
                                     reads=[bankB[bi], bf(f"ep{e}")], writes=[aTB[blk * 4 + fc]])
                        hide = (fh == 1 and nxt is not None)
                        if hide:
                            nsl[0] = norm_pre(nh_ap[0], hB2[nub][0])
                            nsl[1] = norm_pre(nh_ap[1], hB2[nub][1])
                        for dh in range(2):
                            s2 = [load_block(l, "mlp_w2", fh * 2 + kb, dh) for kb in range(2)]
                            for t in range(4):
                                bi = alloc_bank()
                                for kb in range(2):
                                    for c in range(8):
                                        P.op("pe", _I("matmul", banks[bi][:], lhsT=aT[:, kb * 8 + c, t * 128:(t + 1) * 128], rhs=wring[s2[kb]][:, c, :],
                                                      start=(kb == 0 and c == 0), stop=(kb == 1 and c == 7)),
                                             reads=[aTB[kb * 8 + c], wrB[s2[kb]]], writes=[bankB[bi]])
                                P.op("dve", _I("tensor_tensor", out=hcur[:, t, dh * 512:(dh + 1) * 512], in0=banks[bi][:],
                                               in1=hcur[:, t, dh * 512:(dh + 1) * 512], op=ALU.add),
                                     reads=[bankB[bi], hB[t]], writes=[hB[t]])
                                if hide and dh == 1 and t == 1:
                                    norm_tr(nsl[2], 2, hnT, bf("hnT"))
                                    norm_tr(nsl[3], 3, hnT, bf("hnT"))
                            if hide and dh == 0:
                                norm_tr(nsl[0], 0, hnT, bf("hnT"))
                                nsl[2] = norm_pre(nh_ap[2], hB2[nub][2])
                                norm_tr(nsl[1], 1, hnT, bf("hnT"))
                                nsl[3] = norm_pre(nh_ap[3], hB2[nub][3])
                        if hide:
                            norm_done = True

                    if last:
                        for t in range(4):
                            rs, sB = rms_rstd(hap[t], [hB[t]], D)
                            P.op("dve", _I("scalar_tensor_tensor", out=hcur[:, t, :], in0=hcur[:, t, :], scalar=rs, in1=gbf,
                                           op0=ALU.mult, op1=ALU.mult),
                                 reads=[hB[t], sB, bf("biasT")], writes=[hB[t]])
                    dst = y if last else hscr
                    final_ev = P.op("pool", _I("dma_start", out=dst[s, tok0:tok0 + G, :].rearrange("(t p) d -> p t d", p=128), in_=hcur[:]),
                                    reads=hB, writes=[bf(("hscr", s, g))], dma=s_hst[ub])
                    unit += 1
        P.wait_event("pool", final_ev)
        P.emit()
    return nc


_CACHE = {}


def _get_prog(S, NSEQ):
    key = (S, NSEQ)
    if key not in _CACHE:
        _CACHE[key] = build_program(S, NSEQ)
    return _CACHE[key]


def make_in_maps(inputs, ncores, nseq):
    consts = host_consts()
    maps = []
    for c in range(ncores):
        m = {}
        for k, v in inputs.items():
            v = np.asarray(v, dtype=np.float32)
            if k in ("x", "mem"):
                m[k] = np.ascontiguousarray(v[c * nseq:(c + 1) * nseq])
            elif k in ("mem_norm_g", "final_norm_g"):
                m[k] = np.ascontiguousarray(v.reshape(1, -1))
            else:
                m[k] = np.ascontiguousarray(v)
        m.update(consts)
        maps.append(m)
    return maps


def kernel(**inputs):
    x = np.asarray(inputs["x"])
    Bsz, S, _ = x.shape
    ncores = 8
    nseq = Bsz // ncores
    nc = _get_prog(S, nseq)
    maps = make_in_maps(inputs, ncores, nseq)
    res = run_bass_kernel_spmd(nc, maps, core_ids=list(range(ncores)))
    out = np.concatenate([np.asarray(r["y"]) for r in res.results], axis=0)
    return out.astype(np.float32)
```

```python
import math
import numpy as np
from contextlib import ExitStack
import concourse.bass as bass
import concourse.mybir as mybir
from concourse.bass_utils import run_bass_kernel_spmd

F32 = mybir.dt.float32
BF16 = mybir.dt.bfloat16
AF = mybir.ActivationFunctionType
ALU = mybir.AluOpType

D = 1024
NMEM = 256
EPS = 1e-6
G = 512
FW = 1152


class Buf:
    __slots__ = ("w", "r", "al", "stamp")

    def __init__(self):
        self.w = None
        self.r = {}
        self.al = ()
        self.stamp = 0


class Op:
    __slots__ = ("fn", "waits", "signal", "dma")

    def __init__(self, fn, waits):
        self.fn = fn
        self.waits = waits
        self.signal = False
        self.dma = None


class Prog:
    ENGS = ("pe", "act", "dve", "pool", "sp")

    def __init__(self, nc, es):
        self.nc = nc
        self.es = es
        self.ops = {e: [] for e in self.ENGS}
        self.ncomp = {e: 0 for e in self.ENGS}
        self.comp_idx = {e: [] for e in self.ENGS}
        self.seen = {e: {} for e in self.ENGS}
        self.sems = {e: es.enter_context(nc.semaphore("s_" + e)) for e in self.ENGS}
        self.dma_cnt = {}
        self.gctr = 0

    def dma_sem(self, name):
        s = self.es.enter_context(self.nc.semaphore(name))
        self.dma_cnt[id(s)] = [s, 0]
        return s

    def _need(self, eng, ev, waits, same_ok):
        if ev is None:
            return
        if ev[0] == 'c' and ev[1] == eng and not same_ok and eng == "pe":
            return
        key = (ev[0], ev[1])
        if self.seen[eng].get(key, -1) >= ev[2]:
            return
        self.seen[eng][key] = ev[2]
        waits.append(ev)

    def op(self, eng, fn, reads=(), writes=(), dma=None):
        waits = []
        for b in reads:
            self._need(eng, b.w, waits, True)
        for b in writes:
            self._need(eng, b.w, waits, False)
            for ev in b.r.values():
                self._need(eng, ev, waits, False)
            for a in b.al:
                self._need(eng, a.w, waits, True)
                for ev in a.r.values():
                    self._need(eng, ev, waits, True)
        o = Op(fn, waits)
        if dma is not None:
            ent = self.dma_cnt[id(dma)]
            ent[1] += 16
            o.dma = dma
            ev = ('d', id(dma), ent[1])
        else:
            seq = self.ncomp[eng]
            self.ncomp[eng] += 1
            self.comp_idx[eng].append(len(self.ops[eng]))
            ev = ('c', eng, seq)
        self.ops[eng].append(o)
        self.gctr += 1
        for b in reads:
            b.r[(ev[0], ev[1])] = ev
            b.stamp = self.gctr
        for b in writes:
            b.w = ev
            b.r = {}
            b.stamp = self.gctr
        return ev

    def wait_event(self, eng, ev):
        waits = []
        self._need(eng, ev, waits, True)
        if waits:
            self.ops[eng].append(Op(None, waits))

    def emit(self):
        nc = self.nc
        for e in self.ENGS:
            for o in self.ops[e]:
                for ev in o.waits:
                    if ev[0] == 'c':
                        self.ops[ev[1]][self.comp_idx[ev[1]][ev[2]]].signal = True
        cnt = {}
        for e in self.ENGS:
            c = 0
            arr = []
            for idx in self.comp_idx[e]:
                if self.ops[e][idx].signal:
                    c += 1
                arr.append(c)
            cnt[e] = arr
        sem_by_id = {k: v[0] for k, v in self.dma_cnt.items()}

        def run(e, h):
            for o in self.ops[e]:
                for ev in o.waits:
                    if ev[0] == 'c':
                        h.wait_ge(self.sems[ev[1]], cnt[ev[1]][ev[2]])
                    else:
                        h.wait_ge(sem_by_id[ev[1]], ev[2])
                if o.fn is None:
                    continue
                ins = o.fn(h)
                if o.dma is not None:
                    ins.then_inc(o.dma, 16)
                elif o.signal:
                    ins.then_inc(self.sems[e], 1)

        with nc.Block() as block:
            @block.tensor
            def _(h):
                run("pe", h)

            @block.scalar
            def _(h):
                run("act", h)

            @block.vector
            def _(h):
                run("dve", h)

            @block.gpsimd
            def _(h):
                run("pool", h)

            @block.sync
            def _(h):
                run("sp", h)


def t5_bucket_np(n):
    n = np.asarray(n, dtype=np.int64)
    nf = np.maximum(n, 1).astype(np.float32)
    large = 16 + (np.log(nf / np.float32(16)) / np.float32(math.log(128 / 16)) * np.float32(16)).astype(np.int32)
    large = np.minimum(large, 31)
    return np.where(n < 16, n, large)


def host_consts():
    ident = np.eye(128, dtype=np.float32)
    onehot = np.zeros((33, FW), np.float32)
    for i in range(FW):
        d = i - 511
        if d < 0:
            onehot[32, i] = -240000.0
        else:
            onehot[int(t5_bucket_np(d)), i] = 8.0
    invc = np.zeros((128, 4, 16), np.float32)
    for gi, w in enumerate((2, 4, 8, 16)):
        for t in range(16):
            invc[:, gi, t] = 1.0 / min(t + 1, w)
    return {"c_ident": ident, "c_onehot": onehot, "c_invc": invc}


WSPEC = [
    ("ab_w_in", "even", 1024, 2048), ("ab_w_out", "even", 1024, 1024),
    ("conv_w_in", "odd", 1024, 3072), ("conv_w_out", "odd", 1024, 1024),
    ("xattn_wkv", "all", 1024, 2048), ("xattn_wq", "all", 1024, 1024), ("xattn_wo", "all", 1024, 1024),
    ("mlp_w1", "all", 1024, 4096), ("mlp_w2", "all", 4096, 1024),
]


def _I(name, *a, **k):
    return lambda h: getattr(h, name)(*a, **k)


def build_program(S, NSEQ, DEPTH=4):
    NG = S // G
    NT = S // 128
    nc = bass.Bass("TRN2", target_bir_lowering=False)
    n_even = (DEPTH + 1) // 2
    n_odd = DEPTH // 2

    def din(name, shape):
        return nc.dram_tensor(name, list(shape), F32, kind="ExternalInput")

    x = din("x", [NSEQ, S, D]).ap()
    mem = din("mem", [NSEQ, NMEM, D]).ap()
    rel_bias = din("rel_bias", [32, 4]).ap()
    mem_norm_g = din("mem_norm_g", [1, D]).ap()
    norm_mix_g = din("norm_mix_g", [DEPTH, D]).ap()
    norm_xattn_g = din("norm_xattn_g", [DEPTH, D]).ap()
    norm_mlp_g = din("norm_mlp_g", [DEPTH, D]).ap()
    final_norm_g = din("final_norm_g", [1, D]).ap()
    W = {}
    for name, kind, K, N in WSPEC:
        n = {"even": n_even, "odd": n_odd, "all": DEPTH}[kind]
        W[name] = din(name, [max(n, 1), K, N]).ap()
    lam_in = [din(nm, [max(n_even, 1), 64]).ap() for nm in ("lambda_q1", "lambda_k1", "lambda_q2", "lambda_k2")]
    subln_t = din("subln_g", [max(n_even, 1), 128])
    pool_w = din("pool_w", [max(n_even, 1), 4, 128, 128]).ap()
    pscale_t = din("pool_scale", [max(n_even, 1), 512])
    convw_t = din("conv_w", [max(n_odd, 1), 3, D])
    c_ident = din("c_ident", [128, 128]).ap()
    c_onehot = din("c_onehot", [33, FW]).ap()
    c_invc = din("c_invc", [128, 4, 16]).ap()
    y = nc.dram_tensor("y", [NSEQ, S, D], F32, kind="ExternalOutput").ap()

    def col_ap(t, off):
        return bass.AP(t, off, [[1, 128], [1, 1]])

    blk_ids = {}
    nblk = 0
    for l in range(DEPTH):
        for name, kind, K, N in WSPEC:
            if (kind == "even" and l % 2 == 1) or (kind == "odd" and l % 2 == 0):
                continue
            for kb in range(K // 1024):
                for nb in range(N // 512):
                    blk_ids[(l, name, kb, nb)] = nblk
                    nblk += 1
    wblk = nc.dram_tensor("wblk", [nblk, 128, 4096], BF16).ap()
    poolw_bf = nc.dram_tensor("poolw_bf", [max(n_even, 1), 128, 512], BF16).ap()
    hscr = nc.dram_tensor("hscr", [NSEQ, S, D], F32).ap()
    Fd_t = nc.dram_tensor("Fd", [4, FW], BF16)
    Fd = Fd_t.ap()
    FR = 130
    Fd2_t = nc.dram_tensor("Fd2", [4, FR, FW], BF16)
    Fd2 = Fd2_t.ap()

    with ExitStack() as es:
        P = Prog(nc, es)

        def sb(name, shape, dt):
            return es.enter_context(nc.sbuf_tensor(name, list(shape), dt))

        hres2 = [sb(f"hres{i}", [128, 4, D], F32) for i in range(2)]
        hnT = sb("hnT", [128, 8, G], BF16)
        hn_tmps = [sb(f"hn_tmp{i}", [128, D], BF16) for i in range(2)]
        wring = [sb(f"wr{i}", [128, 8, 512], BF16) for i in range(4)]
        memT = sb("memT", [128, 8, NMEM], BF16)
        xKT = sb("xKT", [128, 8, NMEM], BF16)
        xV = sb("xV", [128, 2, D], BF16)
        gb = sb("gb", [128, D], F32)
        KT = sb("KT", [128, 4, S], BF16)
        Vc = sb("Vc", [128, NT, 512], BF16)
        featT = sb("featT", [128, 8, G], BF16)
        qT = sb("qT", [128, 8, G], BF16)
        aT = sb("aT", [128, 16, G], BF16)
        PT = [sb(f"PT{i}", [128, G], BF16) for i in range(4)]
        epall = sb("epall", [128, 4 * G], F32)
        ep = [epall[:, i * G:(i + 1) * G] for i in range(4)]
        aT32 = aT[:].rearrange("p c n -> p (c n)").bitcast(F32)
        uT = aT32[:, 0:2112].rearrange("p (g n) -> p g n", g=4)
        pw = [aT32[:, 2112 + i * 528:2112 + (i + 1) * 528] for i in range(2)]
        pTb = [aT32[:, 3168 + i * 256:3168 + (i + 1) * 256].bitcast(BF16) for i in range(2)]
        biasT = sb("biasT", [128, 5, G], BF16)
        zbuf = [aT32[:, 0:514]] * 2
        zh = sb("zh", [128, 8, 2], F32)
        uh = sb("uh", [128, 4, 16], F32)
        idf = pw[1][:, 0:128]
        idb = sb("idb", [128, 128], BF16)
        onesb = sb("onesb", [128, 128], BF16)
        stt = sb("stt", [128, 4, 4], F32)
        b31 = sb("b31", [128, 4], F32)
        lamt = pw[0][:, 0:512].rearrange("p (i k d) -> p i k d", i=2, k=4)
        lams = sb("lams", [128, 8], F32)
        neglam = sb("neglam", [128, 2], F32)
        gs = sb("gs", [128, 2], F32)
        pscale = sb("pscale", [128, 2, 4], F32)
        poolw_sb = sb("poolw_sb", [128, 4, 128], BF16)
        convw = sb("convw", [128, 2, 3, 8], F32)
        invc = sb("invc", [128, 4, 16], F32)
        rb33 = sb("rb33", [33, 4], F32)
        oh = epall[0:33, 0:FW]
        Fsb = featT[:].rearrange("p c n -> p (c n)")[0:4, 0:FW]

        banks = [es.enter_context(nc.psum_tensor(f"bk{i}", [128, 512], F32)) for i in range(8)]
        bankB = [Buf() for _ in range(8)]
        pinned = set()
        rot = [0]

        def alloc_bank():
            best = None
            for k in range(8):
                i = (rot[0] + k) % 8
                if i in pinned:
                    continue
                if best is None or bankB[i].stamp < bankB[best].stamp:
                    best = i
            rot[0] = best + 1
            bankB[best].stamp = P.gctr + 1
            return best

        Bd = {}

        def bf(name):
            if name not in Bd:
                Bd[name] = Buf()
            return Bd[name]

        _al = [bf("uT"), bf(("pw", 0)), bf(("pw", 1)), bf(("pTb", 0)), bf(("pTb", 1)), bf(("zb", 0))]
        aTB = [bf(("aT", k)) for k in range(16)]
        ftB = [bf(("featT", k)) for k in range(8)]
        qTB = [bf(("qT", k)) for k in range(8)]
        for _a in aTB:
            _a.al = tuple(_al)
        for _b in _al:
            _b.al = tuple(aTB)
        bf("uT").al = tuple(aTB) + (bf(("zb", 0)),)
        bf(("zb", 0)).al = tuple(aTB) + (bf("uT"),)
        gbf = biasT[:].rearrange("p r n -> p (r n)").bitcast(F32)[:, 0:D]
        hB2 = [[Buf() for _ in range(4)] for _ in range(2)]
        junk = qT[:].rearrange("p c n -> p (c n)")[:, 0:D]

        s_misc = P.dma_sem("d_misc")
        n_extra = min(4, NT // 8)
        s_wr = [P.dma_sem(f"d_wr{i}") for i in range(4 + n_extra)]
        s_h = [P.dma_sem("d_h0"), P.dma_sem("d_h1")]
        s_hst = [P.dma_sem("d_hst0"), P.dma_sem("d_hst1")]
        s_gb = P.dma_sem("d_gb")
        s_bias = P.dma_sem("d_bias")
        s_f = P.dma_sem("d_f")
        s_pw = P.dma_sem("d_pw")
        s_f2 = P.dma_sem("d_f2")

        evac_ctr = [0]
        ptc = [0]

        def evac(bi, src_ap, dst_ap, dst_bufs, eng=None):
            if eng is None:
                eng = "act" if evac_ctr[0] % 2 == 0 else "dve"
                evac_ctr[0] += 1
            if eng == "act":
                P.op("act", _I("copy", out=dst_ap, in_=src_ap), reads=[bankB[bi]], writes=dst_bufs)
            else:
                P.op("dve", _I("tensor_copy", out=dst_ap, in_=src_ap), reads=[bankB[bi]], writes=dst_bufs)

        setup_bufs = []

        def sdma(out_ap, in_ap, bname):
            P.op("sp", _I("dma_start", out=out_ap, in_=in_ap), writes=[bf(bname)], dma=s_misc)
            if bf(bname) not in setup_bufs:
                setup_bufs.append(bf(bname))

        sdma(idf, c_ident, ("pw", 1))
        sdma(invc[:], c_invc, "invc")
        sdma(b31[:], rel_bias[31:32, :].partition_broadcast(128), "b31")
        sdma(oh, c_onehot, "ep0"); sdma(oh, c_onehot, "ep1") if False else None; setup_bufs.extend([bf("ep1"), bf("ep2")])
        sdma(rb33[0:32, :], rel_bias, "rb33")
        for i in range(n_even):
            for k in range(4):
                sdma(lamt[:, i, k, :], lam_in[k][i:i + 1, :].partition_broadcast(128), ("pw", 0))
            sdma(gs[:, i:i + 1], col_ap(subln_t, i * 128), "gs")
            for gi in range(4):
                sdma(pscale[:, i, gi:gi + 1], col_ap(pscale_t, i * 512 + gi * 128), "pscale")
        for i in range(n_odd):
            for k in range(3):
                for c in range(8):
                    sdma(convw[:, i, k, c:c + 1], col_ap(convw_t, (i * 3 + k) * D + c * 128), "convw")
        fence = ('d', id(s_misc), P.dma_cnt[id(s_misc)][1])
        for b in setup_bufs:
            b.w = fence

        P.op("dve", _I("tensor_copy", out=idb[:], in_=idf), reads=[bf(("pw", 1))], writes=[bf("idb")])
        P.op("dve", _I("memset", onesb[:], 1.0), writes=[bf("onesb")])
        P.op("dve", _I("memset", rb33[32:33, :], 1.0), reads=[bf("rb33")], writes=[bf("rb33x")])
        for j0 in range(0, FW, 384):
            bi = alloc_bank()
            P.op("pe", _I("matmul", banks[bi][0:4, 0:384], lhsT=rb33[:, :], rhs=oh[:, j0:j0 + 384], start=True, stop=True),
                 reads=[bf("rb33"), bf("rb33x"), bf("ep0"), bf("ep1"), bf("ep2")], writes=[bankB[bi]])
            evac(bi, banks[bi][0:4, 0:384], Fsb[:, j0:j0 + 384], ftB, eng="dve")
        P.op("sp", _I("dma_start", out=Fd, in_=Fsb), reads=ftB, writes=[bf("Fd0")], dma=s_f)
        for hh in range(4):
            P.op("sp", _I("dma_start", out=Fd2[hh], in_=Fd[hh:hh + 1, :].partition_broadcast(FR)), reads=[bf("Fd0")], writes=[bf("Fd")],
                 dma=s_f2)
        bf("Fd").w = ('d', id(s_f2), P.dma_cnt[id(s_f2)][1])
        for i in range(n_even):
            lam_init = 0.8 - 0.6 * math.exp(-0.3 * (2 * i))
            for m in range(2):
                P.op("dve", _I("tensor_tensor", out=lamt[:, i, 2 * m, :], in0=lamt[:, i, 2 * m, :], in1=lamt[:, i, 2 * m + 1, :], op=ALU.mult),
                     reads=[bf(("pw", 0))], writes=[bf(("pw", 0))])
                P.op("act", _I("activation", out=lamt[:, i, 2 * m + 1, :], in_=lamt[:, i, 2 * m, :], func=AF.Identity,
                               accum_out=lams[:, m:m + 1]),
                     reads=[bf(("pw", 0))], writes=[bf(("pw", 0)), bf("lams")])
                P.op("act", _I("activation", out=lams[:, 2 + m:3 + m], in_=lams[:, m:m + 1], func=AF.Exp),
                     reads=[bf("lams")], writes=[bf("lams")])
            P.op("dve", _I("tensor_tensor", out=lams[:, 4:5], in0=lams[:, 3:4], in1=lams[:, 2:3], op=ALU.subtract),
                 reads=[bf("lams")], writes=[bf("lams")])
            P.op("dve", _I("tensor_scalar", out=neglam[:, i:i + 1], in0=lams[:, 4:5], scalar1=-lam_init, scalar2=None, op0=ALU.add),
                 reads=[bf("lams")], writes=[bf("neglam")])
            P.op("dve", _I("tensor_scalar", out=gs[:, i:i + 1], in0=gs[:, i:i + 1], scalar1=1.0 - lam_init, scalar2=None, op0=ALU.mult),
                 reads=[bf("gs")], writes=[bf("gs")])

        wconv_buf = {}
        NCV = 64
        s_cv = [P.dma_sem(f"d_cv{k}") for k in range(NCV)]
        cv_evs = []

        cv_pending = []

        def cv_dma(fn, b, lazy=True):
            if lazy:
                cv_pending.append((fn, b))
                return
            k = len(cv_evs)
            if k >= NCV:
                P.wait_event("pool", cv_evs[k - NCV])
            cv_evs.append(P.op("pool", fn, writes=[b], dma=s_cv[k % NCV]))

        CV_ORDER = ["xattn_wkv", "ab_w_in", "conv_w_in", "ab_w_out", "conv_w_out", "xattn_wq", "xattn_wo", "mlp_w1", "mlp_w2"]
        wspec_sorted = sorted(WSPEC, key=lambda w: CV_ORDER.index(w[0]))
        for l in range(DEPTH):
            if l % 2 == 0:
                wconv_buf[(l, "pool_w")] = Buf()
            for name, kind, K, N in WSPEC:
                if (kind == "even" and l % 2 == 1) or (kind == "odd" and l % 2 == 0):
                    continue
                for kb in range(K // 1024):
                    for nb in range(N // 512):
                        wconv_buf[blk_ids[(l, name, kb, nb)]] = Buf()

        def convert_layer(l):
            i = l // 2
            if l % 2 == 0:
                pb = wconv_buf[(l, "pool_w")]
                cv_dma(_I("dma_start", out=poolw_bf[i].rearrange("c (g d) -> c g d", g=4), in_=pool_w[i].rearrange("g c d -> c g d")), pb)
            for name, kind, K, N in wspec_sorted:
                if (kind == "even" and l % 2 == 1) or (kind == "odd" and l % 2 == 0):
                    continue
                li = l if kind == "all" else i
                for kb in range(K // 1024):
                    for nb in range(N // 512):
                        bid = blk_ids[(l, name, kb, nb)]
                        b = wconv_buf[bid]
                        src = W[name][li, kb * 1024:(kb + 1) * 1024, nb * 512:(nb + 1) * 512].rearrange("(c p) n -> p c n", p=128)
                        dst = wblk[bid].rearrange("p (c n) -> p c n", c=8)
                        cv_dma(_I("dma_start", out=dst, in_=src), b)

        def cv_flush(n):
            for _ in range(n):
                if cv_pending:
                    cv_dma(*cv_pending.pop(0), lazy=False)

        convert_layer(0)
        cv_flush(1000)

        wr_ctr = [0]
        wrB = [Buf() for _ in range(4 + n_extra)]
        for k in range(n_extra):
            wring.append(Vc[:, 8 * k:8 * k + 8, :])
            vb = tuple(bf(("V", gg)) for gg in (2 * k, 2 * k + 1) if gg < NG)
            wrB[4 + k].al = vb
            for b_ in vb:
                b_.al = (wrB[4 + k],)

        def load_block(l, name, kb, nb):
            bid = blk_ids[(l, name, kb, nb)]
            slot = wr_ctr[0] % (4 + n_extra if l % 2 == 1 else 4)
            wr_ctr[0] += 1
            P.op("sp", _I("dma_start", out=wring[slot][:], in_=wblk[bid].rearrange("p (c n) -> p c n", c=8)),
                 reads=[wconv_buf[bid]], writes=[wrB[slot]], dma=s_wr[slot])
            return slot

        def load_gain(row_ap, final=False):
            if final:
                P.op("pool", _I("dma_start", out=gbf, in_=row_ap.partition_broadcast(128)), writes=[bf("biasT")], dma=s_bias)
            else:
                P.op("pool", _I("dma_start", out=gb[:], in_=row_ap.partition_broadcast(128)), writes=[bf("gb")], dma=s_gb)

        st_ctr = [0]

        def rms_rstd(src_ap, src_bufs, n):
            k = st_ctr[0] % 4
            st_ctr[0] += 1
            sk = stt[:, k, :]
            sB = bf(("st", k))
            P.op("act", _I("activation", out=junk[:, 0:n], in_=src_ap, func=AF.Square, accum_out=sk[:, 0:1]),
                 reads=src_bufs, writes=[qTB[0], qTB[1], sB])
            P.op("act", _I("activation", out=sk[:, 1:2], in_=sk[:, 0:1], func=AF.Ln, scale=1.0 / n, bias=EPS),
                 reads=[sB], writes=[sB])
            P.op("act", _I("activation", out=sk[:, 2:3], in_=sk[:, 1:2], func=AF.Exp, scale=-0.5),
                 reads=[sB], writes=[sB])
            return sk[:, 2:3], sB

        nrm_ctr = [0]

        def norm_pre(src_ap, src_buf):
            slot = nrm_ctr[0] % 2
            nrm_ctr[0] += 1
            rs, sB = rms_rstd(src_ap, [src_buf], D)
            for hf in range(2):
                P.op("dve", _I("scalar_tensor_tensor", out=hn_tmps[slot][:, hf * 512:(hf + 1) * 512], in0=src_ap[:, hf * 512:(hf + 1) * 512],
                               scalar=rs, in1=gb[:, hf * 512:(hf + 1) * 512], op0=ALU.mult, op1=ALU.mult),
                     reads=[src_buf, sB, bf("gb")], writes=[bf(("hn_tmp", slot, hf))])
            return slot

        def norm_tr(slot, t, dstT, dst_buf):
            bi = alloc_bank()
            pv = banks[bi][:].bitcast(BF16).rearrange("p (c n) -> p c n", c=8)
            for c in range(8):
                P.op("pe", _I("transpose", out=pv[:, c, :], in_=hn_tmps[slot][:, c * 128:(c + 1) * 128], identity=idb[:]),
                     reads=[bf(("hn_tmp", slot, c // 4)), bf("idb")], writes=[bankB[bi]])
            evac(bi, pv, dstT[:, :, t * 128:(t + 1) * 128], [dst_buf])

        def norm_T(src_aps, src_bufs, dstT, dst_buf, ntile):
            slots = {}
            for t in range(ntile):
                slots[t] = norm_pre(src_aps[t], src_bufs[t])
                if t >= 1:
                    norm_tr(slots[t - 1], t - 1, dstT, dst_buf)
            norm_tr(slots[ntile - 1], ntile - 1, dstT, dst_buf)

        def proj_add(l, name, hcur, hB, nxt_gain=None, corder=(0, 1, 2, 3, 4, 5, 6, 7)):
            slots = [load_block(l, name, 0, nb) for nb in range(2)]
            if nxt_gain is not None:
                load_gain(nxt_gain)
            nslot = {}
            for t in range(4):
                for nb in range(2):
                    bi = alloc_bank()
                    for ci, c in enumerate(corder):
                        P.op("pe", _I("matmul", banks[bi][:], lhsT=featT[:, c, t * 128:(t + 1) * 128], rhs=wring[slots[nb]][:, c, :],
                                      start=(ci == 0), stop=(ci == 7)),
                             reads=[ftB[c], wrB[slots[nb]]], writes=[bankB[bi]])
                    P.op("dve", _I("tensor_tensor", out=hcur[:, t, nb * 512:(nb + 1) * 512], in0=banks[bi][:],
                                   in1=hcur[:, t, nb * 512:(nb + 1) * 512], op=ALU.add),
                         reads=[bankB[bi], hB[t]], writes=[hB[t]])
                if nxt_gain is not None:
                    nslot[t] = norm_pre(hcur[:, t, :], hB[t])
                    if t >= 1:
                        norm_tr(nslot[t - 1], t - 1, hnT, bf("hnT"))
            if nxt_gain is not None:
                norm_tr(nslot[3], 3, hnT, bf("hnT"))

        def featmajor_proj(slot, col0, dst_ap, dst_bufs, rhs_ap=None, rhs_buf=None, n=G, eng=None):
            bi = alloc_bank()
            rhs_ap = hnT if rhs_ap is None else rhs_ap
            rhs_buf = bf("hnT") if rhs_buf is None else rhs_buf
            for c in range(8):
                P.op("pe", _I("matmul", banks[bi][:, 0:n], lhsT=wring[slot][:, c, col0:col0 + 128], rhs=rhs_ap[:, c, 0:n],
                              start=(c == 0), stop=(c == 7)),
                     reads=[wrB[slot], rhs_buf], writes=[bankB[bi]])
            if dst_ap is not None:
                evac(bi, banks[bi][:, 0:n], dst_ap, dst_bufs, eng=eng)
            return bi

        qT4 = qT[:].rearrange("p (h m) n -> p h m n", m=2)
        def load_h(ub, s, l, g):
            srcT = x if l == 0 else hscr
            P.op("pool", _I("dma_start", out=hres2[ub][:], in_=srcT[s, g * G:(g + 1) * G, :].rearrange("(t p) d -> p t d", p=128)),
                 reads=[bf(("hscr", s, g))], writes=hB2[ub], dma=s_h[ub])

        final_ev = None
        unit = 0
        gain_ready = False
        for s in range(NSEQ):
            ub0 = unit % 2
            load_gain(mem_norm_g[0:1, :])
            P.op("pool", _I("dma_start", out=hres2[ub0][:, 0:2, :], in_=mem[s].rearrange("(t p) d -> p t d", p=128)),
                 writes=hB2[ub0][0:2], dma=s_h[ub0])
            norm_T([hres2[ub0][:, t, :] for t in range(2)], hB2[ub0][0:2], memT, bf("memT"), 2)
            load_h(ub0, s, 0, 0)
            gain_ready = False
            norm_done = False

            for l in range(DEPTH):
                i = l // 2
                even = (l % 2 == 0)
                last = (l == DEPTH - 1)
                slots = [load_block(l, "xattn_wkv", 0, nb) for nb in range(4)]
                for kc in range(8):
                    featmajor_proj(slots[kc // 4], (kc % 4) * 128, xKT[:, kc, :], [bf("xKT")], rhs_ap=memT, rhs_buf=bf("memT"), n=NMEM)
                for mt in range(2):
                    for half in range(2):
                        bi = alloc_bank()
                        for c in range(8):
                            P.op("pe", _I("matmul", banks[bi][:], lhsT=memT[:, c, mt * 128:(mt + 1) * 128], rhs=wring[slots[2 + half]][:, c, :],
                                          start=(c == 0), stop=(c == 7)),
                                 reads=[bf("memT"), wrB[slots[2 + half]]], writes=[bankB[bi]])
                        evac(bi, banks[bi][:], xV[:, mt, half * 512:(half + 1) * 512], [bf("xV")])
                if even:
                    P.op("sp", _I("dma_start", out=poolw_sb[:], in_=poolw_bf[i].rearrange("c (g d) -> c g d", g=4)),
                         reads=[wconv_buf[(l, "pool_w")]], writes=[bf("poolw_sb")], dma=s_pw)
                    P.op("pool", _I("memset", uh[:], 0.0), writes=[bf("uh")])
                else:
                    P.op("pool", _I("memset", zh[:], 0.0), writes=[bf("zh")])
                if s == 0 and l + 1 < DEPTH:
                    convert_layer(l + 1)
                    cv_per_unit = -(-len(cv_pending) // NG)

                for g in range(NG):
                    tok0 = g * G
                    ub = unit % 2
                    hcur = hres2[ub]
                    hB = hB2[ub]
                    hap = [hcur[:, t, :] for t in range(4)]
                    if g + 1 < NG:
                        nxt = (l, g + 1)
                    elif l + 1 < DEPTH:
                        nxt = (l + 1, 0)
                    else:
                        nxt = None

                    if not norm_done:
                        if not gain_ready:
                            load_gain(norm_mix_g[l:l + 1, :])
                        norm_T(hap, hB, hnT, bf("hnT"), 4)
                    norm_done = False
                    if even:
                        sq = load_block(l, "ab_w_in", 0, 0)
                        sk = load_block(l, "ab_w_in", 0, 1)
                        sv = load_block(l, "ab_w_in", 0, 2)
                        su = load_block(l, "ab_w_in", 0, 3)
                        P.op("pool", _I("memset", qT4[64:128, :, 0, :], 0.0), writes=qTB)
                        P.op("pool", _I("memset", qT4[0:64, :, 1, :], 0.0), writes=qTB)
                        for hh in range(4):
                            bi = featmajor_proj(sq, hh * 128, None, None)
                            evac(bi, banks[bi][0:64, :], qT4[0:64, hh, 0, :], [qTB[2 * hh]])
                            evac(bi, banks[bi][64:128, :], qT4[64:128, hh, 1, :], [qTB[2 * hh + 1]])
                        for hh in range(4):
                            featmajor_proj(sk, hh * 128, KT[:, hh, tok0:tok0 + G], [bf(("KT", g))])
                        for t in range(4):
                            bi = alloc_bank()
                            for c in range(8):
                                P.op("pe", _I("matmul", banks[bi][:], lhsT=hnT[:, c, t * 128:(t + 1) * 128], rhs=wring[sv][:, c, :],
                                              start=(c == 0), stop=(c == 7)),
                                     reads=[bf("hnT"), wrB[sv]], writes=[bankB[bi]])
                            evac(bi, banks[bi][:], Vc[:, g * 4 + t, :], [bf(("V", g))])
                        for gi in range(4):
                            featmajor_proj(su, gi * 128, uT[:, gi, 16:528], [bf("uT")])
                        def load_bias(hh):
                            src_ap = bass.AP(Fd2_t, hh * FR * FW + 127, [[FW - 1, 128], [128, 5], [1, G]])
                            P.op("pool", _I("dma_start", out=biasT[:], in_=src_ap), reads=[bf("Fd")], writes=[bf("biasT")], dma=s_bias)

                        load_bias(0)
                        P.op("pool", _I("tensor_copy", out=uT[:, :, 0:16], in_=uh[:]), reads=[bf("uh")], writes=[bf("uT")])
                        pts4 = [(pTb[0], bf(("pTb", 0))), (pTb[1], bf(("pTb", 1))),
                                (hn_tmps[0][:, 0:G], bf(("hn_tmp", 0, 0))), (hn_tmps[0][:, G:2 * G], bf(("hn_tmp", 0, 1)))]
                        for gi, wdw in enumerate((2, 4, 8, 16)):
                            nst = int(math.log2(wdw))
                            cur = uT[:, gi, :]
                            curB = bf("uT")
                            for k in range(nst):
                                sh = 1 << k
                                lo = 2 * sh - 1
                                dstb = pw[k % 2]
                                P.op("pool", _I("tensor_tensor", out=dstb[:, lo:528], in0=cur[:, lo:528], in1=cur[:, lo - sh:528 - sh], op=ALU.add),
                                     reads=[curB], writes=[bf(("pw", k % 2))])
                                cur = dstb
                                curB = bf(("pw", k % 2))
                            pt, ptB = pts4[gi]
                            P.op("dve", _I("scalar_tensor_tensor", out=pt, in0=cur[:, 16:528], scalar=1.0 / wdw, in1=uT[:, gi, 16:528],
                                           op0=ALU.mult, op1=ALU.subtract),
                                 reads=[curB, bf("uT")], writes=[ptB])
                            if g == 0:
                                P.op("dve", _I("tensor_tensor", out=ep[0][:, 0:16], in0=cur[:, 16:32], in1=invc[:, gi, :], op=ALU.mult),
                                     reads=[curB, bf("invc")], writes=[bf("ep0")])
                                P.op("dve", _I("tensor_tensor", out=pt[:, 0:16], in0=ep[0][:, 0:16], in1=uT[:, gi, 16:32], op=ALU.subtract),
                                     reads=[bf("ep0"), bf("uT")], writes=[ptB])
                        P.op("pool", _I("tensor_copy", out=uh[:], in_=uT[:, :, 512:528]), reads=[bf("uT")], writes=[bf("uh")])

                        def pool_part_b(i=i, pts4=pts4):
                            for gi in range(4):
                                pt, ptB = pts4[gi]
                                bi = alloc_bank()
                                P.op("pe", _I("matmul", banks[bi][:], lhsT=poolw_sb[:, gi, :], rhs=pt, start=True, stop=True),
                                     reads=[bf("poolw_sb"), ptB], writes=[bankB[bi]])
                                P.op("act", _I("activation", out=featT[:, 4 + gi, :], in_=banks[bi][:], func=AF.Copy, scale=pscale[:, i, gi:gi + 1]),
                                     reads=[bankB[bi], bf("pscale")], writes=[ftB[4 + gi]])

                        nk = 4 * g + 4
                        sqb = hn_tmps[1][:, 0:G]
                        sqB = bf(("hn_tmp", 1, 0))
                        pend_p2 = []
                        pend = []

                        def pop_tile():
                            fn, j_, pt_, pb_, c_, ep_ = pend.pop(0)
                            fn(j_, pt_, pb_, c_)
                            if ep_ is not None:
                                ep_()
                        for hh in range(4):
                            if hh > 0:
                                load_bias(hh)
                            for m in range(2):
                                acc = []

                                def do_pv(j, ptile, ptB_, c0, hh=hh, acc=acc, nk=nk):
                                    if not acc:
                                        for _ in range(2):
                                            bi_ = alloc_bank()
                                            pinned.add(bi_)
                                            acc.append(bi_)
                                    P.op("pe", _I("matmul", banks[acc[0]][:, c0:G], lhsT=Vc[:, j, hh * 128:(hh + 1) * 128], rhs=ptile[:, c0:G],
                                                  start=(j == 0), stop=(j == nk - 1)),
                                         reads=[bf(("V", j // 4)), ptB_], writes=[bankB[acc[0]]])
                                    P.op("pe", _I("matmul", banks[acc[1]][:, c0:G], lhsT=onesb[:], rhs=ptile[:, c0:G],
                                                  start=(j == 0), stop=(j == nk - 1)),
                                         reads=[bf("onesb"), ptB_], writes=[bankB[acc[1]]])

                                def ep1(m=m, acc=acc):
                                    P.op("act", _I("activation", out=ep[m], in_=banks[acc[1]][:], func=AF.Ln),
                                         reads=[bankB[acc[1]]], writes=[bf(f"ep{m}")])
                                    P.op("act", _I("activation", out=ep[m], in_=ep[m], func=AF.Exp, scale=-1.0),
                                         reads=[bf(f"ep{m}")], writes=[bf(f"ep{m}")])
                                    P.op("dve", _I("tensor_tensor", out=ep[2 + m], in0=banks[acc[0]][:], in1=ep[m], op=ALU.mult),
                                         reads=[bankB[acc[0]], bf(f"ep{m}")], writes=[bf(f"ep{2 + m}")])
                                    for bi_ in acc:
                                        pinned.discard(bi_)

                                for j in range(nk):
                                    near = j >= 4 * g - 1
                                    ri = j - 4 * g + 1
                                    c0 = 128 * (j - 4 * g) if j > 4 * g else 0
                                    bi = alloc_bank()
                                    if near:
                                        P.op("pe", _I("matmul", banks[bi][:, c0:G], lhsT=idb[:], rhs=biasT[:, 4 - ri, c0:G], start=True, stop=False),
                                             reads=[bf("idb"), bf("biasT")], writes=[bankB[bi]])
                                    P.op("pe", _I("matmul", banks[bi][:, c0:G], lhsT=KT[:, hh, j * 128:(j + 1) * 128], rhs=qT4[:, hh, m, c0:G],
                                                  start=(not near), stop=True),
                                         reads=[bf(("KT", j // 4)), qTB[2 * hh + m]], writes=[bankB[bi]])
                                    pidx = ptc[0] % 4
                                    ptc[0] += 1
                                    ptile = PT[pidx]
                                    ptB_ = bf(("PT", pidx))
                                    if near:
                                        P.op("act", _I("activation", out=ptile[:, c0:G], in_=banks[bi][:, c0:G], func=AF.Exp, scale=0.125),
                                             reads=[bankB[bi]], writes=[ptB_])
                                    else:
                                        P.op("act", _I("activation", out=ptile[:, c0:G], in_=banks[bi][:, c0:G], func=AF.Exp, scale=0.125,
                                                       bias=b31[:, hh:hh + 1]),
                                             reads=[bankB[bi], bf("b31")], writes=[ptB_])
                                    pend.append((do_pv, j, ptile, ptB_, c0, ep1 if j == nk - 1 else None))
                                    while len(pend) > 2:
                                        pop_tile()
                                    if m == 0:
                                        for jj, fn in list(pend_p2):
                                            if j >= min(jj + 1, nk - 1):
                                                pend_p2.remove((jj, fn))
                                                fn()

                            def part2a(hh=hh, i=i):
                                P.op("dve", _I("scalar_tensor_tensor", out=ep[2], in0=ep[3], scalar=neglam[:, i:i + 1], in1=ep[2],
                                               op0=ALU.mult, op1=ALU.add),
                                     reads=[bf("ep3"), bf("ep2"), bf("neglam")], writes=[bf("ep2")])

                            def part2b(hh=hh, i=i):
                                P.op("act", _I("activation", out=sqb, in_=ep[2], func=AF.Square), reads=[bf("ep2")], writes=[sqB])

                            def part2(hh=hh, i=i):
                                bi = alloc_bank()
                                P.op("pe", _I("matmul", banks[bi][:], lhsT=onesb[:], rhs=sqb, start=True, stop=True),
                                     reads=[bf("onesb"), sqB], writes=[bankB[bi]])
                                P.op("act", _I("activation", out=ep[0], in_=banks[bi][:], func=AF.Ln, scale=1.0 / 128, bias=EPS),
                                     reads=[bankB[bi]], writes=[bf("ep0")])
                                P.op("act", _I("activation", out=ep[0], in_=ep[0], func=AF.Exp, scale=-0.5),
                                     reads=[bf("ep0")], writes=[bf("ep0")])
                                P.op("dve", _I("scalar_tensor_tensor", out=featT[:, hh, :], in0=ep[2], scalar=gs[:, i:i + 1], in1=ep[0],
                                               op0=ALU.mult, op1=ALU.mult),
                                     reads=[bf("ep2"), bf("ep0"), bf("gs")], writes=[ftB[hh]])

                            pend_p2.extend([(1, part2a), (3, part2b), (7, part2)])
                        while pend:
                            pop_tile()
                        tail_p2 = [fn for _, fn in pend_p2]
                        assert len(tail_p2) == 3

                        def deferred(tail_p2=tail_p2):
                            for fn in tail_p2:
                                fn()
                        pool_part_b()
                        deferred()
                        proj_add(l, "ab_w_out", hcur, hB, nxt_gain=norm_xattn_g[l:l + 1, :], corder=(4, 5, 6, 7, 0, 1, 2, 3))
                    else:
                        order = [0, 2, 4, 1, 3, 5]
                        slot_of = {}
                        for cc in range(8):
                            if cc % 4 == 0:
                                for nb in order[(cc // 4) * 3:(cc // 4) * 3 + 3]:
                                    slot_of[nb] = load_block(l, "conv_w_in", 0, nb)
                            col = (cc % 4) * 128
                            bb = featmajor_proj(slot_of[cc // 4], col, None, None)
                            featmajor_proj(slot_of[2 + cc // 4], col, ep[0][:], [bf("ep0")], eng="act")
                            bx = featmajor_proj(slot_of[4 + cc // 4], col, None, None)
                            zb = zbuf[cc % 2]
                            zB = bf(("zb", 0))
                            P.op("pool", _I("tensor_copy", out=zb[:, 0:2], in_=zh[:, cc, :]), reads=[bf("zh")], writes=[zB])
                            P.op("dve", _I("tensor_tensor", out=zb[:, 2:514], in0=banks[bx][:], in1=ep[0][:], op=ALU.mult),
                                 reads=[bankB[bx], bf("ep0")], writes=[zB])
                            P.op("pool", _I("tensor_copy", out=zh[:, cc, :], in_=zb[:, 512:514]), reads=[zB], writes=[bf("zh")])
                            P.op("pool", _I("tensor_scalar", out=ep[1][:], in0=zb[:, 0:512], scalar1=convw[:, i, 0, cc:cc + 1], scalar2=0.0,
                                            op0=ALU.mult, op1=ALU.add),
                                 reads=[zB, bf("convw")], writes=[bf("ep1")])
                            for k in (1, 2):
                                P.op("dve", _I("scalar_tensor_tensor", out=ep[1][:], in0=zb[:, k:k + 512], scalar=convw[:, i, k, cc:cc + 1],
                                               in1=ep[1][:], op0=ALU.mult, op1=ALU.add),
                                     reads=[zB, bf("convw"), bf("ep1")], writes=[bf("ep1")])
                            P.op("dve", _I("tensor_tensor", out=featT[:, cc, :], in0=banks[bb][:], in1=ep[1][:], op=ALU.mult),
                                 reads=[bankB[bb], bf("ep1")], writes=[ftB[cc]])
                        proj_add(l, "conv_w_out", hcur, hB, nxt_gain=norm_xattn_g[l:l + 1, :])

                    if nxt is not None:
                        load_h((unit + 1) % 2, s, nxt[0], nxt[1])
                    if s == 0:
                        cv_flush(cv_per_unit if g + 1 < NG else 1000)

                    sqs = [load_block(l, "xattn_wq", 0, nb) for nb in range(2)]
                    for oc in range(8):
                        featmajor_proj(sqs[oc // 4], (oc % 4) * 128, qT[:, oc, :], [qTB[oc]])
                    def x_scores(hh):
                        pts = []
                        for mt in range(2):
                            bi = alloc_bank()
                            for dc in range(2):
                                P.op("pe", _I("matmul", banks[bi][:], lhsT=xKT[:, 2 * hh + dc, mt * 128:(mt + 1) * 128], rhs=qT[:, 2 * hh + dc, :],
                                              start=(dc == 0), stop=(dc == 1)),
                                     reads=[bf("xKT"), qTB[2 * hh + dc]], writes=[bankB[bi]])
                            pidx = (2 * hh + mt) % 4
                            P.op("act", _I("activation", out=PT[pidx][:], in_=banks[bi][:], func=AF.Exp, scale=1.0 / 16),
                                 reads=[bankB[bi]], writes=[bf(("PT", pidx))])
                            pts.append((PT[pidx], bf(("PT", pidx))))
                        return pts

                    def x_pv(hh, pts):
                        bo = []
                        for dc in range(2):
                            bi = alloc_bank()
                            bo.append(bi)
                            for mt in range(2):
                                P.op("pe", _I("matmul", banks[bi][:], lhsT=xV[:, mt, hh * 256 + dc * 128: hh * 256 + (dc + 1) * 128],
                                              rhs=pts[mt][0][:], start=(mt == 0), stop=(mt == 1)),
                                     reads=[bf("xV"), pts[mt][1]], writes=[bankB[bi]])
                        bl = alloc_bank()
                        for mt in range(2):
                            P.op("pe", _I("matmul", banks[bl][:], lhsT=onesb[:], rhs=pts[mt][0][:], start=(mt == 0), stop=(mt == 1)),
                                 reads=[bf("onesb"), pts[mt][1]], writes=[bankB[bl]])
                        e = ep[hh % 2]
                        eB = bf(f"ep{hh % 2}")
                        P.op("act", _I("activation", out=e, in_=banks[bl][:], func=AF.Ln), reads=[bankB[bl]], writes=[eB])
                        P.op("act", _I("activation", out=e, in_=e, func=AF.Exp, scale=-1.0), reads=[eB], writes=[eB])
                        for dc in range(2):
                            P.op("dve", _I("tensor_tensor", out=featT[:, 2 * hh + dc, :], in0=banks[bo[dc]][:], in1=e, op=ALU.mult),
                                 reads=[bankB[bo[dc]], eB], writes=[ftB[2 * hh + dc]])

                    xp = x_scores(0)
                    for hh in range(4):
                        xn = x_scores(hh + 1) if hh + 1 < 4 else None
                        x_pv(hh, xp)
                        xp = xn
                    proj_add(l, "xattn_wo", hcur, hB, nxt_gain=norm_mlp_g[l:l + 1, :])

                    if last:
                        load_gain(final_norm_g[0:1, :], final=True)
                    if nxt is not None:
                        load_gain(norm_mix_g[nxt[0]:nxt[0] + 1, :])
                        gain_ready = True
                    else:
                        gain_ready = False
                    epc = 0
                    nub = (unit + 1) % 2
                    nh_ap = [hres2[nub][:, t, :] for t in range(4)]
                    nsl = {}
                    for fh in range(2):
                        for blk in range(4):
                            s1 = load_block(l, "mlp_w1", 0, fh * 4 + blk)
                            for fc in range(4):
                                bi = featmajor_proj(s1, fc * 128, None, None)
                                e = epc % 4
                                epc += 1
                                P.op("act", _I("activation", out=ep[e], in_=banks[bi][:], func=AF.Relu),
                                     reads=[bankB[bi]], writes=[bf(f"ep{e}")])
                                P.op("dve", _I("tensor_tensor", out=aT[:, blk * 4 + fc, :], in0=banks[bi][:], in1=ep[e], op=ALU.mult),
                                     reads=[bankB[bi], bf(f"ep{e}")], writes=[aTB[blk * 4 + fc]])
                        hide = (fh == 1 and nxt is not None)
                        if hide:
                            nsl[0] = norm_pre(nh_ap[0], hB2[nub][0])
                            nsl[1] = norm_pre(nh_ap[1], hB2[nub][1])
                        for dh in range(2):
                            s2 = [load_block(l, "mlp_w2", fh * 2 + kb, dh) for kb in range(2)]
                            for t in range(4):
                                bi = alloc_bank()
                                for kb in range(2):
                                    for c in range(8):
                                        P.op("pe", _I("matmul", banks[bi][:], lhsT=aT[:, kb * 8 + c, t * 128:(t + 1) * 128], rhs=wring[s2[kb]][:, c, :],
                                                      start=(kb == 0 and c == 0), stop=(kb == 1 and c == 7)),
                                             reads=[aTB[kb * 8 + c], wrB[s2[kb]]], writes=[bankB[bi]])
                                P.op("dve", _I("tensor_tensor", out=hcur[:, t, dh * 512:(dh + 1) * 512], in0=banks[bi][:],
                                               in1=hcur[:, t, dh * 512:(dh + 1) * 512], op=ALU.add),
                                     reads=[bankB[bi], hB[t]], writes=[hB[t]])
                                if hide and dh == 1 and t == 1:
                                    norm_tr(nsl[2], 2, hnT, bf("hnT"))
                                    norm_tr(nsl[3], 3, hnT, bf("hnT"))
                            if hide and dh == 0:
                                norm_tr(nsl[0], 0, hnT, bf("hnT"))
                                nsl[2] = norm_pre(nh_ap[2], hB2[nub][2])
                                norm_tr(nsl[1], 1, hnT, bf("hnT"))
                                nsl[3] = norm_pre(nh_ap[3], hB2[nub][3])
                        if hide:
                            norm_done = True

                    if last:
                        for t in range(4):
                            rs, sB = rms_rstd(hap[t], [hB[t]], D)
                            P.op("dve", _I("scalar_tensor_tensor", out=hcur[:, t, :], in0=hcur[:, t, :], scalar=rs, in1=gbf,
                                           op0=ALU.mult, op1=ALU.mult),
                                 reads=[hB[t], sB, bf("biasT")], writes=[hB[t]])
                    dst = y if last else hscr
                    final_ev = P.op("pool", _I("dma_start", out=dst[s, tok0:tok0 + G, :].rearrange("(t p) d -> p t d", p=128), in_=hcur[:]),
                                    reads=hB, writes=[bf(("hscr", s, g))], dma=s_hst[ub])
                    unit += 1
        P.wait_event("pool", final_ev)
        P.emit()
    return nc


_CACHE = {}


def _get_prog(S, NSEQ):
    key = (S, NSEQ)
    if key not in _CACHE:
        _CACHE[key] = build_program(S, NSEQ)
    return _CACHE[key]


def make_in_maps(inputs, ncores, nseq):
    consts = host_consts()
    maps = []
    for c in range(ncores):
        m = {}
        for k, v in inputs.items():
            v = np.asarray(v, dtype=np.float32)
            if k in ("x", "mem"):
                m[k] = np.ascontiguousarray(v[c * nseq:(c + 1) * nseq])
            elif k in ("mem_norm_g", "final_norm_g"):
                m[k] = np.ascontiguousarray(v.reshape(1, -1))
            else:
                m[k] = np.ascontiguousarray(v)
        m.update(consts)
        maps.append(m)
    return maps


def kernel(**inputs):
    x = np.asarray(inputs["x"])
    Bsz, S, _ = x.shape
    ncores = 8
    nseq = Bsz // ncores
    nc = _get_prog(S, nseq)
    maps = make_in_maps(inputs, ncores, nseq)
    res = run_bass_kernel_spmd(nc, maps, core_ids=list(range(ncores)))
    out = np.concatenate([np.asarray(r["y"]) for r in res.results], axis=0)
    return out.astype(np.float32)
```
